# Optimizing a Trainium2 kernel written in Bass

```python
import math
import jax, jax.numpy as jnp
from jax import lax
import numpy as np

D_MODEL = 1024
BATCH = 16
SEQ = 2048
DEPTH = 4

GRID_W = 64
CTX_LEN = 256
N_MIXERS = 3
N_A = (DEPTH + 2) // 3
N_B = (DEPTH + 1) // 3
N_C = DEPTH // 3
ALPHA = (2.0 * DEPTH) ** 0.25
BETA_INIT = (8.0 * DEPTH) ** -0.25
D_FF = 4 * D_MODEL
EPS = 1e-6
LN_EPS = 1e-5

H_A = 8
DK_A = D_MODEL // H_A
DV_A = D_MODEL // H_A
KEY_A = H_A * DK_A
VAL_A = H_A * DV_A
CONV_CH_A = 2 * KEY_A + VAL_A
IN_A = CONV_CH_A + VAL_A + 4 * H_A
CONV_K = 5
CHUNK_A = 64

H_B = 8
DH_B = D_MODEL // (2 * H_B)
DV_B = 2 * DH_B
QK_B = 2 * H_B * DH_B
IN_B = 2 * QK_B + H_B * DV_B
Q_BLOCK = 128
ROPE_BASE = 10000.0

H_C = 4
DQK_C = D_MODEL // (2 * H_C)
DV_C = D_MODEL // H_C
QK_C = H_C * DQK_C
VAL_C = H_C * DV_C
IN_C = 2 * QK_C + 2 * VAL_C + 4 * H_C
CHUNK_C = 64

kernel_name = 'hybrid_gdn_diffattn_mlstm_deepnorm_dit'


def layer_norm(x, g, b):
    xf = x.astype(jnp.float32)
    mu = jnp.mean(xf, -1, keepdims=True)
    var = jnp.mean(jnp.square(xf - mu), -1, keepdims=True)
    return ((xf - mu) * lax.rsqrt(var + LN_EPS) * g.astype(jnp.float32) + b.astype(jnp.float32)).astype(x.dtype)


def rms_norm(x, g):
    xf = x.astype(jnp.float32)
    return (xf * lax.rsqrt(jnp.mean(xf * xf, -1, keepdims=True) + EPS) * g.astype(jnp.float32)).astype(x.dtype)


def l2_normalize(x):
    xf = x.astype(jnp.float32)
    return xf * lax.rsqrt(jnp.sum(xf * xf, -1, keepdims=True) + EPS)


def modulate(x, shift, scale):
    return x * (1.0 + scale) + shift


def centred_dwconv(x, w):
    k = w.shape[0]
    return lax.conv_general_dilated(x, w[:, None, :].astype(x.dtype), window_strides=(1,),
                                    padding=((k // 2, k // 2),), dimension_numbers=('NWC', 'WIO', 'NWC'),
                                    feature_group_count=x.shape[-1])


def sq_relu_mlp(h, w1, w2):
    return jnp.square(jax.nn.relu(h @ w1)) @ w2


def gated_delta_chunked(q, k, v, beta, g, s0):
    b, h, n, _ = q.shape
    L = CHUNK_A
    nc = n // L
    r = lambda t: t.reshape((b, h, nc, L) + t.shape[3:])
    q, k, v, beta, g = r(q), r(k), r(v), r(beta), r(g)
    g = jnp.cumsum(g, -1)
    tri = jnp.tril(jnp.ones((L, L), bool))
    strict = jnp.tril(jnp.ones((L, L), bool), -1)
    diff = g[..., :, None] - g[..., None, :]
    decay = jnp.where(tri, jnp.exp(jnp.where(tri, diff, 0.0)), 0.0)
    kb = k * beta[..., None]
    a_mat = jnp.where(strict, jnp.einsum('bhcid,bhcjd->bhcij', kb, k) * decay, 0.0)
    u = lax.linalg.triangular_solve(a_mat, v * beta[..., None], left_side=True, lower=True, unit_diagonal=True)
    w = lax.linalg.triangular_solve(a_mat, kb * jnp.exp(g)[..., None], left_side=True, lower=True, unit_diagonal=True)
    qk = jnp.where(tri, jnp.einsum('bhcid,bhcjd->bhcij', q, k) * decay, 0.0)
    qg = q * jnp.exp(g)[..., None]
    kdec = k * jnp.exp(g[..., -1:] - g)[..., None]
    g_last = jnp.exp(g[..., -1])

    def step(s, xs):
        u_c, w_c, qk_c, qg_c, kdec_c, gl_c = xs
        v_new = u_c - jnp.einsum('bhld,bhdv->bhlv', w_c, s)
        o = jnp.einsum('bhld,bhdv->bhlv', qg_c, s) + jnp.einsum('bhlm,bhmv->bhlv', qk_c, v_new)
        s = s * gl_c[..., None, None] + jnp.einsum('bhld,bhlv->bhdv', kdec_c, v_new)
        return s, o

    xs = tuple(jnp.moveaxis(t, 2, 0) for t in (u, w, qk, qg, kdec, g_last))
    s_fin, o = lax.scan(step, s0, xs)
    o = jnp.moveaxis(o, 0, 2).reshape(b, h, n, v.shape[-1])
    return o, s_fin


def gated_deltanet(hl, hc, w_in, conv_w, a_log, dt_bias, norm_g, w_out, need_ctx):
    f32 = jnp.float32

    def prepare(hh):
        bsz, n, _ = hh.shape
        p = hh @ w_in
        qkv = jax.nn.silu(centred_dwconv(p[..., :CONV_CH_A], conv_w))
        q = l2_normalize(qkv[..., :KEY_A].reshape(bsz, n, H_A, DK_A)) * (DK_A ** -0.5)
        k = l2_normalize(qkv[..., KEY_A:2 * KEY_A].reshape(bsz, n, H_A, DK_A))
        v = qkv[..., 2 * KEY_A:].reshape(bsz, n, H_A, DV_A).astype(f32)
        z = p[..., CONV_CH_A:CONV_CH_A + VAL_A].reshape(bsz, n, H_A, DV_A)
        gp = p[..., CONV_CH_A + VAL_A:].reshape(bsz, n, 2, 2, H_A).astype(f32)
        g = -jnp.exp(a_log.astype(f32)) * jax.nn.softplus(gp[:, :, :, 0] + dt_bias.astype(f32))
        beta = jax.nn.sigmoid(gp[:, :, :, 1])
        tr = lambda t: jnp.swapaxes(t, 1, 2)
        return tr(q), tr(k), tr(v), jnp.moveaxis(beta, 1, -1), jnp.moveaxis(g, 1, -1), z

    def bidir(q, k, v, beta, g, s_f, s_b):
        flip = lambda t: jnp.flip(t, axis=2)
        o_f, s_f = gated_delta_chunked(q, k, v, beta[:, 0], g[:, 0], s_f)
        o_b, s_b = gated_delta_chunked(flip(q), flip(k), flip(v), flip(beta[:, 1]), flip(g[:, 1]), s_b)
        return o_f + flip(o_b), s_f, s_b

    def finish(o, z):
        o = jnp.swapaxes(o, 1, 2).astype(z.dtype)
        o = rms_norm(o, norm_g) * jax.nn.silu(z)
        return o.reshape(o.shape[0], o.shape[1], VAL_A) @ w_out

    qc, kc, vc, bc, gc, zc = prepare(hc)
    s0 = jnp.zeros((hc.shape[0], H_A, DK_A, DV_A), f32)
    oc, s_f, s_b = bidir(qc, kc, vc, bc, gc, s0, s0)
    ql, kl, vl, bl, gl, zl = prepare(hl)
    ol, _, _ = bidir(ql, kl, vl, bl, gl, s_f, s_b)
    yl = finish(ol, zl)
    yc = finish(oc, zc) if need_ctx else None
    return yl, yc


def axial_rope_tables(n):
    rows = n // GRID_W
    r = jnp.repeat(jnp.arange(rows, dtype=jnp.float32), GRID_W)
    col = jnp.tile(jnp.arange(GRID_W, dtype=jnp.float32), rows)
    half = DH_B // 2
    inv = ROPE_BASE ** (-jnp.arange(0, half, 2, dtype=jnp.float32) / half)
    ang = jnp.concatenate([r[:, None] * inv, col[:, None] * inv], -1)
    return jnp.cos(ang), jnp.sin(ang)


def apply_axial_rope(x, cos, sin):
    fq = DH_B // 4
    xa = x.reshape(x.shape[:-1] + (2, 2, fq))
    x1, x2 = xa[..., 0, :], xa[..., 1, :]
    c = cos.reshape(cos.shape[0], 1, 1, 2, fq).astype(x.dtype)
    s = sin.reshape(sin.shape[0], 1, 1, 2, fq).astype(x.dtype)
    out = jnp.stack([x1 * c - x2 * s, x2 * c + x1 * s], -2)
    return out.reshape(x.shape)


def diff_attention(hl, hc, w_in, lam, subln, w_out, layer_idx, need_ctx):
    b, n, _ = hl.shape
    lam_init = 0.8 - 0.6 * math.exp(-0.3 * layer_idx)
    lf = lam.astype(jnp.float32)
    lam_full = jnp.exp(jnp.sum(lf[0] * lf[1])) - jnp.exp(jnp.sum(lf[2] * lf[3])) + lam_init
    scale = DH_B ** -0.5

    def project(hh):
        bsz, m, _ = hh.shape
        p = hh @ w_in
        q = p[..., :QK_B].reshape(bsz, m, H_B, 2, DH_B)
        k = p[..., QK_B:2 * QK_B].reshape(bsz, m, H_B, 2, DH_B)
        v = p[..., 2 * QK_B:].reshape(bsz, m, H_B, DV_B)
        return q, k, v

    def attend(q, k, v):
        s = jnp.einsum('bqhtd,bkhtd->bhtqk', q, k).astype(jnp.float32) * scale
        pr = jax.nn.softmax(s, axis=-1)
        a = pr[:, :, 0] - lam_full * pr[:, :, 1]
        return jnp.einsum('bhqk,bkhd->bqhd', a.astype(v.dtype), v)

    def finish(o):
        o = rms_norm(o, subln) * (1.0 - lam_init)
        return o.reshape(o.shape[0], o.shape[1], H_B * DV_B) @ w_out

    ql, kl, vl = project(hl)
    qc, kc, vc = project(hc)
    cos, sin = axial_rope_tables(n)
    ql = apply_axial_rope(ql, cos, sin)
    kl = apply_axial_rope(kl, cos, sin)
    k_all = jnp.concatenate([kc, kl], axis=1)
    v_all = jnp.concatenate([vc, vl], axis=1)
    qb = jnp.moveaxis(ql.reshape(b, n // Q_BLOCK, Q_BLOCK, H_B, 2, DH_B), 1, 0)
    ol = lax.map(lambda qblk: attend(qblk, k_all, v_all), qb)
    ol = jnp.moveaxis(ol, 0, 1).reshape(b, n, H_B, DV_B)
    yl = finish(ol)
    yc = finish(attend(qc, kc, vc)) if need_ctx else None
    return yl, yc


def mlstm_chunked(q, k, v, log_i, log_f, state):
    b, h, n, _ = q.shape
    L = CHUNK_C
    nc = n // L
    r = lambda t: t.reshape((b, h, nc, L) + t.shape[3:])
    q, k, v, log_i, log_f = r(q), r(k), r(v), r(log_i), r(log_f)
    bcum = jnp.cumsum(log_f, -1)
    tri = jnp.tril(jnp.ones((L, L), bool))
    dmat = jnp.where(tri, bcum[..., :, None] - bcum[..., None, :] + log_i[..., None, :], -jnp.inf)
    m_intra = jnp.max(dmat, -1)
    wts = jnp.exp(dmat - m_intra[..., None]) * jnp.einsum('bhcid,bhcjd->bhcij', q, k)
    num_intra = jnp.einsum('bhcij,bhcjv->bhciv', wts, v)
    den_intra = jnp.sum(wts, -1)
    a_end = bcum[..., -1:] - bcum + log_i
    m_end = m_intra[..., -1]
    k_end = k * jnp.exp(a_end - m_end[..., None])[..., None]
    b_last = bcum[..., -1]

    def step(carry, xs):
        c_st, n_st, m0 = carry
        q_c, bcum_c, mi_c, num_c, den_c, ke_c, v_c, me_c, bl_c = xs
        m_t = jnp.maximum(bcum_c + m0[..., None], mi_c)
        s_inter = jnp.exp(bcum_c + m0[..., None] - m_t)
        s_intra = jnp.exp(mi_c - m_t)
        num = s_inter[..., None] * jnp.einsum('bhld,bhdv->bhlv', q_c, c_st) + s_intra[..., None] * num_c
        den = s_inter * jnp.einsum('bhld,bhd->bhl', q_c, n_st) + s_intra * den_c
        h_t = num / jnp.maximum(jnp.abs(den), jnp.exp(-m_t))[..., None]
        m_new = jnp.maximum(bl_c + m0, me_c)
        s_old = jnp.exp(bl_c + m0 - m_new)
        s_new = jnp.exp(me_c - m_new)
        c_st = s_old[..., None, None] * c_st + s_new[..., None, None] * jnp.einsum('bhld,bhlv->bhdv', ke_c, v_c)
        n_st = s_old[..., None] * n_st + s_new[..., None] * jnp.sum(ke_c, -2)
        return (c_st, n_st, m_new), h_t

    xs = tuple(jnp.moveaxis(t, 2, 0) for t in (q, bcum, m_intra, num_intra, den_intra, k_end, v, m_end, b_last))
    state, hs = lax.scan(step, state, xs)
    hs = jnp.moveaxis(hs, 0, 2).reshape(b, h, n, v.shape[-1])
    return hs, state


def mlstm_mixer(hl, hc, w_in, gate_bias, norm_g, w_out, need_ctx):
    f32 = jnp.float32

    def prepare(hh):
        bsz, n, _ = hh.shape
        p = hh @ w_in
        tr = lambda t: jnp.swapaxes(t, 1, 2).astype(f32)
        q = tr(p[..., :QK_C].reshape(bsz, n, H_C, DQK_C))
        k = tr(p[..., QK_C:2 * QK_C].reshape(bsz, n, H_C, DQK_C)) * (DQK_C ** -0.5)
        v = tr(p[..., 2 * QK_C:2 * QK_C + VAL_C].reshape(bsz, n, H_C, DV_C))
        o = p[..., 2 * QK_C + VAL_C:2 * QK_C + 2 * VAL_C].reshape(bsz, n, H_C, DV_C)
        gp = (p[..., 2 * QK_C + 2 * VAL_C:].reshape(bsz, n, 4, H_C) + gate_bias).astype(f32)
        gp = jnp.moveaxis(gp, 1, -1)
        log_i = gp[:, 0::2]
        log_f = jax.nn.log_sigmoid(gp[:, 1::2])
        return q, k, v, log_i, log_f, o

    def bidir(q, k, v, log_i, log_f, st_f, st_b):
        flip = lambda t: jnp.flip(t, axis=2)
        h_f, st_f = mlstm_chunked(q, k, v, log_i[:, 0], log_f[:, 0], st_f)
        h_b, st_b = mlstm_chunked(flip(q), flip(k), flip(v), flip(log_i[:, 1]), flip(log_f[:, 1]), st_b)
        return h_f + flip(h_b), st_f, st_b

    def finish(hh, o):
        hh = jnp.swapaxes(hh, 1, 2).astype(o.dtype)
        hh = rms_norm(hh, norm_g.reshape(H_C, DV_C)) * jax.nn.sigmoid(o)
        return hh.reshape(hh.shape[0], hh.shape[1], VAL_C) @ w_out

    qc, kc, vc, ic, fc, oc_gate = prepare(hc)
    bsz = hc.shape[0]
    st0 = (jnp.zeros((bsz, H_C, DQK_C, DV_C), f32), jnp.zeros((bsz, H_C, DQK_C), f32), jnp.zeros((bsz, H_C), f32))
    hc_out, st_f, st_b = bidir(qc, kc, vc, ic, fc, st0, st0)
    ql, kl, vl, il, fl, ol_gate = prepare(hl)
    hl_out, _, _ = bidir(ql, kl, vl, il, fl, st_f, st_b)
    yl = finish(hl_out, ol_gate)
    yc = finish(hc_out, oc_gate) if need_ctx else None
    return yl, yc


def setup_inputs(seed: int = 0) -> dict:
    key = jax.random.key(seed)
    ks = jax.random.split(key, 24)
    f32 = jnp.float32

    def nrm(k, shape, scale):
        return jax.random.normal(k, shape, f32) * scale

    d = D_MODEL
    x = nrm(ks[0], (BATCH, SEQ, d), 1.0)
    c = nrm(ks[1], (BATCH, d), 1.0)
    ctx = nrm(ks[2], (BATCH, CTX_LEN, d), 1.0)
    c_ctx = nrm(ks[3], (d,), 1.0)
    w_mod = nrm(ks[4], (DEPTH, d, 6 * d), d ** -0.5)
    b_mod = nrm(ks[5], (DEPTH, 6 * d), 0.02)
    ln_g = 1.0 + nrm(ks[6], (DEPTH, 2, d), 0.02)
    ln_b = nrm(ks[7], (DEPTH, 2, d), 0.02)
    w_in_a = nrm(ks[8], (N_A, d, IN_A), d ** -0.5)
    conv_a = nrm(ks[9], (N_A, CONV_K, CONV_CH_A), CONV_K ** -0.5)
    a_log_a = jnp.log(jax.random.uniform(ks[10], (N_A, 2, H_A), f32, minval=1.0, maxval=16.0))
    dt = jnp.exp(jax.random.uniform(ks[11], (N_A, 2, H_A), f32, minval=math.log(1e-3), maxval=math.log(1e-1)))
    dt_bias_a = dt + jnp.log(-jnp.expm1(-dt))
    norm_a = 1.0 + nrm(ks[12], (N_A, DV_A), 0.02)
    w_out_a = nrm(ks[13], (N_A, VAL_A, d), BETA_INIT * VAL_A ** -0.5)
    w_in_b = nrm(ks[14], (N_B, d, IN_B), d ** -0.5)
    lam_b = nrm(ks[15], (N_B, 4, DH_B), 0.1)
    subln_b = 1.0 + nrm(ks[16], (N_B, DV_B), 0.02)
    w_out_b = nrm(ks[17], (N_B, H_B * DV_B, d), BETA_INIT * (H_B * DV_B) ** -0.5)
    w_in_c = nrm(ks[18], (N_C, d, IN_C), d ** -0.5)
    f_bias = jnp.linspace(3.0, 6.0, H_C, dtype=f32)
    base = jnp.stack([jnp.zeros_like(f_bias), f_bias, jnp.zeros_like(f_bias), f_bias])
    gate_bias_c = base + nrm(ks[19], (N_C, 4, H_C), 0.1)
    norm_c = 1.0 + nrm(ks[20], (N_C, VAL_C), 0.02)
    w_out_c = nrm(ks[21], (N_C, VAL_C, d), BETA_INIT * VAL_C ** -0.5)
    w1 = nrm(ks[22], (DEPTH, d, D_FF), d ** -0.5)
    w2 = nrm(ks[23], (DEPTH, D_FF, d), BETA_INIT * D_FF ** -0.5)
    return {'x': x, 'c': c, 'ctx': ctx, 'c_ctx': c_ctx, 'w_mod': w_mod, 'b_mod': b_mod,
            'ln_g': ln_g, 'ln_b': ln_b, 'w_in_a': w_in_a, 'conv_a': conv_a, 'a_log_a': a_log_a,
            'dt_bias_a': dt_bias_a, 'norm_a': norm_a, 'w_out_a': w_out_a, 'w_in_b': w_in_b,
            'lam_b': lam_b, 'subln_b': subln_b, 'w_out_b': w_out_b, 'w_in_c': w_in_c,
            'gate_bias_c': gate_bias_c, 'norm_c': norm_c, 'w_out_c': w_out_c, 'w1': w1, 'w2': w2}


def reference(x, c, ctx, c_ctx, w_mod, b_mod, ln_g, ln_b, w_in_a, conv_a, a_log_a, dt_bias_a, norm_a, w_out_a,
              w_in_b, lam_b, subln_b, w_out_b, w_in_c, gate_bias_c, norm_c, w_out_c, w1, w2):
    xl, xc = x, ctx
    s_lat = jax.nn.silu(c)
    s_ctx = jax.nn.silu(c_ctx)
    for i in range(DEPTH):
        need_ctx = i < DEPTH - 1
        kind, j = i % N_MIXERS, i // N_MIXERS
        m_lat = jnp.split((s_lat @ w_mod[i] + b_mod[i])[:, None, :], 6, axis=-1)
        m_ctx = jnp.split(s_ctx @ w_mod[i] + b_mod[i], 6, axis=-1)
        hl = modulate(xl, m_lat[0], m_lat[1])
        hc = modulate(xc, m_ctx[0], m_ctx[1])
        if kind == 0:
            yl, yc = gated_deltanet(hl, hc, w_in_a[j], conv_a[j], a_log_a[j], dt_bias_a[j], norm_a[j], w_out_a[j], need_ctx)
        elif kind == 1:
            yl, yc = diff_attention(hl, hc, w_in_b[j], lam_b[j], subln_b[j], w_out_b[j], i, need_ctx)
        else:
            yl, yc = mlstm_mixer(hl, hc, w_in_c[j], gate_bias_c[j], norm_c[j], w_out_c[j], need_ctx)
        xl = layer_norm(ALPHA * xl + m_lat[2] * yl, ln_g[i, 0], ln_b[i, 0])
        xl = layer_norm(ALPHA * xl + m_lat[5] * sq_relu_mlp(modulate(xl, m_lat[3], m_lat[4]), w1[i], w2[i]),
                        ln_g[i, 1], ln_b[i, 1])
        if need_ctx:
            xc = layer_norm(ALPHA * xc + m_ctx[2] * yc, ln_g[i, 0], ln_b[i, 0])
            xc = layer_norm(ALPHA * xc + m_ctx[5] * sq_relu_mlp(modulate(xc, m_ctx[3], m_ctx[4]), w1[i], w2[i]),
                            ln_g[i, 1], ln_b[i, 1])
    return xl
```

```python
import math
import numpy as np
import concourse.bass as bass
import concourse.mybir as mybir
from concourse.bass_utils import run_bass_kernel_spmd
from contextlib import ExitStack

F32 = mybir.dt.float32
BF16 = mybir.dt.bfloat16
ALU = mybir.AluOpType
AF = mybir.ActivationFunctionType

NCORES = 8
NSEQ = 2
NCTX = 256
NLAT = 2048
NTOK = NCTX + NLAT
NT = NTOK // 128
D = 1024
KC = 8
DFF = 4096
DEPTH = 4
ALPHA = (2.0 * DEPTH) ** 0.25
LN_EPS_S = 1e-5 / (ALPHA * ALPHA)
EPS = 1e-6
SELF_WINDOW = 6
NO_SELF = ("pe", "pool", "sp")


class Buf:
    __slots__ = ("t", "writers", "readers", "name", "excl")

    def __init__(self, t, name="", excl=False):
        self.t = t
        self.writers = {}
        self.readers = {}
        self.name = name
        self.excl = excl

    def __getitem__(self, idx):
        return self.t[idx]


class SlotView:
    def __init__(self, bank, ap):
        self.bank = bank
        self.t = ap
        self.name = "slot"
        self.excl = True

    writers = property(lambda self: self.bank.writers, lambda self, v: setattr(self.bank, "writers", v))
    readers = property(lambda self: self.bank.readers, lambda self, v: setattr(self.bank, "readers", v))


class FW:
    def __init__(self, nc, stack):
        self.nc = nc
        self.stack = stack
        self.engs = {"pe": nc.tensor, "act": nc.scalar, "dve": nc.vector, "pool": nc.gpsimd, "sp": nc.sync}
        self.sem = {}
        self.cnt = {}
        for e in ("pe", "act", "dve", "pool"):
            self.sem[e] = stack.enter_context(nc.semaphore("s_" + e))
            self.cnt[e] = 0
        self.dma_slots = {"sp": [], "pool": []}
        self.dma_next = {"sp": 0, "pool": 0}
        for q, n in (("sp", 32), ("pool", 16)):
            for i in range(n):
                s = stack.enter_context(nc.semaphore("d_%s%d" % (q, i)))
                key = "dma_%s%d" % (q, i)
                self.sem[key] = s
                self.cnt[key] = 0
                self.dma_slots[q].append(key)
        self.seen = {e: {} for e in ("pe", "act", "dve", "pool", "sp")}
        self.n_inst = 0
        self.rr = 0

    def sb(self, name, shape, dtype=F32):
        self.uid = getattr(self, "uid", 0) + 1
        t = self.stack.enter_context(self.nc.sbuf_tensor("sb_%s_%d" % (name, self.uid), list(shape), dtype))
        return Buf(t, name)

    def ps(self, name, shape, dtype=F32):
        t = self.stack.enter_context(self.nc.psum_tensor("ps_" + name, list(shape), dtype))
        return Buf(t, name, excl=True)

    def view(self, ap, name=""):
        return Buf(ap, name)

    def barrier(self):
        deps = {k: v for k, v in self.cnt.items() if v > 0}
        for e in ("pe", "act", "dve", "pool", "sp"):
            self._wait(e, dict(deps))

    def scope(self):
        fw = self

        class _S:
            def __enter__(s2):
                s2.old = fw.stack
                s2.st = ExitStack()
                s2.st.__enter__()
                fw.stack = s2.st
                return s2

            def __exit__(s2, *a):
                fw.barrier()
                fw.stack = s2.old
                return s2.st.__exit__(*a)
        return _S()

    def _wait(self, ename, deps):
        eng = self.engs[ename]
        seen = self.seen[ename]
        for k, v in deps.items():
            if k == ename:
                if ename in NO_SELF or self.cnt[ename] - v >= SELF_WINDOW:
                    continue
            if seen.get(k, 0) >= v:
                continue
            eng.wait_ge(self.sem[k], v)
            seen[k] = v

    @staticmethod
    def _deps(reads, writes):
        deps = {}
        for b in reads:
            for k, v in b.writers.items():
                if deps.get(k, 0) < v:
                    deps[k] = v
            if b.excl:
                for k, v in b.readers.items():
                    if deps.get(k, 0) < v:
                        deps[k] = v
        for b in writes:
            for k, v in b.writers.items():
                if deps.get(k, 0) < v:
                    deps[k] = v
            for k, v in b.readers.items():
                if deps.get(k, 0) < v:
                    deps[k] = v
        return deps

    def op(self, ename, fn, reads=(), writes=()):
        deps = self._deps(reads, writes)
        self._wait(ename, deps)
        if ename == "pe" and getattr(self, "pe_drain", False) and self.cnt["pe"] > 0:
            self.engs["pe"].wait_ge(self.sem["pe"], self.cnt["pe"])
        ins = fn(self.engs[ename])
        self.cnt[ename] += 1
        c = self.cnt[ename]
        ins.then_inc(self.sem[ename], 1)
        self.n_inst += 1
        for b in writes:
            b.writers = {ename: c}
            b.readers = {}
        for b in reads:
            if b.readers.get(ename, 0) < c:
                b.readers[ename] = c
        return ins

    def dma(self, out_ap, in_ap, reads=(), writes=(), q="sp", **kw):
        deps = self._deps(reads, writes)
        slots = self.dma_slots[q]
        key = slots[self.dma_next[q] % len(slots)]
        self.dma_next[q] += 1
        if self.cnt[key] > 0:
            deps[key] = max(deps.get(key, 0), self.cnt[key])
        self._wait(q, deps)
        ins = self.engs[q].dma_start(out=out_ap, in_=in_ap, **kw)
        self.cnt[key] += 16
        c = self.cnt[key]
        ins.then_inc(self.sem[key], 16)
        self.n_inst += 1
        for b in writes:
            b.writers = {key: c}
            b.readers = {}
        for b in reads:
            if b.readers.get(key, 0) < c:
                b.readers[key] = c

    def finish(self, bufs, ename="sp"):
        deps = {}
        for b in bufs:
            for k, v in b.writers.items():
                if deps.get(k, 0) < v:
                    deps[k] = v
        self._wait(ename, deps)

    def _touch(self, out, out_ap):
        if False:
            j = self.touch_junk
            self.op("act", lambda e: e.copy(out=j[:, 0:1], in_=out_ap[:, 0:1]), reads=[out], writes=[j])
            self.pe_gate = self.cnt["act"]

    def mm(self, out, out_ap, lhsT, lhsT_ap, rhs, rhs_ap, start=True, stop=True):
        r = self.op("pe", lambda e: e.matmul(out_ap, lhsT=lhsT_ap, rhs=rhs_ap, start=start, stop=stop),
                    reads=[lhsT, rhs], writes=[out])
        self._touch(out, out_ap)
        return r

    def tr(self, out, out_ap, in_, in_ap, ident, ident_ap):
        r = self.op("pe", lambda e: e.transpose(out_ap, in_ap, ident_ap), reads=[in_, ident], writes=[out])
        self._touch(out, out_ap)
        return r

    def evac(self, out, out_ap, in_, in_ap):
        self.rr += 1
        if self.rr & 1:
            return self.op("act", lambda e: e.copy(out=out_ap, in_=in_ap), reads=[in_], writes=[out])
        return self.op("dve", lambda e: e.tensor_copy(out=out_ap, in_=in_ap), reads=[in_], writes=[out])


DEBUG = False
GDN_HEADS = 8
GDN_STAGE = 9
GDN_YIELDS = 999
GDN_NSTEPS = 18
KKDIRS = (0, 1)
BAR = False
PE_DRAIN = True


def build(layers=(0, 1, 2, 3), do_ffn=True):
    nc = bass.Bass("TRN2", target_bir_lowering=False)

    def din(name, shape):
        return nc.dram_tensor(name, list(shape), F32, kind="ExternalInput").ap()

    x_d = din("x", [NSEQ, NLAT, D])
    ctx_d = din("ctx", [NSEQ, NCTX, D])
    cT_d = din("cT", [128, KC * 3])
    w_mod_d = din("w_mod", [DEPTH, D, 6 * D])
    b_modT_d = din("b_modT", [128, DEPTH * 48])
    ln_gT_d = din("ln_gT", [128, DEPTH * 2 * KC])
    ln_bT_d = din("ln_bT", [128, DEPTH * 2 * KC])
    w_in_a_d = din("w_in_a", [2, D, 4128])
    conv_aT_d = din("conv_aT", [128, 2 * 5 * 24])
    alog_d = din("a_log_a", [2, 16])
    dtb_d = din("dt_bias_a", [2, 16])
    norm_a_d = din("norm_a", [2, 128])
    w_out_a_d = din("w_out_a", [2, D, D])
    w_in_b_d = din("w_in_b", [1, D, 3072])
    w_in_bp_d = din("w_in_bp", [1, D, 2048])
    lam_b_d = din("lam_b", [1, 4, 64])
    subln_b_d = din("subln_b", [1, 128])
    w_out_b_d = din("w_out_b", [1, D, D])
    w_in_c_d = din("w_in_c", [1, D, 3088])
    gbias_c_d = din("gate_bias_c", [1, 16])
    norm_c_d = din("norm_c", [1, D])
    w_out_c_d = din("w_out_c", [1, D, D])
    w1_d = din("w1", [DEPTH, D, DFF])
    w2_d = din("w2", [DEPTH, DFF, D])
    cst_d = din("cst", [128, 10 * 128])
    rope_d = din("rope", [128, 2 * NLAT])
    out_d = nc.dram_tensor("out", [NSEQ, NLAT, D], F32, kind="ExternalOutput").ap()
    xt_d = nc.dram_tensor("xt_scratch", [NSEQ, KC, 128, NTOK], F32, kind="Internal").ap()

    with ExitStack() as st:
        fw = FW(nc, st)
        cst = fw.sb("cst", [128, 10 * 128])
        CST = fw.view(cst_d)
        fw.dma(cst[:], cst_d, reads=[CST], writes=[cst])
        identf = cst[:, 0:128]
        onesf = cst[:, 128:256]
        U = [cst[:, 256:384], cst[:, 384:512]]
        Ms = [cst[:, 512:640], cst[:, 640:768]]
        Mi = [U[1], U[0]]
        NB32 = cst[:, 768:896]
        B6432 = cst[:, 896:1024]
        NB64c = cst[:, 1024:1152]
        identb = fw.sb("identb", [128, 128], BF16)
        fw.op("dve", lambda e: e.tensor_copy(out=identb[:], in_=identf), reads=[cst], writes=[identb])
        fw.touch_junk = fw.sb("touch_junk", [128, 2])
        onesb = fw.sb("onesb", [128, 128], BF16)
        fw.op("dve", lambda e: e.tensor_copy(out=onesb[:], in_=onesf), reads=[cst], writes=[onesb])

        DUMPS = []

        dbg_stage = [None]

        def dump(name, buf, ap, n):
            if not DEBUG:
                return
            dt = nc.dram_tensor("dbg_" + name, [128, n], F32, kind="ExternalOutput").ap()
            tmp = dbg_stage[0]
            for c0 in range(0, n, 512):
                m = min(512, n - c0)
                fw.op("dve", lambda e: e.tensor_copy(out=tmp[:, 0:m], in_=ap[:, c0:c0 + m]), reads=[buf], writes=[tmp])
                v = fw.view(None)
                fw.dma(dt[:, c0:c0 + m], tmp[:, 0:m], reads=[tmp], writes=[v])
                DUMPS.append(v)

        if DEBUG:
            dbg_stage[0] = fw.sb("dbgs", [128, 512])
        banks = [fw.ps("bank%d" % i, [128, 512]) for i in range(7)]
        pbf = fw.ps("pbf", [128, 1024], BF16)

        XT = [[fw.view(None, "xt%d_%d" % (b, j)) for j in range(NT)] for b in range(NSEQ)]

        def xt_ap(b, t0, t1):
            return xt_d[b, :, :, t0:t1].rearrange("c p t -> p c t")

        def xt_bufs(b, t0, t1):
            return XT[b][t0 // 128:(t1 + 127) // 128]

        b_modT = fw.sb("b_modT", [128, DEPTH * 48])
        ln_gT = fw.sb("ln_gT", [128, DEPTH * 2 * KC])
        ln_bT = fw.sb("ln_bT", [128, DEPTH * 2 * KC])
        cT = fw.sb("cT", [128, KC * 3])
        for dst, src in ((b_modT, b_modT_d), (ln_gT, ln_gT_d), (ln_bT, ln_bT_d), (cT, cT_d)):
            fw.dma(dst[:], src, reads=[fw.view(src)], writes=[dst])
        fw.op("act", lambda e: e.activation(out=cT[:], in_=cT[:], func=AF.Silu), reads=[cT], writes=[cT])
        MOD = fw.sb("MOD", [128, DEPTH * 48 * 3])

        def mod_col(l, which, c, col):
            o = ((l * 48) + which * 8 + c) * 3 + col
            return MOD[:, o:o + 1]

        stg = []

        def alloc_stg(blk=512):
            stg[:] = [fw.sb("stg%d_%d" % (i, fw.n_inst), [128, KC * blk]) for i in range(2)]
        stg_i = [0]

        def load_w(dst, dst_ap_fn, src2d, ncols, kc=KC, col0=0, blk=512):
            for c0 in range(0, ncols, blk):
                n = min(blk, ncols - c0)
                s = stg[stg_i[0] % 2]
                stg_i[0] += 1
                sv = s[:, 0:kc * n].rearrange("p (k n) -> p k n", k=kc)
                src = src2d[:, col0 + c0:col0 + c0 + n].rearrange("(k p) n -> p k n", p=128)
                fw.dma(sv, src, reads=[], writes=[s])
                fw.op("pool", lambda e: e.tensor_copy(out=dst_ap_fn(c0, n), in_=sv), reads=[s], writes=[dst])

        def prologue():
            xin = [fw.sb("xin%d" % i, [128, D]) for i in range(2)]
            xo = [fw.sb("xo%d" % i, [128, D]) for i in range(2)]
            for b in range(NSEQ):
                for j in range(NT):
                    t = xin[j % 2]
                    o = xo[j % 2]
                    src = ctx_d[b, j * 128:(j + 1) * 128, :] if j < 2 else x_d[b, (j - 2) * 128:(j - 1) * 128, :]
                    fw.dma(t[:], src, reads=[], writes=[t])
                    for half in range(2):
                        bk = banks[(j * 2 + half) % 4]
                        for c in range(4):
                            cc = half * 4 + c
                            fw.tr(bk, bk[:, c * 128:(c + 1) * 128], t, t[:, cc * 128:(cc + 1) * 128], cst, identf)
                        fw.evac(o, o[:, half * 512:(half + 1) * 512], bk, bk[:])
                    fw.dma(xt_ap(b, j * 128, (j + 1) * 128), o[:].rearrange("p (c t) -> p c t", c=KC),
                           reads=[o], writes=[XT[b][j]])

            wmb = [fw.sb("wmb%d" % i, [128, KC * 512]) for i in range(2)]
            for l in layers:
                for fb in range(12):
                    wb = wmb[fb % 2]
                    src = w_mod_d[l][:, fb * 512:(fb + 1) * 512].rearrange("(k p) n -> p k n", p=128)
                    fw.dma(wb[:].rearrange("p (k n) -> p k n", k=KC), src, reads=[], writes=[wb])
                    bk = banks[fb % 2]
                    for f4 in range(4):
                        for k in range(KC):
                            fw.mm(bk, bk[:, f4 * 3:f4 * 3 + 3], wb, wb[:, k * 512 + f4 * 128:k * 512 + f4 * 128 + 128],
                                  cT, cT[:, k * 3:k * 3 + 3], start=(k == 0), stop=(k == KC - 1))
                    for f4 in range(4):
                        ch = fb * 4 + f4
                        o = (l * 48 + ch) * 3
                        fw.op("dve", lambda e: e.tensor_scalar(out=MOD[:, o:o + 3], in0=bk[:, f4 * 3:f4 * 3 + 3],
                                                               scalar1=b_modT[:, l * 48 + ch:l * 48 + ch + 1], scalar2=None,
                                                               op0=ALU.add), reads=[bk, b_modT], writes=[MOD])
                for which in (1, 4):
                    o = (l * 48 + which * 8) * 3
                    fw.op("dve", lambda e: e.tensor_scalar_add(out=MOD[:, o:o + 24], in0=MOD[:, o:o + 24], scalar1=1.0),
                          reads=[MOD], writes=[MOD])
                for which in (2, 5):
                    o = (l * 48 + which * 8) * 3
                    fw.op("dve", lambda e: e.tensor_scalar_mul(out=MOD[:, o:o + 24], in0=MOD[:, o:o + 24], scalar1=1.0 / ALPHA),
                          reads=[MOD], writes=[MOD])


        NCH = 256
        zt = fw.sb("zt", [128, KC * NCH])

        st_mean = fw.sb("st_mean", [128, NCH])
        st_var = fw.sb("st_var", [128, NCH])
        st_tmp = fw.sb("st_tmp", [128, NCH])
        xres = fw.sb("xres", [128, KC * NCH])
        sq = xres
        xnew = xres

        def resid_ln(l, sub, b, t0, n, col, y_fn):
            gw = 2 if sub == 0 else 5
            for o in range(KC):
                pb, pap = y_fn(o)
                fw.op("dve", lambda e: e.scalar_tensor_tensor(out=zt[:, o * NCH:o * NCH + n], in0=pap,
                                                              scalar=mod_col(l, gw, o, col),
                                                              in1=xres[:, o * NCH:o * NCH + n],
                                                              op0=ALU.mult, op1=ALU.add),
                      reads=[pb, MOD, xres], writes=[zt])
            fw.op("act", lambda e: e.activation(out=sq[:], in_=zt[:], func=AF.Square), reads=[zt], writes=[sq])
            bs, bq = banks[5], banks[6]
            for o in range(KC):
                fw.mm(bs, bs[:, 0:n], cst, onesf, zt, zt[:, o * NCH:o * NCH + n], start=(o == 0), stop=(o == KC - 1))
            for o in range(KC):
                fw.mm(bq, bq[:, 0:n], cst, onesf, sq, sq[:, o * NCH:o * NCH + n], start=(o == 0), stop=(o == KC - 1))
            fw.op("act", lambda e: e.mul(out=st_mean[:, 0:n], in_=bs[:, 0:n], mul=1.0 / D), reads=[bs], writes=[st_mean])
            fw.op("dve", lambda e: e.tensor_tensor(out=st_tmp[:, 0:n], in0=st_mean[:, 0:n], in1=st_mean[:, 0:n], op=ALU.mult),
                  reads=[st_mean], writes=[st_tmp])
            fw.op("dve", lambda e: e.scalar_tensor_tensor(out=st_var[:, 0:n], in0=bq[:, 0:n], scalar=1.0 / D,
                                                          in1=st_tmp[:, 0:n], op0=ALU.mult, op1=ALU.subtract),
                  reads=[bq, st_tmp], writes=[st_var])
            fw.op("dve", lambda e: e.tensor_scalar(out=st_var[:, 0:n], in0=st_var[:, 0:n], scalar1=0.0, scalar2=LN_EPS_S,
                                                   op0=ALU.max, op1=ALU.add), reads=[st_var], writes=[st_var])
            fw.op("act", lambda e: e.activation(out=st_var[:, 0:n], in_=st_var[:, 0:n], func=AF.Sqrt), reads=[st_var], writes=[st_var])
            fw.op("dve", lambda e: e.reciprocal(out=st_var[:, 0:n], in_=st_var[:, 0:n]), reads=[st_var], writes=[st_var])
            gi = (l * 2 + sub) * KC
            for o in range(KC):
                eng = "dve" if o % 2 == 0 else "pool"
                fw.op(eng, lambda e: e.tensor_tensor(out=zt[:, o * NCH:o * NCH + n], in0=zt[:, o * NCH:o * NCH + n],
                                                     in1=st_mean[:, 0:n], op=ALU.subtract), reads=[zt, st_mean], writes=[zt])
                fw.op(eng, lambda e: e.tensor_tensor(out=zt[:, o * NCH:o * NCH + n], in0=zt[:, o * NCH:o * NCH + n],
                                                     in1=st_var[:, 0:n], op=ALU.mult), reads=[zt, st_var], writes=[zt])
                fw.op("dve", lambda e: e.tensor_scalar(out=xnew[:, o * NCH:o * NCH + n], in0=zt[:, o * NCH:o * NCH + n],
                                                       scalar1=ln_gT[:, gi + o:gi + o + 1], scalar2=ln_bT[:, gi + o:gi + o + 1],
                                                       op0=ALU.mult, op1=ALU.add), reads=[zt, ln_gT, ln_bT], writes=[xnew])
            fw.dma(xt_ap(b, t0, t0 + n), xnew[:].rearrange("p (c t) -> p c t", c=KC)[:, :, 0:n],
                   reads=[xnew], writes=xt_bufs(b, t0, t0 + n))

        def load_xres(b, t0, n):
            fw.dma(xres[:].rearrange("p (c t) -> p c t", c=KC)[:, :, 0:n], xt_ap(b, t0, t0 + n),
                   reads=xt_bufs(b, t0, t0 + n), writes=[xres])

        def ffn_pass(l, w1b, w2b, hT, aT, rtmp):
            for b in range(NSEQ):
                for ci in range(NTOK // NCH):
                    t0 = ci * NCH
                    n = NCH
                    col = 2 if ci == 0 else b
                    load_xres(b, t0, n)
                    for c in range(KC):
                        fw.op("dve" if c % 2 else "pool",
                              lambda e: e.tensor_scalar(out=hT[:, c * NCH:c * NCH + n], in0=xres[:, c * NCH:c * NCH + n],
                                                        scalar1=mod_col(l, 4, c, col), scalar2=mod_col(l, 3, c, col),
                                                        op0=ALU.mult, op1=ALU.add), reads=[xres, MOD], writes=[hT])
                    for f in range(32):
                        bk = banks[f % 4]
                        for k in range(KC):
                            fw.mm(bk, bk[:, 0:n], w1b, w1b[:, k * DFF + f * 128:k * DFF + f * 128 + 128],
                                  hT, hT[:, k * NCH:k * NCH + n], start=(k == 0), stop=(k == KC - 1))
                        r = rtmp[f % 2]
                        fw.op("act", lambda e: e.activation(out=r[:, 0:n], in_=bk[:, 0:n], func=AF.Relu), reads=[bk], writes=[r])
                        fw.op("dve" if f % 2 else "pool",
                              lambda e: e.tensor_tensor(out=aT[:, f * NCH:f * NCH + n], in0=r[:, 0:n], in1=r[:, 0:n], op=ALU.mult),
                              reads=[r], writes=[aT])

                    def y_fn(o):
                        bk = banks[4]
                        for f in range(32):
                            fw.mm(bk, bk[:, 0:n], w2b, w2b[:, f * D + o * 128:f * D + o * 128 + 128],
                                  aT, aT[:, f * NCH:f * NCH + n], start=(f == 0), stop=(f == 31))
                        return bk, bk[:, 0:n]
                    resid_ln(l, 1, b, t0, n, col, y_fn)

        def mixer_epilogue(l, b, Y, woutb, YT):
            for ci in range(NTOK // NCH):
                t0 = ci * NCH
                n = NCH
                col = 2 if ci == 0 else b
                load_xres(b, t0, n)
                for jj in range(2):
                    j = ci * 2 + jj
                    for c in range(KC):
                        fw.tr(pbf, pbf[:, c * 128:c * 128 + 128], Y, Y[:, j * D + c * 128:j * D + c * 128 + 128], identb, identb[:])
                    for c in range(KC):
                        fw.evac(YT, YT[:, c * NCH + jj * 128:c * NCH + jj * 128 + 128], pbf, pbf[:, c * 128:c * 128 + 128])

                def y_fn(o):
                    bk = banks[4]
                    for k in range(KC):
                        fw.mm(bk, bk[:, 0:n], woutb, woutb[:, k * D + o * 128:k * D + o * 128 + 128],
                              YT, YT[:, k * NCH:k * NCH + n], start=(k == 0), stop=(k == KC - 1))
                    return bk, bk[:, 0:n]
                resid_ln(l, 0, b, t0, n, col, y_fn)


        def mixer_attn(l, jj):
            lam_init = 0.8 - 0.6 * math.exp(-0.3 * l)
            woutb = fw.sb("woutb", [128, KC * D], BF16)
            YT = fw.sb("YT", [128, KC * NCH], BF16)
            hT = fw.sb("hT_seq", [128, KC * NTOK], BF16)
            Y = fw.sb("Y_seq", [128, NT * D], BF16)
            wh = [fw.sb("wh%d" % i, [128, KC * 128], BF16) for i in range(5)]
            KT = fw.sb("KT", [128, NTOK], BF16)
            QT = fw.sb("QT", [128, NTOK], BF16)
            Va = fw.sb("Va", [128, NT * 130], BF16)
            PT = [fw.sb("PT%d" % i, [128, 512], BF16) for i in range(3)]
            O1 = fw.sb("O1", [128, 4 * 128])
            ot = fw.sb("ot", [128, 128])
            rr_ = fw.sb("rr_", [128, 8])
            t1 = fw.sb("t1", [128, 512])
            t2 = fw.sb("t2", [128, 512])
            rope = fw.sb("rope", [128, 2 * NLAT])
            subl = fw.sb("subl", [128, 128])
            lamt = fw.sb("lamt", [1, 256])
            lamp = fw.sb("lamp", [1, 128])
            lams = fw.sb("lams", [1, 4])
            neglam = fw.sb("neglam", [128, 1])
            junk = fw.sb("junk", [128, 128])
            alloc_stg()
            fw.dma(rope[:], rope_d, reads=[], writes=[rope])
            fw.dma(subl[:], subln_b_d[jj:jj + 1, :].partition_broadcast(128), reads=[], writes=[subl])
            fw.op("dve", lambda e: e.tensor_scalar_mul(out=subl[:], in0=subl[:], scalar1=1.0 - lam_init), reads=[subl], writes=[subl])
            fw.dma(lamt[:], lam_b_d[jj:jj + 1].rearrange("o a b -> o (a b)"), reads=[], writes=[lamt])
            fw.op("dve", lambda e: e.tensor_tensor(out=lamp[:].rearrange("o (a b) -> o a b", a=2), in0=lamt[:].rearrange("o (a t b) -> o a t b", a=2, t=2)[:, :, 0, :],
                                                   in1=lamt[:].rearrange("o (a t b) -> o a t b", a=2, t=2)[:, :, 1, :], op=ALU.mult), reads=[lamt], writes=[lamp])
            for a in range(2):
                fw.op("dve", lambda e: e.reduce_sum(out=lams[:, a:a + 1], in_=lamp[:, a * 64:(a + 1) * 64], axis=mybir.AxisListType.X), reads=[lamp], writes=[lams])
            fw.op("act", lambda e: e.activation(out=lams[:, 0:2], in_=lams[:, 0:2], func=AF.Exp), reads=[lams], writes=[lams])
            fw.op("dve", lambda e: e.scalar_tensor_tensor(out=lams[:, 2:3], in0=lams[:, 1:2], scalar=-lam_init, in1=lams[:, 0:1], op0=ALU.add, op1=ALU.subtract),
                  reads=[lams], writes=[lams])
            bk0 = banks[0]
            fw.mm(bk0, bk0[:, 0:1], cst, onesf[0:1, :], lams, lams[:, 2:3])
            fw.evac(neglam, neglam[:], bk0, bk0[:, 0:1])
            load_w(woutb, lambda c0, n: woutb[:].rearrange("p (k f) -> p k f", k=KC)[:, :, c0:c0 + n], w_out_b_d[jj], D)
            fw.op("pool", lambda e: e.memset(Va[:], 1.0), reads=[], writes=[Va])
            cosT = rope[:, 0:NLAT]
            sinT = rope[:, NLAT:2 * NLAT]
            chunks = [(0, 256)] + [(256 + i * 512, 512) for i in range(4)]
            for b in range(NSEQ):
                for ci in range(NTOK // NCH):
                    t0 = ci * NCH
                    col = 2 if ci == 0 else b
                    load_xres(b, t0, NCH)
                    for c in range(KC):
                        fw.op("dve" if c % 2 else "pool",
                              lambda e: e.tensor_scalar(out=hT[:, c * NTOK + t0:c * NTOK + t0 + NCH], in0=xres[:, c * NCH:(c + 1) * NCH],
                                                        scalar1=mod_col(l, 1, c, col), scalar2=mod_col(l, 0, c, col),
                                                        op0=ALU.mult, op1=ALU.add), reads=[xres, MOD], writes=[hT])
                for h in range(8):
                    srcs = [(w_in_b_d[jj], h * 128), (w_in_b_d[jj], 1024 + h * 128), (w_in_b_d[jj], 2048 + h * 128),
                            (w_in_bp_d[jj], h * 128), (w_in_bp_d[jj], 1024 + h * 128)]
                    for wi, (src, c0) in enumerate(srcs):
                        load_w(wh[wi], lambda cc, n, wi=wi: wh[wi][:].rearrange("p (k f) -> p k f", k=KC)[:, :, cc:cc + n], src, 128, col0=c0, blk=128)
                    wq, wk, wv, wqp, wkp = wh
                    for (dst, wa, wp) in ((KT, wk, wkp), (QT, wq, wqp)):
                        for (t0, n) in chunks:
                            pa, pb = banks[0], banks[1]
                            for k in range(KC):
                                fw.mm(pa, pa[:, 0:n], wa, wa[:, k * 128:(k + 1) * 128], hT, hT[:, k * NTOK + t0:k * NTOK + t0 + n],
                                      start=(k == 0), stop=(k == KC - 1))
                            if t0 < NCTX:
                                fw.evac(dst, dst[:, t0:t0 + n], pa, pa[:, 0:n])
                                continue
                            for k in range(KC):
                                fw.mm(pb, pb[:, 0:n], wp, wp[:, k * 128:(k + 1) * 128], hT, hT[:, k * NTOK + t0:k * NTOK + t0 + n],
                                      start=(k == 0), stop=(k == KC - 1))
                            lt = t0 - NCTX
                            fw.op("dve", lambda e: e.tensor_tensor(out=t1[:, 0:n], in0=pa[:, 0:n], in1=cosT[:, lt:lt + n], op=ALU.mult), reads=[pa, rope], writes=[t1])
                            fw.op("dve", lambda e: e.tensor_tensor(out=t2[:, 0:n], in0=pb[:, 0:n], in1=sinT[:, lt:lt + n], op=ALU.mult), reads=[pb, rope], writes=[t2])
                            fw.op("pool", lambda e: e.tensor_tensor(out=dst[:, t0:t0 + n], in0=t1[:, 0:n], in1=t2[:, 0:n], op=ALU.add), reads=[t1, t2], writes=[dst])
                    for j in range(NT):
                        pv = banks[j % 2]
                        for k in range(KC):
                            fw.mm(pv, pv[:, 0:128], hT, hT[:, k * NTOK + j * 128:k * NTOK + (j + 1) * 128], wv, wv[:, k * 128:(k + 1) * 128],
                                  start=(k == 0), stop=(k == KC - 1))
                        fw.evac(Va, Va[:, j * 130:j * 130 + 128], pv, pv[:, 0:128])
                    pti = 0
                    for (q0, nq) in chunks:
                        ktiles = [0, 1] if q0 < NCTX else list(range(NT))
                        nqs = nq // 128
                        for t in range(2):
                            r0 = t * 64
                            for ki, kt in enumerate(ktiles):
                                ps_ = banks[ki % 2]
                                fw.mm(ps_, ps_[:, 0:nq], KT, KT[r0:r0 + 64, kt * 128:(kt + 1) * 128], QT, QT[r0:r0 + 64, q0:q0 + nq])
                                pt = PT[pti % 3]
                                pti += 1
                                fw.op("act", lambda e: e.activation(out=pt[:, 0:nq], in_=ps_[:, 0:nq], func=AF.Exp, scale=0.125), reads=[ps_], writes=[pt])
                                for qs in range(nqs):
                                    ob = banks[2 + qs]
                                    fw.mm(ob, ob[:, 0:129], pt, pt[:, qs * 128:(qs + 1) * 128], Va, Va[:, kt * 130:kt * 130 + 129],
                                          start=(ki == 0), stop=(ki == len(ktiles) - 1))
                            for qs in range(nqs):
                                ob = banks[2 + qs]
                                fw.op("dve", lambda e: e.reciprocal(out=rr_[:, qs:qs + 1], in_=ob[:, 128:129]), reads=[ob], writes=[rr_])
                                if t == 0:
                                    fw.op("dve", lambda e: e.tensor_scalar(out=O1[:, qs * 128:(qs + 1) * 128], in0=ob[:, 0:128], scalar1=rr_[:, qs:qs + 1],
                                                                           scalar2=None, op0=ALU.mult), reads=[ob, rr_], writes=[O1])
                                else:
                                    tile_j = (q0 + qs * 128) // 128
                                    fw.op("dve", lambda e: e.tensor_scalar(out=ot[:], in0=ob[:, 0:128], scalar1=rr_[:, qs:qs + 1],
                                                                           scalar2=neglam[:, 0:1], op0=ALU.mult, op1=ALU.mult), reads=[ob, rr_, neglam], writes=[ot])
                                    fw.op("dve", lambda e: e.tensor_tensor(out=ot[:], in0=ot[:], in1=O1[:, qs * 128:(qs + 1) * 128], op=ALU.add), reads=[ot, O1], writes=[ot])
                                    fw.op("act", lambda e: e.activation(out=junk[:], in_=ot[:], func=AF.Square, accum_out=rr_[:, 4 + qs:5 + qs]), reads=[ot], writes=[junk, rr_])
                                    fw.op("dve", lambda e: e.tensor_scalar(out=rr_[:, 4 + qs:5 + qs], in0=rr_[:, 4 + qs:5 + qs], scalar1=1.0 / 128, scalar2=EPS,
                                                                           op0=ALU.mult, op1=ALU.add), reads=[rr_], writes=[rr_])
                                    fw.op("act", lambda e: e.activation(out=rr_[:, 4 + qs:5 + qs], in_=rr_[:, 4 + qs:5 + qs], func=AF.Sqrt), reads=[rr_], writes=[rr_])
                                    fw.op("dve", lambda e: e.reciprocal(out=rr_[:, 4 + qs:5 + qs], in_=rr_[:, 4 + qs:5 + qs]), reads=[rr_], writes=[rr_])
                                    fw.op("dve", lambda e: e.scalar_tensor_tensor(out=Y[:, tile_j * D + h * 128:tile_j * D + (h + 1) * 128], in0=ot[:],
                                                                                  scalar=rr_[:, 4 + qs:5 + qs], in1=subl[:], op0=ALU.mult, op1=ALU.mult),
                                          reads=[ot, rr_, subl], writes=[Y])
                    if b == 0 and h == 0:
                        dump("KT", KT, KT[:], NTOK)
                        dump("QT", QT, QT[:], NTOK)
                        dump("Va", Va, Va[:], NT * 130)
                        for j_ in range(NT):
                            dump("Y%d" % j_, Y, Y[:, j_ * D:j_ * D + 128], 128)
                        dump("neglam", neglam, neglam[:], 1)
                        dump("hT", hT, hT[:, 0:NTOK], NTOK)
                mixer_epilogue(l, b, Y, woutb, YT)

        def mixer_mlstm(l, jj):
            woutb = fw.sb("woutb", [128, KC * D], BF16)
            YT = fw.sb("YT", [128, KC * NCH], BF16)
            hT = fw.sb("hT_seq", [128, KC * NTOK], BF16)
            Y = fw.sb("Y_seq", [128, NT * D], BF16)
            wq = fw.sb("wq", [128, KC * 128], BF16)
            wk = fw.sb("wk", [128, KC * 128], BF16)
            wv = fw.sb("wv", [128, KC * 256], BF16)
            wo = fw.sb("wo", [128, KC * 256], BF16)
            wg = fw.sb("wg", [128, KC * 16], BF16)
            KT = fw.sb("KT", [128, NTOK], BF16)
            QT = fw.sb("QT", [128, NTOK], BF16)
            Va = fw.sb("Va", [128, NT * 258], BF16)
            Pb = fw.sb("Pb", [128, NTOK])
            Hacc = fw.sb("Hacc", [128, NT * 256])
            Dt = [fw.sb("Dt%d" % i, [128, 512]) for i in range(2)]
            WT = [fw.sb("WT%d" % i, [128, 512], BF16) for i in range(2)]
            tmpd = fw.sb("tmpd", [128, 128])
            Gt = fw.sb("Gt", [128, NT * 16])
            LF = fw.sb("LF", [128, NT * 16])
            CW = fw.sb("CW", [128, NT * 8])
            TOT = fw.sb("TOT", [128, NT * 8])
            PC = fw.sb("PC", [128, NT * 8])
            AC = fw.sb("AC", [128, NT * 8])
            carry = fw.sb("carry", [128, 4])
            GB = fw.sb("GB", [128, 16])
            normc = fw.sb("normc", [128, D])
            MB = [fw.sb("MB%d" % i, [128, 128]) for i in range(2)]
            sm = fw.sb("sm", [128, 8])
            og = fw.sb("og", [128, 256])
            junk = fw.sb("junk", [128, 256])
            alloc_stg(128)
            fw.dma(GB[:], gbias_c_d[jj:jj + 1, :].partition_broadcast(128), reads=[], writes=[GB])
            fw.dma(normc[:], norm_c_d[jj:jj + 1, :].partition_broadcast(128), reads=[], writes=[normc])
            for dr in range(2):
                fw.op("dve", lambda e: e.tensor_scalar(out=MB[dr][:], in0=U[dr], scalar1=-1.0, scalar2=30000.0, op0=ALU.add, op1=ALU.mult),
                      reads=[cst], writes=[MB[dr]])
            load_w(woutb, lambda c0, n: woutb[:].rearrange("p (k f) -> p k f", k=KC)[:, :, c0:c0 + n], w_out_c_d[jj], D, blk=128)
            load_w(wg, lambda c0, n: wg[:].rearrange("p (k f) -> p k f", k=KC)[:, :, c0:c0 + n], w_in_c_d[jj], 16, col0=3072, blk=16)
            fw.op("pool", lambda e: e.memset(Va[:], 1.0), reads=[], writes=[Va])
            chunks = [(0, 256)] + [(256 + i * 512, 512) for i in range(4)]
            for b in range(NSEQ):
                for ci in range(NTOK // NCH):
                    t0 = ci * NCH
                    col = 2 if ci == 0 else b
                    load_xres(b, t0, NCH)
                    for c in range(KC):
                        fw.op("dve" if c % 2 else "pool",
                              lambda e: e.tensor_scalar(out=hT[:, c * NTOK + t0:c * NTOK + t0 + NCH], in0=xres[:, c * NCH:(c + 1) * NCH],
                                                        scalar1=mod_col(l, 1, c, col), scalar2=mod_col(l, 0, c, col),
                                                        op0=ALU.mult, op1=ALU.add), reads=[xres, MOD], writes=[hT])
                for j in range(NT):
                    pg = banks[j % 2]
                    for k in range(KC):
                        fw.mm(pg, pg[:, 0:16], hT, hT[:, k * NTOK + j * 128:k * NTOK + (j + 1) * 128], wg, wg[:, k * 16:(k + 1) * 16],
                              start=(k == 0), stop=(k == KC - 1))
                    fw.op("dve", lambda e: e.tensor_tensor(out=Gt[:, j * 16:(j + 1) * 16], in0=pg[:, 0:16], in1=GB[:], op=ALU.add), reads=[pg, GB], writes=[Gt])
                fw.op("act", lambda e: e.activation(out=LF[:], in_=Gt[:], func=AF.Exp, scale=-1.0), reads=[Gt], writes=[LF])
                fw.op("act", lambda e: e.activation(out=LF[:], in_=LF[:], func=AF.Ln, bias=1.0, scale=1.0), reads=[LF], writes=[LF])
                fw.op("dve", lambda e: e.tensor_scalar_mul(out=LF[:], in0=LF[:], scalar1=-1.0), reads=[LF], writes=[LF])
                for dr in range(2):
                    pc_, pt_ = banks[0], banks[1]
                    for j in range(NT):
                        fcol = j * 16 + (1 + 2 * dr) * 4
                        fw.mm(pc_, pc_[:, j * 4:(j + 1) * 4], cst, U[dr], LF, LF[:, fcol:fcol + 4])
                        fw.mm(pt_, pt_[:, j * 4:(j + 1) * 4], cst, onesf, LF, LF[:, fcol:fcol + 4])
                    cwv = CW[:].rearrange("p (j d h) -> p j d h", j=NT, d=2)[:, :, dr, :]
                    totv = TOT[:].rearrange("p (j d h) -> p j d h", j=NT, d=2)[:, :, dr, :]
                    fw.op("dve", lambda e: e.tensor_copy(out=cwv, in_=pc_[:, 0:NT * 4].rearrange("p (j h) -> p j h", j=NT)), reads=[pc_], writes=[CW])
                    fw.op("act", lambda e: e.copy(out=totv, in_=pt_[:, 0:NT * 4].rearrange("p (j h) -> p j h", j=NT)), reads=[pt_], writes=[TOT])
                    order = list(range(NT)) if dr == 0 else [1, 0] + list(range(NT - 1, 1, -1))
                    fw.op("dve", lambda e: e.memset(carry[:], 0.0), reads=[], writes=[carry])
                    for j in order:
                        o8 = j * 8 + dr * 4
                        fw.op("dve", lambda e: e.tensor_tensor(out=PC[:, o8:o8 + 4], in0=CW[:, o8:o8 + 4], in1=carry[:], op=ALU.add), reads=[CW, carry], writes=[PC])
                        fw.op("dve", lambda e: e.tensor_tensor(out=carry[:], in0=carry[:], in1=TOT[:, o8:o8 + 4], op=ALU.add), reads=[carry, TOT], writes=[carry])
                    liv = Gt[:].rearrange("p (j g h) -> p j g h", j=NT, g=4)[:, :, 2 * dr, :]
                    pcv = PC[:].rearrange("p (j d h) -> p j d h", j=NT, d=2)[:, :, dr, :]
                    acv = AC[:].rearrange("p (j d h) -> p j d h", j=NT, d=2)[:, :, dr, :]
                    fw.op("dve", lambda e: e.tensor_tensor(out=acv, in0=liv, in1=pcv, op=ALU.subtract), reads=[Gt, PC], writes=[AC])
                for h in range(4):
                    load_w(wq, lambda c0, n: wq[:].rearrange("p (k f) -> p k f", k=KC)[:, :, c0:c0 + n], w_in_c_d[jj], 128, col0=h * 128, blk=128)
                    load_w(wk, lambda c0, n: wk[:].rearrange("p (k f) -> p k f", k=KC)[:, :, c0:c0 + n], w_in_c_d[jj], 128, col0=512 + h * 128, blk=128)
                    load_w(wv, lambda c0, n: wv[:].rearrange("p (k f) -> p k f", k=KC)[:, :, c0:c0 + n], w_in_c_d[jj], 256, col0=1024 + h * 256, blk=128)
                    load_w(wo, lambda c0, n: wo[:].rearrange("p (k f) -> p k f", k=KC)[:, :, c0:c0 + n], w_in_c_d[jj], 256, col0=2048 + h * 256, blk=128)
                    for (dst, wa, sc) in ((KT, wk, 128.0 ** -0.5), (QT, wq, 1.0)):
                        for (t0, n) in chunks:
                            pa = banks[(t0 // 256) % 2]
                            for k in range(KC):
                                fw.mm(pa, pa[:, 0:n], wa, wa[:, k * 128:(k + 1) * 128], hT, hT[:, k * NTOK + t0:k * NTOK + t0 + n],
                                      start=(k == 0), stop=(k == KC - 1))
                            fw.op("act", lambda e: e.mul(out=dst[:, t0:t0 + n], in_=pa[:, 0:n], mul=sc), reads=[pa], writes=[dst])
                    for j in range(NT):
                        pv = banks[j % 2]
                        for k in range(KC):
                            fw.mm(pv, pv[:, 0:256], hT, hT[:, k * NTOK + j * 128:k * NTOK + (j + 1) * 128], wv, wv[:, k * 256:(k + 1) * 256],
                                  start=(k == 0), stop=(k == KC - 1))
                        fw.evac(Va, Va[:, j * 258:j * 258 + 256], pv, pv[:, 0:256])
                    for dr in range(2):
                        for j in range(NT):
                            pp = banks[j % 2]
                            c8 = j * 8 + dr * 4 + h
                            fw.mm(pp, pp[:, 0:128], PC, PC[:, c8:c8 + 1].to_broadcast([128, 128]), cst, identf)
                            fw.evac(Pb, Pb[:, j * 128:(j + 1) * 128], pp, pp[:, 0:128])

                        def vis(js, jt):
                            s_ctx, t_ctx = js < 2, jt < 2
                            if t_ctx and not s_ctx:
                                return 0
                            if s_ctx and not t_ctx:
                                return 1
                            if js == jt:
                                return 2
                            if dr == 0:
                                return 1 if js < jt else 0
                            return 1 if js > jt else 0
                        bi = 0
                        for (q0, nq) in chunks:
                            ttiles = list(range(q0 // 128, (q0 + nq) // 128))
                            slist = [js for js in range(NT) if any(vis(js, jt) for jt in ttiles)]
                            first = {}
                            last = {}
                            for js in slist:
                                for jt in ttiles:
                                    if vis(js, jt):
                                        first.setdefault(jt, js)
                                        last[jt] = js
                            for js in slist:
                                need = [jt for jt in ttiles if vis(js, jt)]
                                a0, a1 = need[0] * 128, (need[-1] + 1) * 128
                                n = a1 - a0
                                ps_ = banks[bi % 2]
                                dt_ = Dt[bi % 2]
                                wt_ = WT[bi % 2]
                                bi += 1
                                fw.mm(ps_, ps_[:, 0:n], KT, KT[:, js * 128:(js + 1) * 128], QT, QT[:, a0:a1])
                                acol = AC[:, js * 8 + dr * 4 + h:js * 8 + dr * 4 + h + 1]
                                full = [jt for jt in need if vis(js, jt) == 1]
                                diag = [jt for jt in need if vis(js, jt) == 2]
                                if full:
                                    f0, f1 = full[0] * 128, (full[-1] + 1) * 128
                                    fw.op("act", lambda e: e.activation(out=dt_[:, f0 - a0:f1 - a0], in_=Pb[:, f0:f1], func=AF.Exp, bias=acol, scale=1.0),
                                          reads=[Pb, AC], writes=[dt_])
                                if diag:
                                    d0 = diag[0] * 128
                                    fw.op("dve", lambda e: e.tensor_tensor(out=tmpd[:], in0=Pb[:, d0:d0 + 128], in1=MB[dr][:], op=ALU.add), reads=[Pb, MB[dr]], writes=[tmpd])
                                    fw.op("act", lambda e: e.activation(out=dt_[:, d0 - a0:d0 - a0 + 128], in_=tmpd[:], func=AF.Exp, bias=acol, scale=1.0),
                                          reads=[tmpd, AC], writes=[dt_])
                                fw.op("dve", lambda e: e.tensor_tensor(out=wt_[:, 0:n], in0=ps_[:, 0:n], in1=dt_[:, 0:n], op=ALU.mult), reads=[ps_, dt_], writes=[wt_])
                                for jt in need:
                                    ob = banks[2 + (jt - ttiles[0])]
                                    fw.mm(ob, ob[:, 0:257], wt_, wt_[:, jt * 128 - a0:jt * 128 - a0 + 128], Va, Va[:, js * 258:js * 258 + 257],
                                          start=(first[jt] == js), stop=(last[jt] == js))
                            for jt in ttiles:
                                ob = banks[2 + (jt - ttiles[0])]
                                fw.op("act", lambda e: e.activation(out=sm[:, 0:1], in_=ob[:, 256:257], func=AF.Abs), reads=[ob], writes=[sm])
                                fw.op("dve", lambda e: e.tensor_scalar_max(out=sm[:, 0:1], in0=sm[:, 0:1], scalar1=1.0), reads=[sm], writes=[sm])
                                fw.op("dve", lambda e: e.reciprocal(out=sm[:, 0:1], in_=sm[:, 0:1]), reads=[sm], writes=[sm])
                                hv = Hacc[:, jt * 256:(jt + 1) * 256]
                                if dr == 0:
                                    fw.op("dve", lambda e: e.tensor_scalar(out=hv, in0=ob[:, 0:256], scalar1=sm[:, 0:1], scalar2=None, op0=ALU.mult), reads=[ob, sm], writes=[Hacc])
                                else:
                                    fw.op("dve", lambda e: e.scalar_tensor_tensor(out=hv, in0=ob[:, 0:256], scalar=sm[:, 0:1], in1=hv, op0=ALU.mult, op1=ALU.add),
                                          reads=[ob, sm, Hacc], writes=[Hacc])
                    for j in range(NT):
                        po = banks[j % 2]
                        for k in range(KC):
                            fw.mm(po, po[:, 0:256], hT, hT[:, k * NTOK + j * 128:k * NTOK + (j + 1) * 128], wo, wo[:, k * 256:(k + 1) * 256],
                                  start=(k == 0), stop=(k == KC - 1))
                        fw.op("act", lambda e: e.activation(out=og[:], in_=po[:, 0:256], func=AF.Sigmoid), reads=[po], writes=[og])
                        hv = Hacc[:, j * 256:(j + 1) * 256]
                        fw.op("act", lambda e: e.activation(out=junk[:], in_=hv, func=AF.Square, accum_out=sm[:, 1:2]), reads=[Hacc], writes=[junk, sm])
                        fw.op("dve", lambda e: e.tensor_scalar(out=sm[:, 1:2], in0=sm[:, 1:2], scalar1=1.0 / 256, scalar2=EPS, op0=ALU.mult, op1=ALU.add), reads=[sm], writes=[sm])
                        fw.op("act", lambda e: e.activation(out=sm[:, 1:2], in_=sm[:, 1:2], func=AF.Sqrt), reads=[sm], writes=[sm])
                        fw.op("dve", lambda e: e.reciprocal(out=sm[:, 1:2], in_=sm[:, 1:2]), reads=[sm], writes=[sm])
                        fw.op("dve", lambda e: e.scalar_tensor_tensor(out=og[:], in0=og[:], scalar=sm[:, 1:2], in1=normc[:, h * 256:(h + 1) * 256], op0=ALU.mult, op1=ALU.mult),
                              reads=[og, sm, normc], writes=[og])
                        fw.op("dve", lambda e: e.tensor_tensor(out=Y[:, j * D + h * 256:j * D + (h + 1) * 256], in0=og[:], in1=hv, op=ALU.mult), reads=[og, Hacc], writes=[Y])
                mixer_epilogue(l, b, Y, woutb, YT)

        def mixer_gdn(l, jj):
            convw = fw.sb("convw", [128, 240])
            normg = fw.sb("normg", [128, 128])
            DTB = fw.sb("DTB", [128, 32])
            NEGA = fw.sb("NEGA", [128, 32])
            fw.dma(convw[:], conv_aT_d, reads=[], writes=[convw])
            fw.dma(normg[:], norm_a_d[jj:jj + 1, :].partition_broadcast(128), reads=[], writes=[normg])
            fw.op("dve", lambda e: e.memset(DTB[:], 0.0), reads=[], writes=[DTB])
            fw.op("dve", lambda e: e.memset(NEGA[:], 0.0), reads=[], writes=[NEGA])
            for dr in range(2):
                fw.dma(DTB[:, dr * 16:dr * 16 + 8], dtb_d[jj:jj + 1, dr * 8:dr * 8 + 8].partition_broadcast(128), reads=[], writes=[DTB])
                fw.dma(NEGA[:, dr * 16:dr * 16 + 8], alog_d[jj:jj + 1, dr * 8:dr * 8 + 8].partition_broadcast(128), reads=[], writes=[NEGA])
            fw.op("act", lambda e: e.activation(out=NEGA[:], in_=NEGA[:], func=AF.Exp), reads=[NEGA], writes=[NEGA])
            fw.op("dve", lambda e: e.tensor_scalar_mul(out=NEGA[:], in0=NEGA[:], scalar1=-1.0), reads=[NEGA], writes=[NEGA])
            chunks = [(0, 256)] + [(256 + i * 512, 512) for i in range(4)]
            slots = [[SlotView(banks[2 * dr + q % 2], banks[2 * dr + q % 2][:, (q // 2) * 128:(q // 2 + 1) * 128]) for q in range(8)] for dr in range(2)]
            si = [0, 0]
            uid = [0]

            def mmq(dr, lb, lap, rb, rap):
                sl = slots[dr][si[dr] % 8]
                si[dr] += 1
                fw.mm(sl, sl.t, lb, lap, rb, rap)
                return sl

            mmq.slots = slots
            mmq.si = si
            for b in range(NSEQ):
                with fw.scope():
                    hT = fw.sb("hT_seq", [128, KC * NTOK], BF16)
                    ZY = fw.sb("ZY_seq", [128, NT * D], BF16)
                    Gt = fw.sb("Gt", [128, NT * 32])
                    GL = fw.sb("GL", [128, NT * 32])
                    BT = fw.sb("BT", [128, NT * 32])
                    CC = fw.sb("CC", [128, NT * 16])
                    for ci in range(NTOK // NCH):
                        t0 = ci * NCH
                        col = 2 if ci == 0 else b
                        load_xres(b, t0, NCH)
                        for c in range(KC):
                            fw.op("dve" if c % 2 else "pool",
                                  lambda e: e.tensor_scalar(out=hT[:, c * NTOK + t0:c * NTOK + t0 + NCH], in0=xres[:, c * NCH:(c + 1) * NCH],
                                                            scalar1=mod_col(l, 1, c, col), scalar2=mod_col(l, 0, c, col),
                                                            op0=ALU.mult, op1=ALU.add), reads=[xres, MOD], writes=[hT])
                    with fw.scope():
                        alloc_stg(128)
                        wz = fw.sb("wz", [128, KC * 128], BF16)
                        wgt = fw.sb("wgt", [128, KC * 32], BF16)
                        for hb in range(8):
                            load_w(wz, lambda c0, n: wz[:].rearrange("p (k f) -> p k f", k=KC)[:, :, c0:c0 + n], w_in_a_d[jj], 128, col0=3072 + hb * 128, blk=128)
                            for j in range(NT):
                                pz = banks[4 + j % 3]
                                for k in range(KC):
                                    fw.mm(pz, pz[:, 0:128], hT, hT[:, k * NTOK + j * 128:k * NTOK + (j + 1) * 128], wz, wz[:, k * 128:(k + 1) * 128],
                                          start=(k == 0), stop=(k == KC - 1))
                                fw.op("act", lambda e: e.activation(out=ZY[:, j * D + hb * 128:j * D + (hb + 1) * 128], in_=pz[:, 0:128], func=AF.Silu), reads=[pz], writes=[ZY])
                        load_w(wgt, lambda c0, n: wgt[:].rearrange("p (k f) -> p k f", k=KC)[:, :, c0:c0 + n], w_in_a_d[jj], 32, col0=4096, blk=32)
                        for j in range(NT):
                            pg = banks[4 + j % 3]
                            for k in range(KC):
                                fw.mm(pg, pg[:, 0:32], hT, hT[:, k * NTOK + j * 128:k * NTOK + (j + 1) * 128], wgt, wgt[:, k * 32:(k + 1) * 32],
                                      start=(k == 0), stop=(k == KC - 1))
                            fw.evac(Gt, Gt[:, j * 32:(j + 1) * 32], pg, pg[:, 0:32])
                            fw.op("dve", lambda e: e.tensor_tensor(out=GL[:, j * 32:(j + 1) * 32], in0=Gt[:, j * 32:(j + 1) * 32], in1=DTB[:], op=ALU.add), reads=[Gt, DTB], writes=[GL])
                        fw.op("act", lambda e: e.activation(out=GL[:], in_=GL[:], func=AF.Exp), reads=[GL], writes=[GL])
                        fw.op("act", lambda e: e.activation(out=GL[:], in_=GL[:], func=AF.Ln, bias=1.0, scale=1.0), reads=[GL], writes=[GL])
                        for j in range(NT):
                            fw.op("dve", lambda e: e.tensor_tensor(out=GL[:, j * 32:(j + 1) * 32], in0=GL[:, j * 32:(j + 1) * 32], in1=NEGA[:], op=ALU.mult), reads=[GL, NEGA], writes=[GL])
                        fw.op("act", lambda e: e.activation(out=BT[:], in_=Gt[:], func=AF.Sigmoid), reads=[Gt], writes=[BT])
                        pc_ = banks[6]
                        for j in range(NT):
                            for dr in range(2):
                                fw.mm(pc_, pc_[:, j * 16 + dr * 8:j * 16 + dr * 8 + 8], cst, U[dr], GL, GL[:, j * 32 + dr * 16:j * 32 + dr * 16 + 8])
                        fw.evac(CC, CC[:], pc_, pc_[:, 0:NT * 16])
                    for h in range(GDN_HEADS):
                        with fw.scope():
                            gdn_head(l, jj, b, h, hT, ZY, GL, BT, CC, convw, normg, mmq, chunks)
                    with fw.scope():
                        alloc_stg(256)
                        woutb = fw.sb("woutb", [128, KC * D], BF16)
                        YT = fw.sb("YT", [128, KC * NCH], BF16)
                        load_w(woutb, lambda c0, n: woutb[:].rearrange("p (k f) -> p k f", k=KC)[:, :, c0:c0 + n], w_out_a_d[jj], D, blk=256)
                        mixer_epilogue(l, b, ZY, woutb, YT)

        def gdn_head(l, jj, b, h, hT, ZY, GL, BT, CC, convw, normg, mmq, chunks):
            if GDN_STAGE < 1:
                return
            alloc_stg(128)
            wq = fw.sb("wq", [128, KC * 128], BF16)
            wk = fw.sb("wk", [128, KC * 128], BF16)
            wv = fw.sb("wv", [128, KC * 128], BF16)
            Pc = fw.sb("Pc", [128, 260])
            Pl = fw.sb("Pl", [128, 2052])
            raw = fw.sb("raw", [128, 512])
            sqt = fw.sb("sqt", [128, 512])
            rs = fw.sb("rs", [128, 512])
            QT = fw.sb("QT", [128, NTOK], BF16)
            KT = fw.sb("KT", [128, NTOK], BF16)
            VT = fw.sb("VT", [128, NTOK], BF16)
            Ktok = fw.sb("Ktok", [128, NT * 128], BF16)
            Vtok = fw.sb("Vtok", [128, NT * 128], BF16)
            Oacc = fw.sb("Oacc", [128, NT * 128])
            S = [fw.sb("S%d" % d, [128, 128]) for d in range(2)]
            Sbf = [fw.sb("Sbf%d" % d, [128, 128], BF16) for d in range(2)]
            sm = fw.sb("smg", [128, 4])
            junk = fw.sb("junkg", [128, 128])
            yt = fw.sb("ytg", [128, 128])
            fw.op("pool", lambda e: e.memset(Pc[:], 0.0), reads=[], writes=[Pc])
            fw.op("pool", lambda e: e.memset(Pl[:], 0.0), reads=[], writes=[Pl])
            fw.op("pool", lambda e: e.memset(Oacc[:], 0.0), reads=[], writes=[Oacc])
            for d in range(2):
                fw.op("pool", lambda e: e.memset(S[d][:], 0.0), reads=[], writes=[S[d]])
                fw.op("pool", lambda e: e.memset(Sbf[d][:], 0.0), reads=[], writes=[Sbf[d]])
            for wi, (wt_, c0) in enumerate(((wq, h * 128), (wk, 1024 + h * 128), (wv, 2048 + h * 128))):
                load_w(wt_, lambda cc, n, wt_=wt_: wt_[:].rearrange("p (k f) -> p k f", k=KC)[:, :, cc:cc + n], w_in_a_d[jj], 128, col0=c0, blk=128)
            for kind, (wt_, dst) in enumerate(((wq, QT), (wk, KT), (wv, VT))):
                chn = kind * 8 + h
                for (t0, n) in chunks:
                    pa = banks[4 + (t0 // 256) % 3]
                    for k in range(KC):
                        fw.mm(pa, pa[:, 0:n], wt_, wt_[:, k * 128:(k + 1) * 128], hT, hT[:, k * NTOK + t0:k * NTOK + t0 + n],
                              start=(k == 0), stop=(k == KC - 1))
                    if t0 < NCTX:
                        fw.evac(Pc, Pc[:, 2:2 + n], pa, pa[:, 0:n])
                    else:
                        fw.evac(Pl, Pl[:, 2 + t0 - NCTX:2 + t0 - NCTX + n], pa, pa[:, 0:n])
                for (t0, n) in chunks:
                    src, off = (Pc, t0) if t0 < NCTX else (Pl, t0 - NCTX)
                    cw = lambda k: convw[:, (jj * 5 + k) * 24 + chn:(jj * 5 + k) * 24 + chn + 1]
                    fw.op("dve", lambda e: e.tensor_scalar(out=raw[:, 0:n], in0=src[:, off:off + n], scalar1=cw(0), scalar2=None, op0=ALU.mult),
                          reads=[src, convw], writes=[raw])
                    for k in range(1, 5):
                        fw.op("dve", lambda e: e.scalar_tensor_tensor(out=raw[:, 0:n], in0=src[:, off + k:off + k + n], scalar=cw(k), in1=raw[:, 0:n],
                                                                      op0=ALU.mult, op1=ALU.add), reads=[src, convw, raw], writes=[raw])
                    if kind == 2:
                        fw.op("act", lambda e: e.activation(out=dst[:, t0:t0 + n], in_=raw[:, 0:n], func=AF.Silu), reads=[raw], writes=[dst])
                        continue
                    fw.op("act", lambda e: e.activation(out=raw[:, 0:n], in_=raw[:, 0:n], func=AF.Silu), reads=[raw], writes=[raw])
                    fw.op("act", lambda e: e.activation(out=sqt[:, 0:n], in_=raw[:, 0:n], func=AF.Square), reads=[raw], writes=[sqt])
                    pss = banks[4 + (t0 // 256) % 3]
                    fw.mm(pss, pss[:, 0:n], cst, onesf, sqt, sqt[:, 0:n])
                    fw.op("dve", lambda e: e.tensor_scalar_add(out=rs[:, 0:n], in0=pss[:, 0:n], scalar1=EPS), reads=[pss], writes=[rs])
                    fw.op("act", lambda e: e.activation(out=rs[:, 0:n], in_=rs[:, 0:n], func=AF.Sqrt), reads=[rs], writes=[rs])
                    fw.op("dve", lambda e: e.reciprocal(out=rs[:, 0:n], in_=rs[:, 0:n]), reads=[rs], writes=[rs])
                    sc = 128.0 ** -0.5 if kind == 0 else 1.0
                    fw.op("dve", lambda e: e.scalar_tensor_tensor(out=dst[:, t0:t0 + n], in0=raw[:, 0:n], scalar=sc, in1=rs[:, 0:n], op0=ALU.mult, op1=ALU.mult),
                          reads=[raw, rs], writes=[dst])
            for (srcT, dstK) in ((KT, Ktok), (VT, Vtok)):
                for j in range(NT):
                    q8 = j % 8
                    fw.tr(pbf, pbf[:, q8 * 128:(q8 + 1) * 128], srcT, srcT[:, j * 128:(j + 1) * 128], identb, identb[:])
                    fw.evac(dstK, dstK[:, j * 128:(j + 1) * 128], pbf, pbf[:, q8 * 128:(q8 + 1) * 128])
            if GDN_STAGE < 2:
                return
            tcache = {}

            def T(dr, par, name, dtype=F32, w=128):
                key = (dr, par, name)
                if key not in tcache:
                    tcache[key] = fw.sb("t%d%d%s_%d%d" % (dr, par, name, b, h), [128, w], dtype)
                return tcache[key]

            def chunk_gen(dr, j, par):
                gcol = j * 32 + dr * 16 + h
                bcol = gcol + 8
                ccol = j * 16 + dr * 8 + h
                jl = 127 if dr == 0 else 0
                t0, t1_ = j * 128, (j + 1) * 128
                ccap = CC[:, ccol:ccol + 1]
                btap = BT[:, bcol:bcol + 1]
                TT = lambda name, dtype=F32, w=128: T(dr, par, name, dtype, w)
                DV = lambda fn, r, w_: fw.op("dve", fn, reads=r, writes=w_)
                AC_ = lambda fn, r, w_: fw.op("act", fn, reads=r, writes=w_)
                PL = lambda fn, r, w_: fw.op("pool", fn, reads=r, writes=w_)
                sl = mmq(dr, GL, GL[:, gcol:gcol + 1].to_broadcast([128, 128]), cst, U[dr])
                Cb = TT("Cb")
                AC_(lambda e: e.copy(out=Cb[:], in_=sl.t), [sl], [Cb])
                yield
                ta, Dm, tb, DTm, E, qg = TT("ta"), TT("Dm"), TT("tb"), TT("DTm"), TT("E"), TT("qg", BF16)
                DV(lambda e: e.tensor_scalar(out=ta[:], in0=Cb[:], scalar1=ccap, scalar2=0.0, op0=ALU.subtract, op1=ALU.max), [Cb, CC], [ta])
                DV(lambda e: e.tensor_scalar(out=tb[:], in0=Cb[:], scalar1=ccap, scalar2=0.0, op0=ALU.subtract, op1=ALU.min), [Cb, CC], [tb])
                AC_(lambda e: e.activation(out=Dm[:], in_=ta[:], func=AF.Exp, scale=-1.0), [ta], [Dm])
                AC_(lambda e: e.activation(out=DTm[:], in_=tb[:], func=AF.Exp), [tb], [DTm])
                AC_(lambda e: e.activation(out=E[:], in_=Cb[:], func=AF.Exp), [Cb], [E])
                cols = TT("cols", F32, 8)
                AC_(lambda e: e.activation(out=cols[:, 0:1], in_=ccap, func=AF.Exp), [CC], [cols])
                AC_(lambda e: e.activation(out=cols[:, 2:3], in_=ccap, func=AF.Exp, scale=-1.0, bias=Cb[:, jl:jl + 1]), [CC, Cb], [cols])
                AC_(lambda e: e.activation(out=cols[:, 3:4], in_=Cb[:, jl:jl + 1], func=AF.Exp), [Cb], [cols])
                yield
                PL(lambda e: e.tensor_tensor(out=Dm[:], in0=Dm[:], in1=Ms[dr], op=ALU.mult), [Dm, cst], [Dm])
                yield
                PL(lambda e: e.tensor_tensor(out=DTm[:], in0=DTm[:], in1=Mi[1 - dr], op=ALU.mult), [DTm, cst], [DTm])
                yield
                DV(lambda e: e.tensor_tensor(out=qg[:], in0=QT[:, t0:t1_], in1=E[:], op=ALU.mult), [QT, E], [qg])
                yield
                DV(lambda e: e.tensor_tensor(out=cols[:, 1:2], in0=cols[:, 0:1], in1=btap, op=ALU.mult), [cols, BT], [cols])
                yield
                bke, bv, kdec = TT("bke"), TT("bv"), TT("kdec", BF16)
                DV(lambda e: e.tensor_scalar(out=bke[:], in0=Ktok[:, t0:t1_], scalar1=cols[:, 1:2], scalar2=None, op0=ALU.mult), [Ktok, cols], [bke])
                yield
                DV(lambda e: e.tensor_scalar(out=bv[:], in0=Vtok[:, t0:t1_], scalar1=btap, scalar2=None, op0=ALU.mult), [Vtok, BT], [bv])
                yield
                DV(lambda e: e.tensor_scalar(out=kdec[:], in0=Ktok[:, t0:t1_], scalar1=cols[:, 2:3], scalar2=None, op0=ALU.mult), [Ktok, cols], [kdec])
                yield
                sl = mmq(dr, KT, KT[:, t0:t1_], KT, KT[:, t0:t1_])
                yield
                A, AT = TT("A"), TT("AT")
                yield
                DV(lambda e: e.scalar_tensor_tensor(out=A[:], in0=sl.t, scalar=btap, in1=Dm[:], op0=ALU.mult, op1=ALU.mult), [sl, BT, Dm], [A])
                sl2 = mmq.slots[dr][mmq.si[dr] % 8]
                mmq.si[dr] += 1
                fw.tr(sl2, sl2.t, A, A[:], cst, identf)
                AC_(lambda e: e.copy(out=AT[:], in_=sl2.t), [sl2], [AT])
                yield
                N0, N0T, A1, A1T, A2 = TT("N0"), TT("N0T"), TT("A1"), TT("A1T"), TT("A2")
                PL(lambda e: e.tensor_tensor(out=N0[:], in0=A[:], in1=NB32, op=ALU.mult), [A, cst], [N0])
                PL(lambda e: e.tensor_tensor(out=N0T[:], in0=AT[:], in1=NB32, op=ALU.mult), [AT, cst], [N0T])
                PL(lambda e: e.tensor_tensor(out=A1[:], in0=A[:], in1=B6432, op=ALU.mult), [A, cst], [A1])
                PL(lambda e: e.tensor_tensor(out=A1T[:], in0=AT[:], in1=B6432, op=ALU.mult), [AT, cst], [A1T])
                PL(lambda e: e.tensor_tensor(out=A2[:], in0=A[:], in1=NB64c, op=ALU.mult), [A, cst], [A2])
                R, Tm = TT("R0"), TT("T0")
                DV(lambda e: e.tensor_tensor(out=R[:], in0=N0T[:], in1=identf, op=ALU.add), [N0T, cst], [R])
                DV(lambda e: e.tensor_tensor(out=Tm[:], in0=N0[:], in1=identf, op=ALU.add), [N0, cst], [Tm])
                P, PT = N0, N0T
                yield
                for k in range(4):
                    s1 = mmq(dr, PT, PT[:], P, P[:])
                    s2 = mmq(dr, P, P[:], PT, PT[:])
                    yield
                    P2, PT2 = TT("P%d" % (k % 2)), TT("PT%d" % (k % 2))
                    AC_(lambda e: e.copy(out=P2[:], in_=s1.t), [s1], [P2])
                    DV(lambda e: e.tensor_copy(out=PT2[:], in_=s2.t), [s2], [PT2])
                    yield
                    s3 = mmq(dr, P2, P2[:], R, R[:])
                    s4 = mmq(dr, PT2, PT2[:], Tm, Tm[:])
                    yield
                    Rn, Tn = TT("R%d" % ((k + 1) % 2)), TT("T%d" % ((k + 1) % 2))
                    DV(lambda e: e.tensor_tensor(out=Rn[:], in0=R[:], in1=s3.t, op=ALU.add), [R, s3], [Rn])
                    DV(lambda e: e.tensor_tensor(out=Tn[:], in0=Tm[:], in1=s4.t, op=ALU.add), [Tm, s4], [Tn])
                    R, Tm, P, PT = Rn, Tn, P2, PT2
                    yield
                s1 = mmq(dr, A1T, A1T[:], Tm, Tm[:])
                s2 = mmq(dr, A1, A1[:], R, R[:])
                yield
                Xp, X = TT("Xp"), TT("X")
                AC_(lambda e: e.copy(out=Xp[:], in_=s1.t), [s1], [Xp])
                DV(lambda e: e.tensor_copy(out=X[:], in_=s2.t), [s2], [X])
                yield
                s3 = mmq(dr, R, R[:], Xp, Xp[:])
                s4 = mmq(dr, Tm, Tm[:], X, X[:])
                yield
                T1, R1 = TT("T1"), TT("R1")
                DV(lambda e: e.tensor_tensor(out=T1[:], in0=Tm[:], in1=s3.t, op=ALU.subtract), [Tm, s3], [T1])
                DV(lambda e: e.tensor_tensor(out=R1[:], in0=R[:], in1=s4.t, op=ALU.subtract), [R, s4], [R1])
                yield
                s1 = mmq(dr, A2, A2[:], R1, R1[:])
                yield
                X2 = TT("X2")
                AC_(lambda e: e.copy(out=X2[:], in_=s1.t), [s1], [X2])
                yield
                s2 = mmq(dr, T1, T1[:], X2, X2[:])
                yield
                R2 = TT("R2")
                DV(lambda e: e.tensor_tensor(out=R2[:], in0=R1[:], in1=s2.t, op=ALU.subtract), [R1, s2], [R2])
                yield
                s1 = mmq(dr, R2, R2[:], bv, bv[:])
                s2 = mmq(dr, bke, bke[:], R2, R2[:])
                s3 = mmq(dr, KT, KT[:, t0:t1_], QT, QT[:, t0:t1_])
                yield
                u, wT, QKm = TT("u"), TT("wT", BF16), TT("QKm", BF16)
                AC_(lambda e: e.copy(out=u[:], in_=s1.t), [s1], [u])
                AC_(lambda e: e.copy(out=wT[:], in_=s2.t), [s2], [wT])
                DV(lambda e: e.tensor_tensor(out=QKm[:], in0=s3.t, in1=DTm[:], op=ALU.mult), [s3, DTm], [QKm])
                yield
                s1 = mmq(dr, wT, wT[:], Sbf[dr], Sbf[dr][:])
                yield
                vn = TT("vn", BF16)
                DV(lambda e: e.tensor_tensor(out=vn[:], in0=u[:], in1=s1.t, op=ALU.subtract), [u, s1], [vn])
                yield
                ob = banks[4 + dr]
                fw.mm(ob, ob[:, 0:128], qg, qg[:], Sbf[dr], Sbf[dr][:], start=True, stop=False)
                fw.mm(ob, ob[:, 0:128], QKm, QKm[:], vn, vn[:], start=False, stop=True)
                s2 = mmq(dr, kdec, kdec[:], vn, vn[:])
                yield
                DV(lambda e: e.tensor_tensor(out=Oacc[:, t0:t1_], in0=Oacc[:, t0:t1_], in1=ob[:, 0:128], op=ALU.add), [Oacc, ob], [Oacc])
                DV(lambda e: e.scalar_tensor_tensor(out=S[dr][:], in0=S[dr][:], scalar=cols[:, 3:4], in1=s2.t, op0=ALU.mult, op1=ALU.add), [S[dr], cols, s2], [S[dr]])
                AC_(lambda e: e.copy(out=Sbf[dr][:], in_=S[dr][:]), [S[dr]], [Sbf[dr]])
                yield

            orders = [list(range(NT)), [1, 0] + list(range(NT - 1, 1, -1))]
            fw.pe_drain = PE_DRAIN
            for step in range(GDN_NSTEPS if GDN_STAGE >= 3 else 0):
                gens = [chunk_gen(dr, orders[dr][step], 0) for dr in range(2)]
                alive = list(gens)
                ny = 0
                while alive and ny < GDN_YIELDS:
                    ny += 1
                    nxt = []
                    for g in alive:
                        try:
                            next(g)
                            nxt.append(g)
                        except StopIteration:
                            pass
                        if BAR:
                            fw.barrier()
                    alive = nxt
            fw.pe_drain = False
            if b == 0 and h == 0:
                dump("gQT", QT, QT[:], NTOK)
                dump("gKT", KT, KT[:], NTOK)
                dump("gVT", VT, VT[:], NTOK)
                dump("gGL", GL, GL[:], NT * 32)
                dump("gBT", BT, BT[:], NT * 32)
                dump("gCC", CC, CC[:], NT * 16)
                dump("gO", Oacc, Oacc[:], NT * 128)
                dump("gS0", S[0], S[0][:], 128)
                dump("gS1", S[1], S[1][:], 128)
            for j in range(NT if GDN_STAGE >= 4 else 0):
                ov = Oacc[:, j * 128:(j + 1) * 128]
                fw.op("act", lambda e: e.activation(out=junk[:], in_=ov, func=AF.Square, accum_out=sm[:, 0:1]), reads=[Oacc], writes=[junk, sm])
                fw.op("dve", lambda e: e.tensor_scalar(out=sm[:, 0:1], in0=sm[:, 0:1], scalar1=1.0 / 128, scalar2=EPS, op0=ALU.mult, op1=ALU.add), reads=[sm], writes=[sm])
                fw.op("act", lambda e: e.activation(out=sm[:, 0:1], in_=sm[:, 0:1], func=AF.Sqrt), reads=[sm], writes=[sm])
                fw.op("dve", lambda e: e.reciprocal(out=sm[:, 0:1], in_=sm[:, 0:1]), reads=[sm], writes=[sm])
                fw.op("dve", lambda e: e.scalar_tensor_tensor(out=yt[:], in0=ov, scalar=sm[:, 0:1], in1=normg[:], op0=ALU.mult, op1=ALU.mult), reads=[Oacc, sm, normg], writes=[yt])
                zv = ZY[:, j * D + h * 128:j * D + (h + 1) * 128]
                fw.op("dve", lambda e: e.tensor_tensor(out=zv, in0=yt[:], in1=zv, op=ALU.mult), reads=[yt, ZY], writes=[ZY])

        mixers = {0: mixer_gdn, 1: mixer_attn, 2: mixer_mlstm}

        with fw.scope():
            prologue()
        for l in layers:
            kind = l % 3
            if kind in mixers:
                with fw.scope():
                    mixers[kind](l, l // 3)
            if do_ffn:
                with fw.scope():
                    w1b = fw.sb("w1b", [128, KC * DFF], BF16)
                    w2b = fw.sb("w2b", [128, 32 * D], BF16)
                    with fw.scope():
                        alloc_stg()
                        load_w(w1b, lambda c0, n: w1b[:].rearrange("p (k f) -> p k f", k=KC)[:, :, c0:c0 + n], w1_d[l], DFF)
                        for kg in range(4):
                            load_w(w2b, lambda c0, n, kg=kg: w2b[:].rearrange("p (k f) -> p k f", k=32)[:, kg * 8:(kg + 1) * 8, c0:c0 + n],
                                   w2_d[l][kg * 1024:(kg + 1) * 1024, :], D)
                    hT = fw.sb("hT_ffn", [128, KC * NCH], BF16)
                    aT = fw.sb("aT_ffn", [128, 32 * NCH], BF16)
                    rtmp = [fw.sb("rtmp%d" % i, [128, NCH]) for i in range(2)]
                    ffn_pass(l, w1b, w2b, hT, aT, rtmp)
        with fw.scope():
            xin = [fw.sb("oin%d" % i, [128, KC * 128]) for i in range(2)]
            xo = [fw.sb("oo%d" % i, [128, D]) for i in range(2)]
            OUT = [[fw.view(None) for j in range(NT)] for b in range(NSEQ)]
            for b in range(NSEQ):
                for j in range(2, NT):
                    t = xin[j % 2]
                    o = xo[j % 2]
                    fw.dma(t[:].rearrange("p (c t) -> p c t", c=KC), xt_ap(b, j * 128, (j + 1) * 128), reads=[XT[b][j]], writes=[t])
                    for half in range(2):
                        bk = banks[(j * 2 + half) % 4]
                        for c in range(4):
                            cc = half * 4 + c
                            fw.tr(bk, bk[:, c * 128:(c + 1) * 128], t, t[:, cc * 128:(cc + 1) * 128], cst, identf)
                        fw.evac(o, o[:, half * 512:(half + 1) * 512], bk, bk[:])
                    fw.dma(out_d[b, (j - 2) * 128:(j - 1) * 128, :], o[:], reads=[o], writes=[OUT[b][j]])
            fw.finish([OUT[b][j] for b in range(NSEQ) for j in range(2, NT)] + DUMPS, "sp")
        fw.barrier()
        print("n_inst", fw.n_inst)
    return nc


def _consts():
    i = np.arange(128)
    ident = np.eye(128, dtype=np.float32)
    ones = np.ones((128, 128), np.float32)
    U_f = (i[:, None] <= i[None, :]).astype(np.float32)
    U_b = U_f.T.copy()
    Ms_f = (i[None, :] < i[:, None]).astype(np.float32)
    Ms_b = Ms_f.T.copy()
    b32 = (i[:, None] // 32 == i[None, :] // 32).astype(np.float32)
    b64 = (i[:, None] // 64 == i[None, :] // 64).astype(np.float32)
    z = np.zeros((128, 128), np.float32)
    return np.ascontiguousarray(np.concatenate([ident, ones, U_f, U_b, Ms_f, Ms_b, -b32, b64 - b32, 1.0 - b64, z], axis=1))


def _rope_tables():
    t = np.arange(NLAT)
    r = (t // 64).astype(np.float64)
    c = (t % 64).astype(np.float64)
    half = 32
    inv = 10000.0 ** (-np.arange(0, half, 2, dtype=np.float64) / half)
    cos = np.zeros((128, NLAT), np.float32)
    sin = np.zeros((128, NLAT), np.float32)
    for p in range(128):
        d = p % 64
        axis, hf, f = d // 32, (d % 32) // 16, d % 16
        ang = (r if axis == 0 else c) * inv[f]
        cos[p] = np.cos(ang)
        sin[p] = np.sin(ang) * (-1.0 if hf == 0 else 1.0)
    return np.ascontiguousarray(np.concatenate([cos, sin], axis=1))


def _layout_inputs(inputs, core):
    f = lambda a: np.ascontiguousarray(np.asarray(a, dtype=np.float32))
    b0 = core * NSEQ
    c = np.asarray(inputs["c"], np.float32)
    cvec = np.stack([c[b0], c[b0 + 1], np.asarray(inputs["c_ctx"], np.float32)], 0)
    cT = cvec.reshape(3, KC, 128).transpose(2, 1, 0).reshape(128, KC * 3)
    b_modT = np.asarray(inputs["b_mod"], np.float32).reshape(DEPTH, 48, 128).transpose(2, 0, 1).reshape(128, DEPTH * 48)
    ln_gT = np.asarray(inputs["ln_g"], np.float32).reshape(DEPTH, 2, KC, 128).transpose(3, 0, 1, 2).reshape(128, -1)
    ln_bT = np.asarray(inputs["ln_b"], np.float32).reshape(DEPTH, 2, KC, 128).transpose(3, 0, 1, 2).reshape(128, -1)
    conv_aT = np.asarray(inputs["conv_a"], np.float32).reshape(2, 5, 24, 128).transpose(3, 0, 1, 2).reshape(128, -1)
    wb = np.asarray(inputs["w_in_b"], np.float32)
    d = np.arange(64)
    partner = (d // 32) * 32 + (1 - (d % 32) // 16) * 16 + d % 16
    cols = (np.arange(2048) // 64) * 64 + partner[np.arange(2048) % 64]
    w_in_bp = wb[:, :, cols]
    return {
        "x": f(inputs["x"][b0:b0 + NSEQ]), "ctx": f(inputs["ctx"][b0:b0 + NSEQ]), "cT": f(cT),
        "w_mod": f(inputs["w_mod"]), "b_modT": f(b_modT), "ln_gT": f(ln_gT), "ln_bT": f(ln_bT),
        "w_in_a": f(inputs["w_in_a"]), "conv_aT": f(conv_aT),
        "a_log_a": f(np.asarray(inputs["a_log_a"]).reshape(2, 16)), "dt_bias_a": f(np.asarray(inputs["dt_bias_a"]).reshape(2, 16)),
        "norm_a": f(inputs["norm_a"]), "w_out_a": f(inputs["w_out_a"]),
        "w_in_b": f(wb), "w_in_bp": f(w_in_bp), "lam_b": f(inputs["lam_b"]), "subln_b": f(inputs["subln_b"]),
        "w_out_b": f(inputs["w_out_b"]), "w_in_c": f(inputs["w_in_c"]),
        "gate_bias_c": f(np.asarray(inputs["gate_bias_c"]).reshape(1, 16)), "norm_c": f(inputs["norm_c"]),
        "w_out_c": f(inputs["w_out_c"]), "w1": f(inputs["w1"]), "w2": f(inputs["w2"]),
        "cst": _consts(), "rope": _rope_tables(),
    }


def kernel(**inputs):
    nc = build()
    shared = None
    in_maps = []
    for core in range(NCORES):
        m = _layout_inputs(inputs, core)
        if shared is None:
            shared = m
        else:
            for k in m:
                if k not in ("x", "ctx", "cT"):
                    m[k] = shared[k]
        in_maps.append(m)
    res = run_bass_kernel_spmd(nc, in_maps, core_ids=list(range(NCORES)))
    out = np.concatenate([np.asarray(r["out"]) for r in res.results], axis=0)
    return out.astype(np.float32)
```

```python
import math
import numpy as np
import concourse.bass as bass
import concourse.mybir as mybir
from concourse.bass_utils import run_bass_kernel_spmd
from contextlib import ExitStack

F32 = mybir.dt.float32
BF16 = mybir.dt.bfloat16
ALU = mybir.AluOpType
AF = mybir.ActivationFunctionType

NCORES = 8
NSEQ = 2
NCTX = 256
NLAT = 2048
NTOK = NCTX + NLAT
NT = NTOK // 128
D = 1024
KC = 8
DFF = 4096
DEPTH = 4
ALPHA = (2.0 * DEPTH) ** 0.25
LN_EPS_S = 1e-5 / (ALPHA * ALPHA)
EPS = 1e-6
SELF_WINDOW = 6
NO_SELF = ("pe", "pool", "sp")


class Buf:
    __slots__ = ("t", "writers", "readers", "name", "excl")

    def __init__(self, t, name="", excl=False):
        self.t = t
        self.writers = {}
        self.readers = {}
        self.name = name
        self.excl = excl

    def __getitem__(self, idx):
        return self.t[idx]


class SlotView:
    def __init__(self, bank, ap):
        self.bank = bank
        self.t = ap
        self.name = "slot"
        self.excl = True

    writers = property(lambda self: self.bank.writers, lambda self, v: setattr(self.bank, "writers", v))
    readers = property(lambda self: self.bank.readers, lambda self, v: setattr(self.bank, "readers", v))


class FW:
    def __init__(self, nc, stack):
        self.nc = nc
        self.stack = stack
        self.engs = {"pe": nc.tensor, "act": nc.scalar, "dve": nc.vector, "pool": nc.gpsimd, "sp": nc.sync}
        self.sem = {}
        self.cnt = {}
        for e in ("pe", "act", "dve", "pool"):
            self.sem[e] = stack.enter_context(nc.semaphore("s_" + e))
            self.cnt[e] = 0
        self.dma_slots = {"sp": [], "pool": []}
        self.dma_next = {"sp": 0, "pool": 0}
        for q, n in (("sp", 32), ("pool", 16)):
            for i in range(n):
                s = stack.enter_context(nc.semaphore("d_%s%d" % (q, i)))
                key = "dma_%s%d" % (q, i)
                self.sem[key] = s
                self.cnt[key] = 0
                self.dma_slots[q].append(key)
        self.seen = {e: {} for e in ("pe", "act", "dve", "pool", "sp")}
        self.n_inst = 0
        self.rr = 0

    def sb(self, name, shape, dtype=F32):
        self.uid = getattr(self, "uid", 0) + 1
        t = self.stack.enter_context(self.nc.sbuf_tensor("sb_%s_%d" % (name, self.uid), list(shape), dtype))
        return Buf(t, name)

    def ps(self, name, shape, dtype=F32):
        t = self.stack.enter_context(self.nc.psum_tensor("ps_" + name, list(shape), dtype))
        return Buf(t, name, excl=True)

    def view(self, ap, name=""):
        return Buf(ap, name)

    def barrier(self):
        deps = {k: v for k, v in self.cnt.items() if v > 0}
        for e in ("pe", "act", "dve", "pool", "sp"):
            self._wait(e, dict(deps))

    def scope(self):
        fw = self

        class _S:
            def __enter__(s2):
                s2.old = fw.stack
                s2.st = ExitStack()
                s2.st.__enter__()
                fw.stack = s2.st
                return s2

            def __exit__(s2, *a):
                fw.barrier()
                fw.stack = s2.old
                return s2.st.__exit__(*a)
        return _S()

    def _wait(self, ename, deps):
        eng = self.engs[ename]
        seen = self.seen[ename]
        for k, v in deps.items():
            if k == ename:
                if ename in NO_SELF or self.cnt[ename] - v >= SELF_WINDOW:
                    continue
            if seen.get(k, 0) >= v:
                continue
            eng.wait_ge(self.sem[k], v)
            seen[k] = v

    @staticmethod
    def _deps(reads, writes):
        deps = {}
        for b in reads:
            for k, v in b.writers.items():
                if deps.get(k, 0) < v:
                    deps[k] = v
            if b.excl:
                for k, v in b.readers.items():
                    if deps.get(k, 0) < v:
                        deps[k] = v
        for b in writes:
            for k, v in b.writers.items():
                if deps.get(k, 0) < v:
                    deps[k] = v
            for k, v in b.readers.items():
                if deps.get(k, 0) < v:
                    deps[k] = v
        return deps

    def op(self, ename, fn, reads=(), writes=()):
        deps = self._deps(reads, writes)
        self._wait(ename, deps)
        if ename == "pe" and getattr(self, "pe_drain", False) and self.cnt["pe"] > 0:
            self.engs["pe"].wait_ge(self.sem["pe"], self.cnt["pe"])
        ins = fn(self.engs[ename])
        self.cnt[ename] += 1
        c = self.cnt[ename]
        ins.then_inc(self.sem[ename], 1)
        self.n_inst += 1
        for b in writes:
            b.writers = {ename: c}
            b.readers = {}
        for b in reads:
            if b.readers.get(ename, 0) < c:
                b.readers[ename] = c
        return ins

    def dma(self, out_ap, in_ap, reads=(), writes=(), q="sp", **kw):
        deps = self._deps(reads, writes)
        slots = self.dma_slots[q]
        key = slots[self.dma_next[q] % len(slots)]
        self.dma_next[q] += 1
        if self.cnt[key] > 0:
            deps[key] = max(deps.get(key, 0), self.cnt[key])
        self._wait(q, deps)
        ins = self.engs[q].dma_start(out=out_ap, in_=in_ap, **kw)
        self.cnt[key] += 16
        c = self.cnt[key]
        ins.then_inc(self.sem[key], 16)
        self.n_inst += 1
        for b in writes:
            b.writers = {key: c}
            b.readers = {}
        for b in reads:
            if b.readers.get(key, 0) < c:
                b.readers[key] = c

    def finish(self, bufs, ename="sp"):
        deps = {}
        for b in bufs:
            for k, v in b.writers.items():
                if deps.get(k, 0) < v:
                    deps[k] = v
        self._wait(ename, deps)

    def _touch(self, out, out_ap):
        if False:
            j = self.touch_junk
            self.op("act", lambda e: e.copy(out=j[:, 0:1], in_=out_ap[:, 0:1]), reads=[out], writes=[j])
            self.pe_gate = self.cnt["act"]

    def mm(self, out, out_ap, lhsT, lhsT_ap, rhs, rhs_ap, start=True, stop=True):
        r = self.op("pe", lambda e: e.matmul(out_ap, lhsT=lhsT_ap, rhs=rhs_ap, start=start, stop=stop),
                    reads=[lhsT, rhs], writes=[out])
        self._touch(out, out_ap)
        return r

    def tr(self, out, out_ap, in_, in_ap, ident, ident_ap):
        r = self.op("pe", lambda e: e.transpose(out_ap, in_ap, ident_ap), reads=[in_, ident], writes=[out])
        self._touch(out, out_ap)
        return r

    def evac(self, out, out_ap, in_, in_ap):
        self.rr += 1
        if self.rr & 1:
            return self.op("act", lambda e: e.copy(out=out_ap, in_=in_ap), reads=[in_], writes=[out])
        return self.op("dve", lambda e: e.tensor_copy(out=out_ap, in_=in_ap), reads=[in_], writes=[out])


DEBUG = False
GDN_HEADS = 8
GDN_STAGE = 9
GDN_YIELDS = 999
GDN_NSTEPS = 18
KKDIRS = (0, 1)
BAR = False
PE_DRAIN = False


def build(layers=(0, 1, 2, 3), do_ffn=True):
    nc = bass.Bass("TRN2", target_bir_lowering=False)

    def din(name, shape):
        return nc.dram_tensor(name, list(shape), F32, kind="ExternalInput").ap()

    x_d = din("x", [NSEQ, NLAT, D])
    ctx_d = din("ctx", [NSEQ, NCTX, D])
    cT_d = din("cT", [128, KC * 3])
    w_mod_d = din("w_mod", [DEPTH, D, 6 * D])
    b_modT_d = din("b_modT", [128, DEPTH * 48])
    ln_gT_d = din("ln_gT", [128, DEPTH * 2 * KC])
    ln_bT_d = din("ln_bT", [128, DEPTH * 2 * KC])
    w_in_a_d = din("w_in_a", [2, D, 4128])
    conv_aT_d = din("conv_aT", [128, 2 * 5 * 24])
    alog_d = din("a_log_a", [2, 16])
    dtb_d = din("dt_bias_a", [2, 16])
    norm_a_d = din("norm_a", [2, 128])
    w_out_a_d = din("w_out_a", [2, D, D])
    w_in_b_d = din("w_in_b", [1, D, 3072])
    w_in_bp_d = din("w_in_bp", [1, D, 2048])
    lam_b_d = din("lam_b", [1, 4, 64])
    subln_b_d = din("subln_b", [1, 128])
    w_out_b_d = din("w_out_b", [1, D, D])
    w_in_c_d = din("w_in_c", [1, D, 3088])
    gbias_c_d = din("gate_bias_c", [1, 16])
    norm_c_d = din("norm_c", [1, D])
    w_out_c_d = din("w_out_c", [1, D, D])
    w1_d = din("w1", [DEPTH, D, DFF])
    w2_d = din("w2", [DEPTH, DFF, D])
    cst_d = din("cst", [128, 10 * 128])
    rope_d = din("rope", [128, 2 * NLAT])
    out_d = nc.dram_tensor("out", [NSEQ, NLAT, D], F32, kind="ExternalOutput").ap()
    xt_d = nc.dram_tensor("xt_scratch", [NSEQ, KC, 128, NTOK], F32, kind="Internal").ap()

    with ExitStack() as st:
        fw = FW(nc, st)
        cst = fw.sb("cst", [128, 10 * 128])
        CST = fw.view(cst_d)
        fw.dma(cst[:], cst_d, reads=[CST], writes=[cst])
        identf = cst[:, 0:128]
        onesf = cst[:, 128:256]
        U = [cst[:, 256:384], cst[:, 384:512]]
        Ms = [cst[:, 512:640], cst[:, 640:768]]
        Mi = [U[1], U[0]]
        NB32 = cst[:, 768:896]
        B6432 = cst[:, 896:1024]
        NB64c = cst[:, 1024:1152]
        identb = fw.sb("identb", [128, 128], BF16)
        fw.op("dve", lambda e: e.tensor_copy(out=identb[:], in_=identf), reads=[cst], writes=[identb])
        fw.touch_junk = fw.sb("touch_junk", [128, 2])
        onesb = fw.sb("onesb", [128, 128], BF16)
        fw.op("dve", lambda e: e.tensor_copy(out=onesb[:], in_=onesf), reads=[cst], writes=[onesb])

        DUMPS = []

        dbg_stage = [None]

        def dump(name, buf, ap, n):
            if not DEBUG:
                return
            dt = nc.dram_tensor("dbg_" + name, [128, n], F32, kind="ExternalOutput").ap()
            tmp = dbg_stage[0]
            for c0 in range(0, n, 512):
                m = min(512, n - c0)
                fw.op("dve", lambda e: e.tensor_copy(out=tmp[:, 0:m], in_=ap[:, c0:c0 + m]), reads=[buf], writes=[tmp])
                v = fw.view(None)
                fw.dma(dt[:, c0:c0 + m], tmp[:, 0:m], reads=[tmp], writes=[v])
                DUMPS.append(v)

        if DEBUG:
            dbg_stage[0] = fw.sb("dbgs", [128, 512])
        banks = [fw.ps("bank%d" % i, [128, 512]) for i in range(7)]
        pbf = fw.ps("pbf", [128, 1024], BF16)

        XT = [[fw.view(None, "xt%d_%d" % (b, j)) for j in range(NT)] for b in range(NSEQ)]

        def xt_ap(b, t0, t1):
            return xt_d[b, :, :, t0:t1].rearrange("c p t -> p c t")

        def xt_bufs(b, t0, t1):
            return XT[b][t0 // 128:(t1 + 127) // 128]

        b_modT = fw.sb("b_modT", [128, DEPTH * 48])
        ln_gT = fw.sb("ln_gT", [128, DEPTH * 2 * KC])
        ln_bT = fw.sb("ln_bT", [128, DEPTH * 2 * KC])
        cT = fw.sb("cT", [128, KC * 3])
        for dst, src in ((b_modT, b_modT_d), (ln_gT, ln_gT_d), (ln_bT, ln_bT_d), (cT, cT_d)):
            fw.dma(dst[:], src, reads=[fw.view(src)], writes=[dst])
        fw.op("act", lambda e: e.activation(out=cT[:], in_=cT[:], func=AF.Silu), reads=[cT], writes=[cT])
        MOD = fw.sb("MOD", [128, DEPTH * 48 * 3])

        def mod_col(l, which, c, col):
            o = ((l * 48) + which * 8 + c) * 3 + col
            return MOD[:, o:o + 1]

        stg = []

        def alloc_stg(blk=512):
            stg[:] = [fw.sb("stg%d_%d" % (i, fw.n_inst), [128, KC * blk]) for i in range(2)]
        stg_i = [0]

        def load_w(dst, dst_ap_fn, src2d, ncols, kc=KC, col0=0, blk=512):
            for c0 in range(0, ncols, blk):
                n = min(blk, ncols - c0)
                s = stg[stg_i[0] % 2]
                stg_i[0] += 1
                sv = s[:, 0:kc * n].rearrange("p (k n) -> p k n", k=kc)
                src = src2d[:, col0 + c0:col0 + c0 + n].rearrange("(k p) n -> p k n", p=128)
                fw.dma(sv, src, reads=[], writes=[s])
                fw.op("pool", lambda e: e.tensor_copy(out=dst_ap_fn(c0, n), in_=sv), reads=[s], writes=[dst])

        def prologue():
            xin = [fw.sb("xin%d" % i, [128, D]) for i in range(2)]
            xo = [fw.sb("xo%d" % i, [128, D]) for i in range(2)]
            for b in range(NSEQ):
                for j in range(NT):
                    t = xin[j % 2]
                    o = xo[j % 2]
                    src = ctx_d[b, j * 128:(j + 1) * 128, :] if j < 2 else x_d[b, (j - 2) * 128:(j - 1) * 128, :]
                    fw.dma(t[:], src, reads=[], writes=[t])
                    for half in range(2):
                        bk = banks[(j * 2 + half) % 4]
                        for c in range(4):
                            cc = half * 4 + c
                            fw.tr(bk, bk[:, c * 128:(c + 1) * 128], t, t[:, cc * 128:(cc + 1) * 128], cst, identf)
                        fw.evac(o, o[:, half * 512:(half + 1) * 512], bk, bk[:])
                    fw.dma(xt_ap(b, j * 128, (j + 1) * 128), o[:].rearrange("p (c t) -> p c t", c=KC),
                           reads=[o], writes=[XT[b][j]])

            wmb = [fw.sb("wmb%d" % i, [128, KC * 512]) for i in range(2)]
            for l in layers:
                for fb in range(12):
                    wb = wmb[fb % 2]
                    src = w_mod_d[l][:, fb * 512:(fb + 1) * 512].rearrange("(k p) n -> p k n", p=128)
                    fw.dma(wb[:].rearrange("p (k n) -> p k n", k=KC), src, reads=[], writes=[wb])
                    bk = banks[fb % 2]
                    for f4 in range(4):
                        for k in range(KC):
                            fw.mm(bk, bk[:, f4 * 3:f4 * 3 + 3], wb, wb[:, k * 512 + f4 * 128:k * 512 + f4 * 128 + 128],
                                  cT, cT[:, k * 3:k * 3 + 3], start=(k == 0), stop=(k == KC - 1))
                    for f4 in range(4):
                        ch = fb * 4 + f4
                        o = (l * 48 + ch) * 3
                        fw.op("dve", lambda e: e.tensor_scalar(out=MOD[:, o:o + 3], in0=bk[:, f4 * 3:f4 * 3 + 3],
                                                               scalar1=b_modT[:, l * 48 + ch:l * 48 + ch + 1], scalar2=None,
                                                               op0=ALU.add), reads=[bk, b_modT], writes=[MOD])
                for which in (1, 4):
                    o = (l * 48 + which * 8) * 3
                    fw.op("dve", lambda e: e.tensor_scalar_add(out=MOD[:, o:o + 24], in0=MOD[:, o:o + 24], scalar1=1.0),
                          reads=[MOD], writes=[MOD])
                for which in (2, 5):
                    o = (l * 48 + which * 8) * 3
                    fw.op("dve", lambda e: e.tensor_scalar_mul(out=MOD[:, o:o + 24], in0=MOD[:, o:o + 24], scalar1=1.0 / ALPHA),
                          reads=[MOD], writes=[MOD])


        NCH = 256
        zt = fw.sb("zt", [128, KC * NCH])

        st_mean = fw.sb("st_mean", [128, NCH])
        st_var = fw.sb("st_var", [128, NCH])
        st_tmp = fw.sb("st_tmp", [128, NCH])
        xres = fw.sb("xres", [128, KC * NCH])
        sq = xres
        xnew = xres

        def resid_ln(l, sub, b, t0, n, col, y_fn):
            gw = 2 if sub == 0 else 5
            for o in range(KC):
                pb, pap = y_fn(o)
                fw.op("dve", lambda e: e.scalar_tensor_tensor(out=zt[:, o * NCH:o * NCH + n], in0=pap,
                                                              scalar=mod_col(l, gw, o, col),
                                                              in1=xres[:, o * NCH:o * NCH + n],
                                                              op0=ALU.mult, op1=ALU.add),
                      reads=[pb, MOD, xres], writes=[zt])
            fw.op("act", lambda e: e.activation(out=sq[:], in_=zt[:], func=AF.Square), reads=[zt], writes=[sq])
            bs, bq = banks[5], banks[6]
            for o in range(KC):
                fw.mm(bs, bs[:, 0:n], cst, onesf, zt, zt[:, o * NCH:o * NCH + n], start=(o == 0), stop=(o == KC - 1))
            for o in range(KC):
                fw.mm(bq, bq[:, 0:n], cst, onesf, sq, sq[:, o * NCH:o * NCH + n], start=(o == 0), stop=(o == KC - 1))
            fw.op("act", lambda e: e.mul(out=st_mean[:, 0:n], in_=bs[:, 0:n], mul=1.0 / D), reads=[bs], writes=[st_mean])
            fw.op("dve", lambda e: e.tensor_tensor(out=st_tmp[:, 0:n], in0=st_mean[:, 0:n], in1=st_mean[:, 0:n], op=ALU.mult),
                  reads=[st_mean], writes=[st_tmp])
            fw.op("dve", lambda e: e.scalar_tensor_tensor(out=st_var[:, 0:n], in0=bq[:, 0:n], scalar=1.0 / D,
                                                          in1=st_tmp[:, 0:n], op0=ALU.mult, op1=ALU.subtract),
                  reads=[bq, st_tmp], writes=[st_var])
            fw.op("dve", lambda e: e.tensor_scalar(out=st_var[:, 0:n], in0=st_var[:, 0:n], scalar1=0.0, scalar2=LN_EPS_S,
                                                   op0=ALU.max, op1=ALU.add), reads=[st_var], writes=[st_var])
            fw.op("act", lambda e: e.activation(out=st_var[:, 0:n], in_=st_var[:, 0:n], func=AF.Sqrt), reads=[st_var], writes=[st_var])
            fw.op("dve", lambda e: e.reciprocal(out=st_var[:, 0:n], in_=st_var[:, 0:n]), reads=[st_var], writes=[st_var])
            gi = (l * 2 + sub) * KC
            for o in range(KC):
                eng = "dve" if o % 2 == 0 else "pool"
                fw.op(eng, lambda e: e.tensor_tensor(out=zt[:, o * NCH:o * NCH + n], in0=zt[:, o * NCH:o * NCH + n],
                                                     in1=st_mean[:, 0:n], op=ALU.subtract), reads=[zt, st_mean], writes=[zt])
                fw.op(eng, lambda e: e.tensor_tensor(out=zt[:, o * NCH:o * NCH + n], in0=zt[:, o * NCH:o * NCH + n],
                                                     in1=st_var[:, 0:n], op=ALU.mult), reads=[zt, st_var], writes=[zt])
                fw.op("dve", lambda e: e.tensor_scalar(out=xnew[:, o * NCH:o * NCH + n], in0=zt[:, o * NCH:o * NCH + n],
                                                       scalar1=ln_gT[:, gi + o:gi + o + 1], scalar2=ln_bT[:, gi + o:gi + o + 1],
                                                       op0=ALU.mult, op1=ALU.add), reads=[zt, ln_gT, ln_bT], writes=[xnew])
            fw.dma(xt_ap(b, t0, t0 + n), xnew[:].rearrange("p (c t) -> p c t", c=KC)[:, :, 0:n],
                   reads=[xnew], writes=xt_bufs(b, t0, t0 + n))

        def load_xres(b, t0, n):
            fw.dma(xres[:].rearrange("p (c t) -> p c t", c=KC)[:, :, 0:n], xt_ap(b, t0, t0 + n),
                   reads=xt_bufs(b, t0, t0 + n), writes=[xres])

        def ffn_pass(l, w1b, w2b, hT, aT, rtmp):
            for b in range(NSEQ):
                for ci in range(NTOK // NCH):
                    t0 = ci * NCH
                    n = NCH
                    col = 2 if ci == 0 else b
                    load_xres(b, t0, n)
                    for c in range(KC):
                        fw.op("dve" if c % 2 else "pool",
                              lambda e: e.tensor_scalar(out=hT[:, c * NCH:c * NCH + n], in0=xres[:, c * NCH:c * NCH + n],
                                                        scalar1=mod_col(l, 4, c, col), scalar2=mod_col(l, 3, c, col),
                                                        op0=ALU.mult, op1=ALU.add), reads=[xres, MOD], writes=[hT])
                    for f in range(32):
                        bk = banks[f % 4]
                        for k in range(KC):
                            fw.mm(bk, bk[:, 0:n], w1b, w1b[:, k * DFF + f * 128:k * DFF + f * 128 + 128],
                                  hT, hT[:, k * NCH:k * NCH + n], start=(k == 0), stop=(k == KC - 1))
                        r = rtmp[f % 2]
                        fw.op("act", lambda e: e.activation(out=r[:, 0:n], in_=bk[:, 0:n], func=AF.Relu), reads=[bk], writes=[r])
                        fw.op("dve" if f % 2 else "pool",
                              lambda e: e.tensor_tensor(out=aT[:, f * NCH:f * NCH + n], in0=r[:, 0:n], in1=r[:, 0:n], op=ALU.mult),
                              reads=[r], writes=[aT])

                    def y_fn(o):
                        bk = banks[4]
                        for f in range(32):
                            fw.mm(bk, bk[:, 0:n], w2b, w2b[:, f * D + o * 128:f * D + o * 128 + 128],
                                  aT, aT[:, f * NCH:f * NCH + n], start=(f == 0), stop=(f == 31))
                        return bk, bk[:, 0:n]
                    resid_ln(l, 1, b, t0, n, col, y_fn)

        def mixer_epilogue(l, b, Y, woutb, YT):
            for ci in range(NTOK // NCH):
                t0 = ci * NCH
                n = NCH
                col = 2 if ci == 0 else b
                load_xres(b, t0, n)
                for jj in range(2):
                    j = ci * 2 + jj
                    for c in range(KC):
                        fw.tr(pbf, pbf[:, c * 128:c * 128 + 128], Y, Y[:, j * D + c * 128:j * D + c * 128 + 128], identb, identb[:])
                    for c in range(KC):
                        fw.evac(YT, YT[:, c * NCH + jj * 128:c * NCH + jj * 128 + 128], pbf, pbf[:, c * 128:c * 128 + 128])

                def y_fn(o):
                    bk = banks[4]
                    for k in range(KC):
                        fw.mm(bk, bk[:, 0:n], woutb, woutb[:, k * D + o * 128:k * D + o * 128 + 128],
                              YT, YT[:, k * NCH:k * NCH + n], start=(k == 0), stop=(k == KC - 1))
                    return bk, bk[:, 0:n]
                resid_ln(l, 0, b, t0, n, col, y_fn)


        def mixer_attn(l, jj):
            lam_init = 0.8 - 0.6 * math.exp(-0.3 * l)
            woutb = fw.sb("woutb", [128, KC * D], BF16)
            YT = fw.sb("YT", [128, KC * NCH], BF16)
            hT = fw.sb("hT_seq", [128, KC * NTOK], BF16)
            Y = fw.sb("Y_seq", [128, NT * D], BF16)
            wh = [fw.sb("wh%d" % i, [128, KC * 128], BF16) for i in range(5)]
            KT = fw.sb("KT", [128, NTOK], BF16)
            QT = fw.sb("QT", [128, NTOK], BF16)
            Va = fw.sb("Va", [128, NT * 130], BF16)
            PT = [fw.sb("PT%d" % i, [128, 512], BF16) for i in range(3)]
            O1 = fw.sb("O1", [128, 4 * 128])
            ot = fw.sb("ot", [128, 128])
            rr_ = fw.sb("rr_", [128, 8])
            t1 = fw.sb("t1", [128, 512])
            t2 = fw.sb("t2", [128, 512])
            rope = fw.sb("rope", [128, 2 * NLAT])
            subl = fw.sb("subl", [128, 128])
            lamt = fw.sb("lamt", [1, 256])
            lamp = fw.sb("lamp", [1, 128])
            lams = fw.sb("lams", [1, 4])
            neglam = fw.sb("neglam", [128, 1])
            junk = fw.sb("junk", [128, 128])
            alloc_stg()
            fw.dma(rope[:], rope_d, reads=[], writes=[rope])
            fw.dma(subl[:], subln_b_d[jj:jj + 1, :].partition_broadcast(128), reads=[], writes=[subl])
            fw.op("dve", lambda e: e.tensor_scalar_mul(out=subl[:], in0=subl[:], scalar1=1.0 - lam_init), reads=[subl], writes=[subl])
            fw.dma(lamt[:], lam_b_d[jj:jj + 1].rearrange("o a b -> o (a b)"), reads=[], writes=[lamt])
            fw.op("dve", lambda e: e.tensor_tensor(out=lamp[:].rearrange("o (a b) -> o a b", a=2), in0=lamt[:].rearrange("o (a t b) -> o a t b", a=2, t=2)[:, :, 0, :],
                                                   in1=lamt[:].rearrange("o (a t b) -> o a t b", a=2, t=2)[:, :, 1, :], op=ALU.mult), reads=[lamt], writes=[lamp])
            for a in range(2):
                fw.op("dve", lambda e: e.reduce_sum(out=lams[:, a:a + 1], in_=lamp[:, a * 64:(a + 1) * 64], axis=mybir.AxisListType.X), reads=[lamp], writes=[lams])
            fw.op("act", lambda e: e.activation(out=lams[:, 0:2], in_=lams[:, 0:2], func=AF.Exp), reads=[lams], writes=[lams])
            fw.op("dve", lambda e: e.scalar_tensor_tensor(out=lams[:, 2:3], in0=lams[:, 1:2], scalar=-lam_init, in1=lams[:, 0:1], op0=ALU.add, op1=ALU.subtract),
                  reads=[lams], writes=[lams])
            bk0 = banks[0]
            fw.mm(bk0, bk0[:, 0:1], cst, onesf[0:1, :], lams, lams[:, 2:3])
            fw.evac(neglam, neglam[:], bk0, bk0[:, 0:1])
            load_w(woutb, lambda c0, n: woutb[:].rearrange("p (k f) -> p k f", k=KC)[:, :, c0:c0 + n], w_out_b_d[jj], D)
            fw.op("pool", lambda e: e.memset(Va[:], 1.0), reads=[], writes=[Va])
            cosT = rope[:, 0:NLAT]
            sinT = rope[:, NLAT:2 * NLAT]
            chunks = [(0, 256)] + [(256 + i * 512, 512) for i in range(4)]
            for b in range(NSEQ):
                for ci in range(NTOK // NCH):
                    t0 = ci * NCH
                    col = 2 if ci == 0 else b
                    load_xres(b, t0, NCH)
                    for c in range(KC):
                        fw.op("dve" if c % 2 else "pool",
                              lambda e: e.tensor_scalar(out=hT[:, c * NTOK + t0:c * NTOK + t0 + NCH], in0=xres[:, c * NCH:(c + 1) * NCH],
                                                        scalar1=mod_col(l, 1, c, col), scalar2=mod_col(l, 0, c, col),
                                                        op0=ALU.mult, op1=ALU.add), reads=[xres, MOD], writes=[hT])
                for h in range(8):
                    srcs = [(w_in_b_d[jj], h * 128), (w_in_b_d[jj], 1024 + h * 128), (w_in_b_d[jj], 2048 + h * 128),
                            (w_in_bp_d[jj], h * 128), (w_in_bp_d[jj], 1024 + h * 128)]
                    for wi, (src, c0) in enumerate(srcs):
                        load_w(wh[wi], lambda cc, n, wi=wi: wh[wi][:].rearrange("p (k f) -> p k f", k=KC)[:, :, cc:cc + n], src, 128, col0=c0, blk=128)
                    wq, wk, wv, wqp, wkp = wh
                    for (dst, wa, wp) in ((KT, wk, wkp), (QT, wq, wqp)):
                        for (t0, n) in chunks:
                            pa, pb = banks[0], banks[1]
                            for k in range(KC):
                                fw.mm(pa, pa[:, 0:n], wa, wa[:, k * 128:(k + 1) * 128], hT, hT[:, k * NTOK + t0:k * NTOK + t0 + n],
                                      start=(k == 0), stop=(k == KC - 1))
                            if t0 < NCTX:
                                fw.evac(dst, dst[:, t0:t0 + n], pa, pa[:, 0:n])
                                continue
                            for k in range(KC):
                                fw.mm(pb, pb[:, 0:n], wp, wp[:, k * 128:(k + 1) * 128], hT, hT[:, k * NTOK + t0:k * NTOK + t0 + n],
                                      start=(k == 0), stop=(k == KC - 1))
                            lt = t0 - NCTX
                            fw.op("dve", lambda e: e.tensor_tensor(out=t1[:, 0:n], in0=pa[:, 0:n], in1=cosT[:, lt:lt + n], op=ALU.mult), reads=[pa, rope], writes=[t1])
                            fw.op("dve", lambda e: e.tensor_tensor(out=t2[:, 0:n], in0=pb[:, 0:n], in1=sinT[:, lt:lt + n], op=ALU.mult), reads=[pb, rope], writes=[t2])
                            fw.op("pool", lambda e: e.tensor_tensor(out=dst[:, t0:t0 + n], in0=t1[:, 0:n], in1=t2[:, 0:n], op=ALU.add), reads=[t1, t2], writes=[dst])
                    for j in range(NT):
                        pv = banks[j % 2]
                        for k in range(KC):
                            fw.mm(pv, pv[:, 0:128], hT, hT[:, k * NTOK + j * 128:k * NTOK + (j + 1) * 128], wv, wv[:, k * 128:(k + 1) * 128],
                                  start=(k == 0), stop=(k == KC - 1))
                        fw.evac(Va, Va[:, j * 130:j * 130 + 128], pv, pv[:, 0:128])
                    pti = 0
                    for (q0, nq) in chunks:
                        ktiles = [0, 1] if q0 < NCTX else list(range(NT))
                        nqs = nq // 128
                        for t in range(2):
                            r0 = t * 64
                            for ki, kt in enumerate(ktiles):
                                ps_ = banks[ki % 2]
                                fw.mm(ps_, ps_[:, 0:nq], KT, KT[r0:r0 + 64, kt * 128:(kt + 1) * 128], QT, QT[r0:r0 + 64, q0:q0 + nq])
                                pt = PT[pti % 3]
                                pti += 1
                                fw.op("act", lambda e: e.activation(out=pt[:, 0:nq], in_=ps_[:, 0:nq], func=AF.Exp, scale=0.125), reads=[ps_], writes=[pt])
                                for qs in range(nqs):
                                    ob = banks[2 + qs]
                                    fw.mm(ob, ob[:, 0:129], pt, pt[:, qs * 128:(qs + 1) * 128], Va, Va[:, kt * 130:kt * 130 + 129],
                                          start=(ki == 0), stop=(ki == len(ktiles) - 1))
                            for qs in range(nqs):
                                ob = banks[2 + qs]
                                fw.op("dve", lambda e: e.reciprocal(out=rr_[:, qs:qs + 1], in_=ob[:, 128:129]), reads=[ob], writes=[rr_])
                                if t == 0:
                                    fw.op("dve", lambda e: e.tensor_scalar(out=O1[:, qs * 128:(qs + 1) * 128], in0=ob[:, 0:128], scalar1=rr_[:, qs:qs + 1],
                                                                           scalar2=None, op0=ALU.mult), reads=[ob, rr_], writes=[O1])
                                else:
                                    tile_j = (q0 + qs * 128) // 128
                                    fw.op("dve", lambda e: e.tensor_scalar(out=ot[:], in0=ob[:, 0:128], scalar1=rr_[:, qs:qs + 1],
                                                                           scalar2=neglam[:, 0:1], op0=ALU.mult, op1=ALU.mult), reads=[ob, rr_, neglam], writes=[ot])
                                    fw.op("dve", lambda e: e.tensor_tensor(out=ot[:], in0=ot[:], in1=O1[:, qs * 128:(qs + 1) * 128], op=ALU.add), reads=[ot, O1], writes=[ot])
                                    fw.op("act", lambda e: e.activation(out=junk[:], in_=ot[:], func=AF.Square, accum_out=rr_[:, 4 + qs:5 + qs]), reads=[ot], writes=[junk, rr_])
                                    fw.op("dve", lambda e: e.tensor_scalar(out=rr_[:, 4 + qs:5 + qs], in0=rr_[:, 4 + qs:5 + qs], scalar1=1.0 / 128, scalar2=EPS,
                                                                           op0=ALU.mult, op1=ALU.add), reads=[rr_], writes=[rr_])
                                    fw.op("act", lambda e: e.activation(out=rr_[:, 4 + qs:5 + qs], in_=rr_[:, 4 + qs:5 + qs], func=AF.Sqrt), reads=[rr_], writes=[rr_])
                                    fw.op("dve", lambda e: e.reciprocal(out=rr_[:, 4 + qs:5 + qs], in_=rr_[:, 4 + qs:5 + qs]), reads=[rr_], writes=[rr_])
                                    fw.op("dve", lambda e: e.scalar_tensor_tensor(out=Y[:, tile_j * D + h * 128:tile_j * D + (h + 1) * 128], in0=ot[:],
                                                                                  scalar=rr_[:, 4 + qs:5 + qs], in1=subl[:], op0=ALU.mult, op1=ALU.mult),
                                          reads=[ot, rr_, subl], writes=[Y])
                    if b == 0 and h == 0:
                        dump("KT", KT, KT[:], NTOK)
                        dump("QT", QT, QT[:], NTOK)
                        dump("Va", Va, Va[:], NT * 130)
                        for j_ in range(NT):
                            dump("Y%d" % j_, Y, Y[:, j_ * D:j_ * D + 128], 128)
                        dump("neglam", neglam, neglam[:], 1)
                        dump("hT", hT, hT[:, 0:NTOK], NTOK)
                mixer_epilogue(l, b, Y, woutb, YT)

        def mixer_mlstm(l, jj):
            woutb = fw.sb("woutb", [128, KC * D], BF16)
            YT = fw.sb("YT", [128, KC * NCH], BF16)
            hT = fw.sb("hT_seq", [128, KC * NTOK], BF16)
            Y = fw.sb("Y_seq", [128, NT * D], BF16)
            wq = fw.sb("wq", [128, KC * 128], BF16)
            wk = fw.sb("wk", [128, KC * 128], BF16)
            wv = fw.sb("wv", [128, KC * 256], BF16)
            wo = fw.sb("wo", [128, KC * 256], BF16)
            wg = fw.sb("wg", [128, KC * 16], BF16)
            KT = fw.sb("KT", [128, NTOK], BF16)
            QT = fw.sb("QT", [128, NTOK], BF16)
            Va = fw.sb("Va", [128, NT * 258], BF16)
            Pb = fw.sb("Pb", [128, NTOK])
            Hacc = fw.sb("Hacc", [128, NT * 256])
            Dt = [fw.sb("Dt%d" % i, [128, 512]) for i in range(2)]
            WT = [fw.sb("WT%d" % i, [128, 512], BF16) for i in range(2)]
            tmpd = fw.sb("tmpd", [128, 128])
            Gt = fw.sb("Gt", [128, NT * 16])
            LF = fw.sb("LF", [128, NT * 16])
            CW = fw.sb("CW", [128, NT * 8])
            TOT = fw.sb("TOT", [128, NT * 8])
            PC = fw.sb("PC", [128, NT * 8])
            AC = fw.sb("AC", [128, NT * 8])
            carry = fw.sb("carry", [128, 4])
            GB = fw.sb("GB", [128, 16])
            normc = fw.sb("normc", [128, D])
            MB = [fw.sb("MB%d" % i, [128, 128]) for i in range(2)]
            sm = fw.sb("sm", [128, 8])
            og = fw.sb("og", [128, 256])
            junk = fw.sb("junk", [128, 256])
            alloc_stg(128)
            fw.dma(GB[:], gbias_c_d[jj:jj + 1, :].partition_broadcast(128), reads=[], writes=[GB])
            fw.dma(normc[:], norm_c_d[jj:jj + 1, :].partition_broadcast(128), reads=[], writes=[normc])
            for dr in range(2):
                fw.op("dve", lambda e: e.tensor_scalar(out=MB[dr][:], in0=U[dr], scalar1=-1.0, scalar2=30000.0, op0=ALU.add, op1=ALU.mult),
                      reads=[cst], writes=[MB[dr]])
            load_w(woutb, lambda c0, n: woutb[:].rearrange("p (k f) -> p k f", k=KC)[:, :, c0:c0 + n], w_out_c_d[jj], D, blk=128)
            load_w(wg, lambda c0, n: wg[:].rearrange("p (k f) -> p k f", k=KC)[:, :, c0:c0 + n], w_in_c_d[jj], 16, col0=3072, blk=16)
            fw.op("pool", lambda e: e.memset(Va[:], 1.0), reads=[], writes=[Va])
            chunks = [(0, 256)] + [(256 + i * 512, 512) for i in range(4)]
            for b in range(NSEQ):
                for ci in range(NTOK // NCH):
                    t0 = ci * NCH
                    col = 2 if ci == 0 else b
                    load_xres(b, t0, NCH)
                    for c in range(KC):
                        fw.op("dve" if c % 2 else "pool",
                              lambda e: e.tensor_scalar(out=hT[:, c * NTOK + t0:c * NTOK + t0 + NCH], in0=xres[:, c * NCH:(c + 1) * NCH],
                                                        scalar1=mod_col(l, 1, c, col), scalar2=mod_col(l, 0, c, col),
                                                        op0=ALU.mult, op1=ALU.add), reads=[xres, MOD], writes=[hT])
                for j in range(NT):
                    pg = banks[j % 2]
                    for k in range(KC):
                        fw.mm(pg, pg[:, 0:16], hT, hT[:, k * NTOK + j * 128:k * NTOK + (j + 1) * 128], wg, wg[:, k * 16:(k + 1) * 16],
                              start=(k == 0), stop=(k == KC - 1))
                    fw.op("dve", lambda e: e.tensor_tensor(out=Gt[:, j * 16:(j + 1) * 16], in0=pg[:, 0:16], in1=GB[:], op=ALU.add), reads=[pg, GB], writes=[Gt])
                fw.op("act", lambda e: e.activation(out=LF[:], in_=Gt[:], func=AF.Exp, scale=-1.0), reads=[Gt], writes=[LF])
                fw.op("act", lambda e: e.activation(out=LF[:], in_=LF[:], func=AF.Ln, bias=1.0, scale=1.0), reads=[LF], writes=[LF])
                fw.op("dve", lambda e: e.tensor_scalar_mul(out=LF[:], in0=LF[:], scalar1=-1.0), reads=[LF], writes=[LF])
                for dr in range(2):
                    pc_, pt_ = banks[0], banks[1]
                    for j in range(NT):
                        fcol = j * 16 + (1 + 2 * dr) * 4
                        fw.mm(pc_, pc_[:, j * 4:(j + 1) * 4], cst, U[dr], LF, LF[:, fcol:fcol + 4])
                        fw.mm(pt_, pt_[:, j * 4:(j + 1) * 4], cst, onesf, LF, LF[:, fcol:fcol + 4])
                    cwv = CW[:].rearrange("p (j d h) -> p j d h", j=NT, d=2)[:, :, dr, :]
                    totv = TOT[:].rearrange("p (j d h) -> p j d h", j=NT, d=2)[:, :, dr, :]
                    fw.op("dve", lambda e: e.tensor_copy(out=cwv, in_=pc_[:, 0:NT * 4].rearrange("p (j h) -> p j h", j=NT)), reads=[pc_], writes=[CW])
                    fw.op("act", lambda e: e.copy(out=totv, in_=pt_[:, 0:NT * 4].rearrange("p (j h) -> p j h", j=NT)), reads=[pt_], writes=[TOT])
                    order = list(range(NT)) if dr == 0 else [1, 0] + list(range(NT - 1, 1, -1))
                    fw.op("dve", lambda e: e.memset(carry[:], 0.0), reads=[], writes=[carry])
                    for j in order:
                        o8 = j * 8 + dr * 4
                        fw.op("dve", lambda e: e.tensor_tensor(out=PC[:, o8:o8 + 4], in0=CW[:, o8:o8 + 4], in1=carry[:], op=ALU.add), reads=[CW, carry], writes=[PC])
                        fw.op("dve", lambda e: e.tensor_tensor(out=carry[:], in0=carry[:], in1=TOT[:, o8:o8 + 4], op=ALU.add), reads=[carry, TOT], writes=[carry])
                    liv = Gt[:].rearrange("p (j g h) -> p j g h", j=NT, g=4)[:, :, 2 * dr, :]
                    pcv = PC[:].rearrange("p (j d h) -> p j d h", j=NT, d=2)[:, :, dr, :]
                    acv = AC[:].rearrange("p (j d h) -> p j d h", j=NT, d=2)[:, :, dr, :]
                    fw.op("dve", lambda e: e.tensor_tensor(out=acv, in0=liv, in1=pcv, op=ALU.subtract), reads=[Gt, PC], writes=[AC])
                for h in range(4):
                    load_w(wq, lambda c0, n: wq[:].rearrange("p (k f) -> p k f", k=KC)[:, :, c0:c0 + n], w_in_c_d[jj], 128, col0=h * 128, blk=128)
                    load_w(wk, lambda c0, n: wk[:].rearrange("p (k f) -> p k f", k=KC)[:, :, c0:c0 + n], w_in_c_d[jj], 128, col0=512 + h * 128, blk=128)
                    load_w(wv, lambda c0, n: wv[:].rearrange("p (k f) -> p k f", k=KC)[:, :, c0:c0 + n], w_in_c_d[jj], 256, col0=1024 + h * 256, blk=128)
                    load_w(wo, lambda c0, n: wo[:].rearrange("p (k f) -> p k f", k=KC)[:, :, c0:c0 + n], w_in_c_d[jj], 256, col0=2048 + h * 256, blk=128)
                    for (dst, wa, sc) in ((KT, wk, 128.0 ** -0.5), (QT, wq, 1.0)):
                        for (t0, n) in chunks:
                            pa = banks[(t0 // 256) % 2]
                            for k in range(KC):
                                fw.mm(pa, pa[:, 0:n], wa, wa[:, k * 128:(k + 1) * 128], hT, hT[:, k * NTOK + t0:k * NTOK + t0 + n],
                                      start=(k == 0), stop=(k == KC - 1))
                            fw.op("act", lambda e: e.mul(out=dst[:, t0:t0 + n], in_=pa[:, 0:n], mul=sc), reads=[pa], writes=[dst])
                    for j in range(NT):
                        pv = banks[j % 2]
                        for k in range(KC):
                            fw.mm(pv, pv[:, 0:256], hT, hT[:, k * NTOK + j * 128:k * NTOK + (j + 1) * 128], wv, wv[:, k * 256:(k + 1) * 256],
                                  start=(k == 0), stop=(k == KC - 1))
                        fw.evac(Va, Va[:, j * 258:j * 258 + 256], pv, pv[:, 0:256])
                    for dr in range(2):
                        for j in range(NT):
                            pp = banks[j % 2]
                            c8 = j * 8 + dr * 4 + h
                            fw.mm(pp, pp[:, 0:128], PC, PC[:, c8:c8 + 1].to_broadcast([128, 128]), cst, identf)
                            fw.evac(Pb, Pb[:, j * 128:(j + 1) * 128], pp, pp[:, 0:128])

                        def vis(js, jt):
                            s_ctx, t_ctx = js < 2, jt < 2
                            if t_ctx and not s_ctx:
                                return 0
                            if s_ctx and not t_ctx:
                                return 1
                            if js == jt:
                                return 2
                            if dr == 0:
                                return 1 if js < jt else 0
                            return 1 if js > jt else 0
                        bi = 0
                        for (q0, nq) in chunks:
                            ttiles = list(range(q0 // 128, (q0 + nq) // 128))
                            slist = [js for js in range(NT) if any(vis(js, jt) for jt in ttiles)]
                            first = {}
                            last = {}
                            for js in slist:
                                for jt in ttiles:
                                    if vis(js, jt):
                                        first.setdefault(jt, js)
                                        last[jt] = js
                            for js in slist:
                                need = [jt for jt in ttiles if vis(js, jt)]
                                a0, a1 = need[0] * 128, (need[-1] + 1) * 128
                                n = a1 - a0
                                ps_ = banks[bi % 2]
                                dt_ = Dt[bi % 2]
                                wt_ = WT[bi % 2]
                                bi += 1
                                fw.mm(ps_, ps_[:, 0:n], KT, KT[:, js * 128:(js + 1) * 128], QT, QT[:, a0:a1])
                                acol = AC[:, js * 8 + dr * 4 + h:js * 8 + dr * 4 + h + 1]
                                full = [jt for jt in need if vis(js, jt) == 1]
                                diag = [jt for jt in need if vis(js, jt) == 2]
                                if full:
                                    f0, f1 = full[0] * 128, (full[-1] + 1) * 128
                                    fw.op("act", lambda e: e.activation(out=dt_[:, f0 - a0:f1 - a0], in_=Pb[:, f0:f1], func=AF.Exp, bias=acol, scale=1.0),
                                          reads=[Pb, AC], writes=[dt_])
                                if diag:
                                    d0 = diag[0] * 128
                                    fw.op("dve", lambda e: e.tensor_tensor(out=tmpd[:], in0=Pb[:, d0:d0 + 128], in1=MB[dr][:], op=ALU.add), reads=[Pb, MB[dr]], writes=[tmpd])
                                    fw.op("act", lambda e: e.activation(out=dt_[:, d0 - a0:d0 - a0 + 128], in_=tmpd[:], func=AF.Exp, bias=acol, scale=1.0),
                                          reads=[tmpd, AC], writes=[dt_])
                                fw.op("dve", lambda e: e.tensor_tensor(out=wt_[:, 0:n], in0=ps_[:, 0:n], in1=dt_[:, 0:n], op=ALU.mult), reads=[ps_, dt_], writes=[wt_])
                                for jt in need:
                                    ob = banks[2 + (jt - ttiles[0])]
                                    fw.mm(ob, ob[:, 0:257], wt_, wt_[:, jt * 128 - a0:jt * 128 - a0 + 128], Va, Va[:, js * 258:js * 258 + 257],
                                          start=(first[jt] == js), stop=(last[jt] == js))
                            for jt in ttiles:
                                ob = banks[2 + (jt - ttiles[0])]
                                fw.op("act", lambda e: e.activation(out=sm[:, 0:1], in_=ob[:, 256:257], func=AF.Abs), reads=[ob], writes=[sm])
                                fw.op("dve", lambda e: e.tensor_scalar_max(out=sm[:, 0:1], in0=sm[:, 0:1], scalar1=1.0), reads=[sm], writes=[sm])
                                fw.op("dve", lambda e: e.reciprocal(out=sm[:, 0:1], in_=sm[:, 0:1]), reads=[sm], writes=[sm])
                                hv = Hacc[:, jt * 256:(jt + 1) * 256]
                                if dr == 0:
                                    fw.op("dve", lambda e: e.tensor_scalar(out=hv, in0=ob[:, 0:256], scalar1=sm[:, 0:1], scalar2=None, op0=ALU.mult), reads=[ob, sm], writes=[Hacc])
                                else:
                                    fw.op("dve", lambda e: e.scalar_tensor_tensor(out=hv, in0=ob[:, 0:256], scalar=sm[:, 0:1], in1=hv, op0=ALU.mult, op1=ALU.add),
                                          reads=[ob, sm, Hacc], writes=[Hacc])
                    for j in range(NT):
                        po = banks[j % 2]
                        for k in range(KC):
                            fw.mm(po, po[:, 0:256], hT, hT[:, k * NTOK + j * 128:k * NTOK + (j + 1) * 128], wo, wo[:, k * 256:(k + 1) * 256],
                                  start=(k == 0), stop=(k == KC - 1))
                        fw.op("act", lambda e: e.activation(out=og[:], in_=po[:, 0:256], func=AF.Sigmoid), reads=[po], writes=[og])
                        hv = Hacc[:, j * 256:(j + 1) * 256]
                        fw.op("act", lambda e: e.activation(out=junk[:], in_=hv, func=AF.Square, accum_out=sm[:, 1:2]), reads=[Hacc], writes=[junk, sm])
                        fw.op("dve", lambda e: e.tensor_scalar(out=sm[:, 1:2], in0=sm[:, 1:2], scalar1=1.0 / 256, scalar2=EPS, op0=ALU.mult, op1=ALU.add), reads=[sm], writes=[sm])
                        fw.op("act", lambda e: e.activation(out=sm[:, 1:2], in_=sm[:, 1:2], func=AF.Sqrt), reads=[sm], writes=[sm])
                        fw.op("dve", lambda e: e.reciprocal(out=sm[:, 1:2], in_=sm[:, 1:2]), reads=[sm], writes=[sm])
                        fw.op("dve", lambda e: e.scalar_tensor_tensor(out=og[:], in0=og[:], scalar=sm[:, 1:2], in1=normc[:, h * 256:(h + 1) * 256], op0=ALU.mult, op1=ALU.mult),
                              reads=[og, sm, normc], writes=[og])
                        fw.op("dve", lambda e: e.tensor_tensor(out=Y[:, j * D + h * 256:j * D + (h + 1) * 256], in0=og[:], in1=hv, op=ALU.mult), reads=[og, Hacc], writes=[Y])
                mixer_epilogue(l, b, Y, woutb, YT)

        def mixer_gdn(l, jj):
            convw = fw.sb("convw", [128, 240])
            normg = fw.sb("normg", [128, 128])
            DTB = fw.sb("DTB", [128, 32])
            NEGA = fw.sb("NEGA", [128, 32])
            fw.dma(convw[:], conv_aT_d, reads=[], writes=[convw])
            fw.dma(normg[:], norm_a_d[jj:jj + 1, :].partition_broadcast(128), reads=[], writes=[normg])
            fw.op("dve", lambda e: e.memset(DTB[:], 0.0), reads=[], writes=[DTB])
            fw.op("dve", lambda e: e.memset(NEGA[:], 0.0), reads=[], writes=[NEGA])
            for dr in range(2):
                fw.dma(DTB[:, dr * 16:dr * 16 + 8], dtb_d[jj:jj + 1, dr * 8:dr * 8 + 8].partition_broadcast(128), reads=[], writes=[DTB])
                fw.dma(NEGA[:, dr * 16:dr * 16 + 8], alog_d[jj:jj + 1, dr * 8:dr * 8 + 8].partition_broadcast(128), reads=[], writes=[NEGA])
            fw.op("act", lambda e: e.activation(out=NEGA[:], in_=NEGA[:], func=AF.Exp), reads=[NEGA], writes=[NEGA])
            fw.op("dve", lambda e: e.tensor_scalar_mul(out=NEGA[:], in0=NEGA[:], scalar1=-1.0), reads=[NEGA], writes=[NEGA])
            chunks = [(0, 256)] + [(256 + i * 512, 512) for i in range(4)]
            slots = [[SlotView(banks[2 * dr + q % 2], banks[2 * dr + q % 2][:, (q // 2) * 128:(q // 2 + 1) * 128]) for q in range(8)] for dr in range(2)]
            si = [0, 0]
            uid = [0]

            def mmq(dr, lb, lap, rb, rap):
                sl = slots[dr][si[dr] % 8]
                si[dr] += 1
                fw.mm(sl, sl.t, lb, lap, rb, rap)
                return sl

            mmq.slots = slots
            mmq.si = si
            for b in range(NSEQ):
                with fw.scope():
                    hT = fw.sb("hT_seq", [128, KC * NTOK], BF16)
                    ZY = fw.sb("ZY_seq", [128, NT * D], BF16)
                    Gt = fw.sb("Gt", [128, NT * 32])
                    GL = fw.sb("GL", [128, NT * 32])
                    BT = fw.sb("BT", [128, NT * 32])
                    CC = fw.sb("CC", [128, NT * 16])
                    for ci in range(NTOK // NCH):
                        t0 = ci * NCH
                        col = 2 if ci == 0 else b
                        load_xres(b, t0, NCH)
                        for c in range(KC):
                            fw.op("dve" if c % 2 else "pool",
                                  lambda e: e.tensor_scalar(out=hT[:, c * NTOK + t0:c * NTOK + t0 + NCH], in0=xres[:, c * NCH:(c + 1) * NCH],
                                                            scalar1=mod_col(l, 1, c, col), scalar2=mod_col(l, 0, c, col),
                                                            op0=ALU.mult, op1=ALU.add), reads=[xres, MOD], writes=[hT])
                    with fw.scope():
                        alloc_stg(128)
                        wz = fw.sb("wz", [128, KC * 128], BF16)
                        wgt = fw.sb("wgt", [128, KC * 32], BF16)
                        for hb in range(8):
                            load_w(wz, lambda c0, n: wz[:].rearrange("p (k f) -> p k f", k=KC)[:, :, c0:c0 + n], w_in_a_d[jj], 128, col0=3072 + hb * 128, blk=128)
                            for j in range(NT):
                                pz = banks[4 + j % 3]
                                for k in range(KC):
                                    fw.mm(pz, pz[:, 0:128], hT, hT[:, k * NTOK + j * 128:k * NTOK + (j + 1) * 128], wz, wz[:, k * 128:(k + 1) * 128],
                                          start=(k == 0), stop=(k == KC - 1))
                                fw.op("act", lambda e: e.activation(out=ZY[:, j * D + hb * 128:j * D + (hb + 1) * 128], in_=pz[:, 0:128], func=AF.Silu), reads=[pz], writes=[ZY])
                        load_w(wgt, lambda c0, n: wgt[:].rearrange("p (k f) -> p k f", k=KC)[:, :, c0:c0 + n], w_in_a_d[jj], 32, col0=4096, blk=32)
                        for j in range(NT):
                            pg = banks[4 + j % 3]
                            for k in range(KC):
                                fw.mm(pg, pg[:, 0:32], hT, hT[:, k * NTOK + j * 128:k * NTOK + (j + 1) * 128], wgt, wgt[:, k * 32:(k + 1) * 32],
                                      start=(k == 0), stop=(k == KC - 1))
                            fw.evac(Gt, Gt[:, j * 32:(j + 1) * 32], pg, pg[:, 0:32])
                            fw.op("dve", lambda e: e.tensor_tensor(out=GL[:, j * 32:(j + 1) * 32], in0=Gt[:, j * 32:(j + 1) * 32], in1=DTB[:], op=ALU.add), reads=[Gt, DTB], writes=[GL])
                        fw.op("act", lambda e: e.activation(out=GL[:], in_=GL[:], func=AF.Exp), reads=[GL], writes=[GL])
                        fw.op("act", lambda e: e.activation(out=GL[:], in_=GL[:], func=AF.Ln, bias=1.0, scale=1.0), reads=[GL], writes=[GL])
                        for j in range(NT):
                            fw.op("dve", lambda e: e.tensor_tensor(out=GL[:, j * 32:(j + 1) * 32], in0=GL[:, j * 32:(j + 1) * 32], in1=NEGA[:], op=ALU.mult), reads=[GL, NEGA], writes=[GL])
                        fw.op("act", lambda e: e.activation(out=BT[:], in_=Gt[:], func=AF.Sigmoid), reads=[Gt], writes=[BT])
                        pc_ = banks[6]
                        for j in range(NT):
                            for dr in range(2):
                                fw.mm(pc_, pc_[:, j * 16 + dr * 8:j * 16 + dr * 8 + 8], cst, U[dr], GL, GL[:, j * 32 + dr * 16:j * 32 + dr * 16 + 8])
                        fw.evac(CC, CC[:], pc_, pc_[:, 0:NT * 16])
                    for h in range(GDN_HEADS):
                        with fw.scope():
                            gdn_head(l, jj, b, h, hT, ZY, GL, BT, CC, convw, normg, mmq, chunks)
                    with fw.scope():
                        alloc_stg(256)
                        woutb = fw.sb("woutb", [128, KC * D], BF16)
                        YT = fw.sb("YT", [128, KC * NCH], BF16)
                        load_w(woutb, lambda c0, n: woutb[:].rearrange("p (k f) -> p k f", k=KC)[:, :, c0:c0 + n], w_out_a_d[jj], D, blk=256)
                        mixer_epilogue(l, b, ZY, woutb, YT)

        def gdn_head(l, jj, b, h, hT, ZY, GL, BT, CC, convw, normg, mmq, chunks):
            if GDN_STAGE < 1:
                return
            alloc_stg(128)
            wq = fw.sb("wq", [128, KC * 128], BF16)
            wk = fw.sb("wk", [128, KC * 128], BF16)
            wv = fw.sb("wv", [128, KC * 128], BF16)
            Pc = fw.sb("Pc", [128, 260])
            Pl = fw.sb("Pl", [128, 2052])
            raw = fw.sb("raw", [128, 512])
            sqt = fw.sb("sqt", [128, 512])
            rs = fw.sb("rs", [128, 512])
            QT = fw.sb("QT", [128, NTOK], BF16)
            KT = fw.sb("KT", [128, NTOK], BF16)
            VT = fw.sb("VT", [128, NTOK], BF16)
            Ktok = fw.sb("Ktok", [128, NT * 128], BF16)
            Vtok = fw.sb("Vtok", [128, NT * 128], BF16)
            Oacc = fw.sb("Oacc", [128, NT * 128])
            S = [fw.sb("S%d" % d, [128, 128]) for d in range(2)]
            Sbf = [fw.sb("Sbf%d" % d, [128, 128], BF16) for d in range(2)]
            sm = fw.sb("smg", [128, 4])
            junk = fw.sb("junkg", [128, 128])
            yt = fw.sb("ytg", [128, 128])
            fw.op("pool", lambda e: e.memset(Pc[:], 0.0), reads=[], writes=[Pc])
            fw.op("pool", lambda e: e.memset(Pl[:], 0.0), reads=[], writes=[Pl])
            fw.op("pool", lambda e: e.memset(Oacc[:], 0.0), reads=[], writes=[Oacc])
            for d in range(2):
                fw.op("pool", lambda e: e.memset(S[d][:], 0.0), reads=[], writes=[S[d]])
                fw.op("pool", lambda e: e.memset(Sbf[d][:], 0.0), reads=[], writes=[Sbf[d]])
            for wi, (wt_, c0) in enumerate(((wq, h * 128), (wk, 1024 + h * 128), (wv, 2048 + h * 128))):
                load_w(wt_, lambda cc, n, wt_=wt_: wt_[:].rearrange("p (k f) -> p k f", k=KC)[:, :, cc:cc + n], w_in_a_d[jj], 128, col0=c0, blk=128)
            for kind, (wt_, dst) in enumerate(((wq, QT), (wk, KT), (wv, VT))):
                chn = kind * 8 + h
                for (t0, n) in chunks:
                    pa = banks[4 + (t0 // 256) % 3]
                    for k in range(KC):
                        fw.mm(pa, pa[:, 0:n], wt_, wt_[:, k * 128:(k + 1) * 128], hT, hT[:, k * NTOK + t0:k * NTOK + t0 + n],
                              start=(k == 0), stop=(k == KC - 1))
                    if t0 < NCTX:
                        fw.evac(Pc, Pc[:, 2:2 + n], pa, pa[:, 0:n])
                    else:
                        fw.evac(Pl, Pl[:, 2 + t0 - NCTX:2 + t0 - NCTX + n], pa, pa[:, 0:n])
                for (t0, n) in chunks:
                    src, off = (Pc, t0) if t0 < NCTX else (Pl, t0 - NCTX)
                    cw = lambda k: convw[:, (jj * 5 + k) * 24 + chn:(jj * 5 + k) * 24 + chn + 1]
                    fw.op("dve", lambda e: e.tensor_scalar(out=raw[:, 0:n], in0=src[:, off:off + n], scalar1=cw(0), scalar2=None, op0=ALU.mult),
                          reads=[src, convw], writes=[raw])
                    for k in range(1, 5):
                        fw.op("dve", lambda e: e.scalar_tensor_tensor(out=raw[:, 0:n], in0=src[:, off + k:off + k + n], scalar=cw(k), in1=raw[:, 0:n],
                                                                      op0=ALU.mult, op1=ALU.add), reads=[src, convw, raw], writes=[raw])
                    if kind == 2:
                        fw.op("act", lambda e: e.activation(out=dst[:, t0:t0 + n], in_=raw[:, 0:n], func=AF.Silu), reads=[raw], writes=[dst])
                        continue
                    fw.op("act", lambda e: e.activation(out=raw[:, 0:n], in_=raw[:, 0:n], func=AF.Silu), reads=[raw], writes=[raw])
                    fw.op("act", lambda e: e.activation(out=sqt[:, 0:n], in_=raw[:, 0:n], func=AF.Square), reads=[raw], writes=[sqt])
                    pss = banks[4 + (t0 // 256) % 3]
                    fw.mm(pss, pss[:, 0:n], cst, onesf, sqt, sqt[:, 0:n])
                    fw.op("dve", lambda e: e.tensor_scalar_add(out=rs[:, 0:n], in0=pss[:, 0:n], scalar1=EPS), reads=[pss], writes=[rs])
                    fw.op("act", lambda e: e.activation(out=rs[:, 0:n], in_=rs[:, 0:n], func=AF.Sqrt), reads=[rs], writes=[rs])
                    fw.op("dve", lambda e: e.reciprocal(out=rs[:, 0:n], in_=rs[:, 0:n]), reads=[rs], writes=[rs])
                    sc = 128.0 ** -0.5 if kind == 0 else 1.0
                    fw.op("dve", lambda e: e.scalar_tensor_tensor(out=dst[:, t0:t0 + n], in0=raw[:, 0:n], scalar=sc, in1=rs[:, 0:n], op0=ALU.mult, op1=ALU.mult),
                          reads=[raw, rs], writes=[dst])
            for (srcT, dstK) in ((KT, Ktok), (VT, Vtok)):
                for j in range(NT):
                    q8 = j % 8
                    fw.tr(pbf, pbf[:, q8 * 128:(q8 + 1) * 128], srcT, srcT[:, j * 128:(j + 1) * 128], identb, identb[:])
                    fw.evac(dstK, dstK[:, j * 128:(j + 1) * 128], pbf, pbf[:, q8 * 128:(q8 + 1) * 128])
            if GDN_STAGE < 2:
                return
            tcache = {}

            def T(dr, par, name, dtype=F32, w=128):
                key = (dr, par, name)
                if key not in tcache:
                    tcache[key] = fw.sb("t%d%d%s_%d%d" % (dr, par, name, b, h), [128, w], dtype)
                return tcache[key]

            def chunk_gen(dr, j, par):
                gcol = j * 32 + dr * 16 + h
                bcol = gcol + 8
                ccol = j * 16 + dr * 8 + h
                jl = 127 if dr == 0 else 0
                t0, t1_ = j * 128, (j + 1) * 128
                ccap = CC[:, ccol:ccol + 1]
                btap = BT[:, bcol:bcol + 1]
                TT = lambda name, dtype=F32, w=128: T(dr, par, name, dtype, w)
                DV = lambda fn, r, w_: fw.op("dve", fn, reads=r, writes=w_)
                AC_ = lambda fn, r, w_: fw.op("act", fn, reads=r, writes=w_)
                PL = lambda fn, r, w_: fw.op("pool", fn, reads=r, writes=w_)
                sl = mmq(dr, GL, GL[:, gcol:gcol + 1].to_broadcast([128, 128]), cst, U[dr])
                Cb = TT("Cb")
                AC_(lambda e: e.copy(out=Cb[:], in_=sl.t), [sl], [Cb])
                yield
                ta, Dm, tb, DTm, E, qg = TT("ta"), TT("Dm"), TT("tb"), TT("DTm"), TT("E"), TT("qg", BF16)
                DV(lambda e: e.tensor_scalar(out=ta[:], in0=Cb[:], scalar1=ccap, scalar2=0.0, op0=ALU.subtract, op1=ALU.max), [Cb, CC], [ta])
                DV(lambda e: e.tensor_scalar(out=tb[:], in0=Cb[:], scalar1=ccap, scalar2=0.0, op0=ALU.subtract, op1=ALU.min), [Cb, CC], [tb])
                AC_(lambda e: e.activation(out=Dm[:], in_=ta[:], func=AF.Exp, scale=-1.0), [ta], [Dm])
                AC_(lambda e: e.activation(out=DTm[:], in_=tb[:], func=AF.Exp), [tb], [DTm])
                AC_(lambda e: e.activation(out=E[:], in_=Cb[:], func=AF.Exp), [Cb], [E])
                cols = TT("cols", F32, 8)
                AC_(lambda e: e.activation(out=cols[:, 0:1], in_=ccap, func=AF.Exp), [CC], [cols])
                AC_(lambda e: e.activation(out=cols[:, 2:3], in_=ccap, func=AF.Exp, scale=-1.0, bias=Cb[:, jl:jl + 1]), [CC, Cb], [cols])
                AC_(lambda e: e.activation(out=cols[:, 3:4], in_=Cb[:, jl:jl + 1], func=AF.Exp), [Cb], [cols])
                yield
                PL(lambda e: e.tensor_tensor(out=Dm[:], in0=Dm[:], in1=Ms[dr], op=ALU.mult), [Dm, cst], [Dm])
                yield
                PL(lambda e: e.tensor_tensor(out=DTm[:], in0=DTm[:], in1=Mi[1 - dr], op=ALU.mult), [DTm, cst], [DTm])
                yield
                DV(lambda e: e.tensor_tensor(out=qg[:], in0=QT[:, t0:t1_], in1=E[:], op=ALU.mult), [QT, E], [qg])
                yield
                DV(lambda e: e.tensor_tensor(out=cols[:, 1:2], in0=cols[:, 0:1], in1=btap, op=ALU.mult), [cols, BT], [cols])
                yield
                bke, bv, kdec = TT("bke"), TT("bv"), TT("kdec", BF16)
                DV(lambda e: e.tensor_scalar(out=bke[:], in0=Ktok[:, t0:t1_], scalar1=cols[:, 1:2], scalar2=None, op0=ALU.mult), [Ktok, cols], [bke])
                yield
                DV(lambda e: e.tensor_scalar(out=bv[:], in0=Vtok[:, t0:t1_], scalar1=btap, scalar2=None, op0=ALU.mult), [Vtok, BT], [bv])
                yield
                DV(lambda e: e.tensor_scalar(out=kdec[:], in0=Ktok[:, t0:t1_], scalar1=cols[:, 2:3], scalar2=None, op0=ALU.mult), [Ktok, cols], [kdec])
                yield
                sl = mmq(dr, KT, KT[:, t0:t1_], KT, KT[:, t0:t1_])
                yield
                A, AT = TT("A"), TT("AT")
                yield
                DV(lambda e: e.scalar_tensor_tensor(out=A[:], in0=sl.t, scalar=btap, in1=Dm[:], op0=ALU.mult, op1=ALU.mult), [sl, BT, Dm], [A])
                sl2 = mmq.slots[dr][mmq.si[dr] % 8]
                mmq.si[dr] += 1
                fw.tr(sl2, sl2.t, A, A[:], cst, identf)
                AC_(lambda e: e.copy(out=AT[:], in_=sl2.t), [sl2], [AT])
                yield
                N0, N0T, A1, A1T, A2 = TT("N0"), TT("N0T"), TT("A1"), TT("A1T"), TT("A2")
                PL(lambda e: e.tensor_tensor(out=N0[:], in0=A[:], in1=NB32, op=ALU.mult), [A, cst], [N0])
                PL(lambda e: e.tensor_tensor(out=N0T[:], in0=AT[:], in1=NB32, op=ALU.mult), [AT, cst], [N0T])
                PL(lambda e: e.tensor_tensor(out=A1[:], in0=A[:], in1=B6432, op=ALU.mult), [A, cst], [A1])
                PL(lambda e: e.tensor_tensor(out=A1T[:], in0=AT[:], in1=B6432, op=ALU.mult), [AT, cst], [A1T])
                PL(lambda e: e.tensor_tensor(out=A2[:], in0=A[:], in1=NB64c, op=ALU.mult), [A, cst], [A2])
                R, Tm = TT("R0"), TT("T0")
                DV(lambda e: e.tensor_tensor(out=R[:], in0=N0T[:], in1=identf, op=ALU.add), [N0T, cst], [R])
                DV(lambda e: e.tensor_tensor(out=Tm[:], in0=N0[:], in1=identf, op=ALU.add), [N0, cst], [Tm])
                P, PT = N0, N0T
                yield
                for k in range(4):
                    s1 = mmq(dr, PT, PT[:], P, P[:])
                    s2 = mmq(dr, P, P[:], PT, PT[:])
                    yield
                    P2, PT2 = TT("P%d" % (k % 2)), TT("PT%d" % (k % 2))
                    AC_(lambda e: e.copy(out=P2[:], in_=s1.t), [s1], [P2])
                    DV(lambda e: e.tensor_copy(out=PT2[:], in_=s2.t), [s2], [PT2])
                    yield
                    s3 = mmq(dr, P2, P2[:], R, R[:])
                    s4 = mmq(dr, PT2, PT2[:], Tm, Tm[:])
                    yield
                    Rn, Tn = TT("R%d" % ((k + 1) % 2)), TT("T%d" % ((k + 1) % 2))
                    DV(lambda e: e.tensor_tensor(out=Rn[:], in0=R[:], in1=s3.t, op=ALU.add), [R, s3], [Rn])
                    DV(lambda e: e.tensor_tensor(out=Tn[:], in0=Tm[:], in1=s4.t, op=ALU.add), [Tm, s4], [Tn])
                    R, Tm, P, PT = Rn, Tn, P2, PT2
                    yield
                s1 = mmq(dr, A1T, A1T[:], Tm, Tm[:])
                s2 = mmq(dr, A1, A1[:], R, R[:])
                yield
                Xp, X = TT("Xp"), TT("X")
                AC_(lambda e: e.copy(out=Xp[:], in_=s1.t), [s1], [Xp])
                DV(lambda e: e.tensor_copy(out=X[:], in_=s2.t), [s2], [X])
                yield
                s3 = mmq(dr, R, R[:], Xp, Xp[:])
                s4 = mmq(dr, Tm, Tm[:], X, X[:])
                yield
                T1, R1 = TT("T1"), TT("R1")
                DV(lambda e: e.tensor_tensor(out=T1[:], in0=Tm[:], in1=s3.t, op=ALU.subtract), [Tm, s3], [T1])
                DV(lambda e: e.tensor_tensor(out=R1[:], in0=R[:], in1=s4.t, op=ALU.subtract), [R, s4], [R1])
                yield
                s1 = mmq(dr, A2, A2[:], R1, R1[:])
                yield
                X2 = TT("X2")
                AC_(lambda e: e.copy(out=X2[:], in_=s1.t), [s1], [X2])
                yield
                s2 = mmq(dr, T1, T1[:], X2, X2[:])
                yield
                R2 = TT("R2")
                DV(lambda e: e.tensor_tensor(out=R2[:], in0=R1[:], in1=s2.t, op=ALU.subtract), [R1, s2], [R2])
                yield
                s1 = mmq(dr, R2, R2[:], bv, bv[:])
                s2 = mmq(dr, bke, bke[:], R2, R2[:])
                s3 = mmq(dr, KT, KT[:, t0:t1_], QT, QT[:, t0:t1_])
                yield
                u, wT, QKm = TT("u"), TT("wT", BF16), TT("QKm", BF16)
                AC_(lambda e: e.copy(out=u[:], in_=s1.t), [s1], [u])
                AC_(lambda e: e.copy(out=wT[:], in_=s2.t), [s2], [wT])
                DV(lambda e: e.tensor_tensor(out=QKm[:], in0=s3.t, in1=DTm[:], op=ALU.mult), [s3, DTm], [QKm])
                yield
                s1 = mmq(dr, wT, wT[:], Sbf[dr], Sbf[dr][:])
                yield
                vn = TT("vn", BF16)
                DV(lambda e: e.tensor_tensor(out=vn[:], in0=u[:], in1=s1.t, op=ALU.subtract), [u, s1], [vn])
                yield
                ob = banks[4 + dr]
                fw.mm(ob, ob[:, 0:128], qg, qg[:], Sbf[dr], Sbf[dr][:], start=True, stop=False)
                fw.mm(ob, ob[:, 0:128], QKm, QKm[:], vn, vn[:], start=False, stop=True)
                s2 = mmq(dr, kdec, kdec[:], vn, vn[:])
                yield
                DV(lambda e: e.tensor_tensor(out=Oacc[:, t0:t1_], in0=Oacc[:, t0:t1_], in1=ob[:, 0:128], op=ALU.add), [Oacc, ob], [Oacc])
                DV(lambda e: e.scalar_tensor_tensor(out=S[dr][:], in0=S[dr][:], scalar=cols[:, 3:4], in1=s2.t, op0=ALU.mult, op1=ALU.add), [S[dr], cols, s2], [S[dr]])
                AC_(lambda e: e.copy(out=Sbf[dr][:], in_=S[dr][:]), [S[dr]], [Sbf[dr]])
                yield

            orders = [list(range(NT)), [1, 0] + list(range(NT - 1, 1, -1))]
            fw.pe_drain = PE_DRAIN
            for step in range(GDN_NSTEPS if GDN_STAGE >= 3 else 0):
                gens = [chunk_gen(dr, orders[dr][step], 0) for dr in range(2)]
                alive = list(gens)
                ny = 0
                while alive and ny < GDN_YIELDS:
                    ny += 1
                    nxt = []
                    for g in alive:
                        try:
                            next(g)
                            nxt.append(g)
                        except StopIteration:
                            pass
                        if BAR:
                            fw.barrier()
                    alive = nxt
            fw.pe_drain = False
            if b == 0 and h == 0:
                dump("gQT", QT, QT[:], NTOK)
                dump("gKT", KT, KT[:], NTOK)
                dump("gVT", VT, VT[:], NTOK)
                dump("gGL", GL, GL[:], NT * 32)
                dump("gBT", BT, BT[:], NT * 32)
                dump("gCC", CC, CC[:], NT * 16)
                dump("gO", Oacc, Oacc[:], NT * 128)
                dump("gS0", S[0], S[0][:], 128)
                dump("gS1", S[1], S[1][:], 128)
            for j in range(NT if GDN_STAGE >= 4 else 0):
                ov = Oacc[:, j * 128:(j + 1) * 128]
                fw.op("act", lambda e: e.activation(out=junk[:], in_=ov, func=AF.Square, accum_out=sm[:, 0:1]), reads=[Oacc], writes=[junk, sm])
                fw.op("dve", lambda e: e.tensor_scalar(out=sm[:, 0:1], in0=sm[:, 0:1], scalar1=1.0 / 128, scalar2=EPS, op0=ALU.mult, op1=ALU.add), reads=[sm], writes=[sm])
                fw.op("act", lambda e: e.activation(out=sm[:, 0:1], in_=sm[:, 0:1], func=AF.Sqrt), reads=[sm], writes=[sm])
                fw.op("dve", lambda e: e.reciprocal(out=sm[:, 0:1], in_=sm[:, 0:1]), reads=[sm], writes=[sm])
                fw.op("dve", lambda e: e.scalar_tensor_tensor(out=yt[:], in0=ov, scalar=sm[:, 0:1], in1=normg[:], op0=ALU.mult, op1=ALU.mult), reads=[Oacc, sm, normg], writes=[yt])
                zv = ZY[:, j * D + h * 128:j * D + (h + 1) * 128]
                fw.op("dve", lambda e: e.tensor_tensor(out=zv, in0=yt[:], in1=zv, op=ALU.mult), reads=[yt, ZY], writes=[ZY])

        mixers = {0: mixer_gdn, 1: mixer_attn, 2: mixer_mlstm}

        with fw.scope():
            prologue()
        for l in layers:
            kind = l % 3
            if kind in mixers:
                with fw.scope():
                    mixers[kind](l, l // 3)
            if do_ffn:
                with fw.scope():
                    w1b = fw.sb("w1b", [128, KC * DFF], BF16)
                    w2b = fw.sb("w2b", [128, 32 * D], BF16)
                    with fw.scope():
                        alloc_stg()
                        load_w(w1b, lambda c0, n: w1b[:].rearrange("p (k f) -> p k f", k=KC)[:, :, c0:c0 + n], w1_d[l], DFF)
                        for kg in range(4):
                            load_w(w2b, lambda c0, n, kg=kg: w2b[:].rearrange("p (k f) -> p k f", k=32)[:, kg * 8:(kg + 1) * 8, c0:c0 + n],
                                   w2_d[l][kg * 1024:(kg + 1) * 1024, :], D)
                    hT = fw.sb("hT_ffn", [128, KC * NCH], BF16)
                    aT = fw.sb("aT_ffn", [128, 32 * NCH], BF16)
                    rtmp = [fw.sb("rtmp%d" % i, [128, NCH]) for i in range(2)]
                    ffn_pass(l, w1b, w2b, hT, aT, rtmp)
        with fw.scope():
            xin = [fw.sb("oin%d" % i, [128, KC * 128]) for i in range(2)]
            xo = [fw.sb("oo%d" % i, [128, D]) for i in range(2)]
            OUT = [[fw.view(None) for j in range(NT)] for b in range(NSEQ)]
            for b in range(NSEQ):
                for j in range(2, NT):
                    t = xin[j % 2]
                    o = xo[j % 2]
                    fw.dma(t[:].rearrange("p (c t) -> p c t", c=KC), xt_ap(b, j * 128, (j + 1) * 128), reads=[XT[b][j]], writes=[t])
                    for half in range(2):
                        bk = banks[(j * 2 + half) % 4]
                        for c in range(4):
                            cc = half * 4 + c
                            fw.tr(bk, bk[:, c * 128:(c + 1) * 128], t, t[:, cc * 128:(cc + 1) * 128], cst, identf)
                        fw.evac(o, o[:, half * 512:(half + 1) * 512], bk, bk[:])
                    fw.dma(out_d[b, (j - 2) * 128:(j - 1) * 128, :], o[:], reads=[o], writes=[OUT[b][j]])
            fw.finish([OUT[b][j] for b in range(NSEQ) for j in range(2, NT)] + DUMPS, "sp")
        fw.barrier()
        print("n_inst", fw.n_inst)
    return nc


def _consts():
    i = np.arange(128)
    ident = np.eye(128, dtype=np.float32)
    ones = np.ones((128, 128), np.float32)
    U_f = (i[:, None] <= i[None, :]).astype(np.float32)
    U_b = U_f.T.copy()
    Ms_f = (i[None, :] < i[:, None]).astype(np.float32)
    Ms_b = Ms_f.T.copy()
    b32 = (i[:, None] // 32 == i[None, :] // 32).astype(np.float32)
    b64 = (i[:, None] // 64 == i[None, :] // 64).astype(np.float32)
    z = np.zeros((128, 128), np.float32)
    return np.ascontiguousarray(np.concatenate([ident, ones, U_f, U_b, Ms_f, Ms_b, -b32, b64 - b32, 1.0 - b64, z], axis=1))


def _rope_tables():
    t = np.arange(NLAT)
    r = (t // 64).astype(np.float64)
    c = (t % 64).astype(np.float64)
    half = 32
    inv = 10000.0 ** (-np.arange(0, half, 2, dtype=np.float64) / half)
    cos = np.zeros((128, NLAT), np.float32)
    sin = np.zeros((128, NLAT), np.float32)
    for p in range(128):
        d = p % 64
        axis, hf, f = d // 32, (d % 32) // 16, d % 16
        ang = (r if axis == 0 else c) * inv[f]
        cos[p] = np.cos(ang)
        sin[p] = np.sin(ang) * (-1.0 if hf == 0 else 1.0)
    return np.ascontiguousarray(np.concatenate([cos, sin], axis=1))


def _layout_inputs(inputs, core):
    f = lambda a: np.ascontiguousarray(np.asarray(a, dtype=np.float32))
    b0 = core * NSEQ
    c = np.asarray(inputs["c"], np.float32)
    cvec = np.stack([c[b0], c[b0 + 1], np.asarray(inputs["c_ctx"], np.float32)], 0)
    cT = cvec.reshape(3, KC, 128).transpose(2, 1, 0).reshape(128, KC * 3)
    b_modT = np.asarray(inputs["b_mod"], np.float32).reshape(DEPTH, 48, 128).transpose(2, 0, 1).reshape(128, DEPTH * 48)
    ln_gT = np.asarray(inputs["ln_g"], np.float32).reshape(DEPTH, 2, KC, 128).transpose(3, 0, 1, 2).reshape(128, -1)
    ln_bT = np.asarray(inputs["ln_b"], np.float32).reshape(DEPTH, 2, KC, 128).transpose(3, 0, 1, 2).reshape(128, -1)
    conv_aT = np.asarray(inputs["conv_a"], np.float32).reshape(2, 5, 24, 128).transpose(3, 0, 1, 2).reshape(128, -1)
    wb = np.asarray(inputs["w_in_b"], np.float32)
    d = np.arange(64)
    partner = (d // 32) * 32 + (1 - (d % 32) // 16) * 16 + d % 16
    cols = (np.arange(2048) // 64) * 64 + partner[np.arange(2048) % 64]
    w_in_bp = wb[:, :, cols]
    return {
        "x": f(inputs["x"][b0:b0 + NSEQ]), "ctx": f(inputs["ctx"][b0:b0 + NSEQ]), "cT": f(cT),
        "w_mod": f(inputs["w_mod"]), "b_modT": f(b_modT), "ln_gT": f(ln_gT), "ln_bT": f(ln_bT),
        "w_in_a": f(inputs["w_in_a"]), "conv_aT": f(conv_aT),
        "a_log_a": f(np.asarray(inputs["a_log_a"]).reshape(2, 16)), "dt_bias_a": f(np.asarray(inputs["dt_bias_a"]).reshape(2, 16)),
        "norm_a": f(inputs["norm_a"]), "w_out_a": f(inputs["w_out_a"]),
        "w_in_b": f(wb), "w_in_bp": f(w_in_bp), "lam_b": f(inputs["lam_b"]), "subln_b": f(inputs["subln_b"]),
        "w_out_b": f(inputs["w_out_b"]), "w_in_c": f(inputs["w_in_c"]),
        "gate_bias_c": f(np.asarray(inputs["gate_bias_c"]).reshape(1, 16)), "norm_c": f(inputs["norm_c"]),
        "w_out_c": f(inputs["w_out_c"]), "w1": f(inputs["w1"]), "w2": f(inputs["w2"]),
        "cst": _consts(), "rope": _rope_tables(),
    }


def kernel(**inputs):
    nc = build()
    shared = None
    in_maps = []
    for core in range(NCORES):
        m = _layout_inputs(inputs, core)
        if shared is None:
            shared = m
        else:
            for k in m:
                if k not in ("x", "ctx", "cT"):
                    m[k] = shared[k]
        in_maps.append(m)
    res = run_bass_kernel_spmd(nc, in_maps, core_ids=list(range(NCORES)))
    out = np.concatenate([np.asarray(r["out"]) for r in res.results], axis=0)
    return out.astype(np.float32)
```

```python
import math
import numpy as np
import concourse.bass as bass
import concourse.mybir as mybir
from concourse.bass_utils import run_bass_kernel_spmd
from contextlib import ExitStack

F32 = mybir.dt.float32
BF16 = mybir.dt.bfloat16
ALU = mybir.AluOpType
AF = mybir.ActivationFunctionType

NCORES = 8
NSEQ = 2
NCTX = 256
NLAT = 2048
NTOK = NCTX + NLAT
NT = NTOK // 128
D = 1024
KC = 8
DFF = 4096
DEPTH = 4
ALPHA = (2.0 * DEPTH) ** 0.25
LN_EPS_S = 1e-5 / (ALPHA * ALPHA)
EPS = 1e-6
SELF_WINDOW = 6
NO_SELF = ("pe", "pool", "sp")


class Buf:
    __slots__ = ("t", "writers", "readers", "name", "excl")

    def __init__(self, t, name="", excl=False):
        self.t = t
        self.writers = {}
        self.readers = {}
        self.name = name
        self.excl = excl

    def __getitem__(self, idx):
        return self.t[idx]


class SlotView:
    def __init__(self, bank, ap):
        self.bank = bank
        self.t = ap
        self.name = "slot"
        self.excl = True

    writers = property(lambda self: self.bank.writers, lambda self, v: setattr(self.bank, "writers", v))
    readers = property(lambda self: self.bank.readers, lambda self, v: setattr(self.bank, "readers", v))


class FW:
    def __init__(self, nc, stack):
        self.nc = nc
        self.stack = stack
        self.engs = {"pe": nc.tensor, "act": nc.scalar, "dve": nc.vector, "pool": nc.gpsimd, "sp": nc.sync}
        self.sem = {}
        self.cnt = {}
        for e in ("pe", "act", "dve", "pool"):
            self.sem[e] = stack.enter_context(nc.semaphore("s_" + e))
            self.cnt[e] = 0
        self.dma_slots = {"sp": [], "pool": []}
        self.dma_next = {"sp": 0, "pool": 0}
        for q, n in (("sp", 32), ("pool", 16)):
            for i in range(n):
                s = stack.enter_context(nc.semaphore("d_%s%d" % (q, i)))
                key = "dma_%s%d" % (q, i)
                self.sem[key] = s
                self.cnt[key] = 0
                self.dma_slots[q].append(key)
        self.seen = {e: {} for e in ("pe", "act", "dve", "pool", "sp")}
        self.n_inst = 0
        self.rr = 0

    def sb(self, name, shape, dtype=F32):
        self.uid = getattr(self, "uid", 0) + 1
        t = self.stack.enter_context(self.nc.sbuf_tensor("sb_%s_%d" % (name, self.uid), list(shape), dtype))
        return Buf(t, name)

    def ps(self, name, shape, dtype=F32):
        t = self.stack.enter_context(self.nc.psum_tensor("ps_" + name, list(shape), dtype))
        return Buf(t, name, excl=True)

    def view(self, ap, name=""):
        return Buf(ap, name)

    def barrier(self):
        deps = {k: v for k, v in self.cnt.items() if v > 0}
        for e in ("pe", "act", "dve", "pool", "sp"):
            self._wait(e, dict(deps))

    def scope(self):
        fw = self

        class _S:
            def __enter__(s2):
                s2.old = fw.stack
                s2.st = ExitStack()
                s2.st.__enter__()
                fw.stack = s2.st
                return s2

            def __exit__(s2, *a):
                fw.barrier()
                fw.stack = s2.old
                return s2.st.__exit__(*a)
        return _S()

    def _wait(self, ename, deps):
        eng = self.engs[ename]
        seen = self.seen[ename]
        for k, v in deps.items():
            if k == ename:
                if ename in NO_SELF or self.cnt[ename] - v >= SELF_WINDOW:
                    continue
            if seen.get(k, 0) >= v:
                continue
            eng.wait_ge(self.sem[k], v)
            seen[k] = v

    @staticmethod
    def _deps(reads, writes):
        deps = {}
        for b in reads:
            for k, v in b.writers.items():
                if deps.get(k, 0) < v:
                    deps[k] = v
            if b.excl:
                for k, v in b.readers.items():
                    if deps.get(k, 0) < v:
                        deps[k] = v
        for b in writes:
            for k, v in b.writers.items():
                if deps.get(k, 0) < v:
                    deps[k] = v
            for k, v in b.readers.items():
                if deps.get(k, 0) < v:
                    deps[k] = v
        return deps

    def op(self, ename, fn, reads=(), writes=()):
        deps = self._deps(reads, writes)
        self._wait(ename, deps)
        if ename == "pe" and getattr(self, "pe_drain", False) and self.cnt["pe"] > 0:
            self.engs["pe"].wait_ge(self.sem["pe"], self.cnt["pe"])
        ins = fn(self.engs[ename])
        self.cnt[ename] += 1
        c = self.cnt[ename]
        ins.then_inc(self.sem[ename], 1)
        self.n_inst += 1
        for b in writes:
            b.writers = {ename: c}
            b.readers = {}
        for b in reads:
            if b.readers.get(ename, 0) < c:
                b.readers[ename] = c
        return ins

    def dma(self, out_ap, in_ap, reads=(), writes=(), q="sp", **kw):
        deps = self._deps(reads, writes)
        slots = self.dma_slots[q]
        key = slots[self.dma_next[q] % len(slots)]
        self.dma_next[q] += 1
        if self.cnt[key] > 0:
            deps[key] = max(deps.get(key, 0), self.cnt[key])
        self._wait(q, deps)
        ins = self.engs[q].dma_start(out=out_ap, in_=in_ap, **kw)
        self.cnt[key] += 16
        c = self.cnt[key]
        ins.then_inc(self.sem[key], 16)
        self.n_inst += 1
        for b in writes:
            b.writers = {key: c}
            b.readers = {}
        for b in reads:
            if b.readers.get(key, 0) < c:
                b.readers[key] = c

    def finish(self, bufs, ename="sp"):
        deps = {}
        for b in bufs:
            for k, v in b.writers.items():
                if deps.get(k, 0) < v:
                    deps[k] = v
        self._wait(ename, deps)

    def _touch(self, out, out_ap):
        if False:
            j = self.touch_junk
            self.op("act", lambda e: e.copy(out=j[:, 0:1], in_=out_ap[:, 0:1]), reads=[out], writes=[j])
            self.pe_gate = self.cnt["act"]

    def mm(self, out, out_ap, lhsT, lhsT_ap, rhs, rhs_ap, start=True, stop=True):
        r = self.op("pe", lambda e: e.matmul(out_ap, lhsT=lhsT_ap, rhs=rhs_ap, start=start, stop=stop),
                    reads=[lhsT, rhs], writes=[out])
        self._touch(out, out_ap)
        return r

    def tr(self, out, out_ap, in_, in_ap, ident, ident_ap):
        r = self.op("pe", lambda e: e.transpose(out_ap, in_ap, ident_ap), reads=[in_, ident], writes=[out])
        self._touch(out, out_ap)
        return r

    def evac(self, out, out_ap, in_, in_ap):
        self.rr += 1
        if self.rr & 1:
            return self.op("act", lambda e: e.copy(out=out_ap, in_=in_ap), reads=[in_], writes=[out])
        return self.op("dve", lambda e: e.tensor_copy(out=out_ap, in_=in_ap), reads=[in_], writes=[out])


DEBUG = False
GDN_HEADS = 8
GDN_STAGE = 9
GDN_YIELDS = 999
GDN_NSTEPS = 18
KKDIRS = (0, 1)
BAR = False
PE_DRAIN = False


def build(layers=(0, 1, 2, 3), do_ffn=True):
    nc = bass.Bass("TRN2", target_bir_lowering=False)

    def din(name, shape):
        return nc.dram_tensor(name, list(shape), F32, kind="ExternalInput").ap()

    x_d = din("x", [NSEQ, NLAT, D])
    ctx_d = din("ctx", [NSEQ, NCTX, D])
    cT_d = din("cT", [128, KC * 3])
    w_mod_d = din("w_mod", [DEPTH, D, 6 * D])
    b_modT_d = din("b_modT", [128, DEPTH * 48])
    ln_gT_d = din("ln_gT", [128, DEPTH * 2 * KC])
    ln_bT_d = din("ln_bT", [128, DEPTH * 2 * KC])
    w_in_a_d = din("w_in_a", [2, D, 4128])
    conv_aT_d = din("conv_aT", [128, 2 * 5 * 24])
    alog_d = din("a_log_a", [2, 16])
    dtb_d = din("dt_bias_a", [2, 16])
    norm_a_d = din("norm_a", [2, 128])
    w_out_a_d = din("w_out_a", [2, D, D])
    w_in_b_d = din("w_in_b", [1, D, 3072])
    w_in_bp_d = din("w_in_bp", [1, D, 2048])
    lam_b_d = din("lam_b", [1, 4, 64])
    subln_b_d = din("subln_b", [1, 128])
    w_out_b_d = din("w_out_b", [1, D, D])
    w_in_c_d = din("w_in_c", [1, D, 3088])
    gbias_c_d = din("gate_bias_c", [1, 16])
    norm_c_d = din("norm_c", [1, D])
    w_out_c_d = din("w_out_c", [1, D, D])
    w1_d = din("w1", [DEPTH, D, DFF])
    w2_d = din("w2", [DEPTH, DFF, D])
    cst_d = din("cst", [128, 10 * 128])
    rope_d = din("rope", [128, 2 * NLAT])
    out_d = nc.dram_tensor("out", [NSEQ, NLAT, D], F32, kind="ExternalOutput").ap()
    xt_d = nc.dram_tensor("xt_scratch", [NSEQ, KC, 128, NTOK], F32, kind="Internal").ap()

    with ExitStack() as st:
        fw = FW(nc, st)
        cst = fw.sb("cst", [128, 10 * 128])
        CST = fw.view(cst_d)
        fw.dma(cst[:], cst_d, reads=[CST], writes=[cst])
        identf = cst[:, 0:128]
        onesf = cst[:, 128:256]
        U = [cst[:, 256:384], cst[:, 384:512]]
        Ms = [cst[:, 512:640], cst[:, 640:768]]
        Mi = [U[1], U[0]]
        NB32 = cst[:, 768:896]
        B6432 = cst[:, 896:1024]
        NB64c = cst[:, 1024:1152]
        identb = fw.sb("identb", [128, 128], BF16)
        fw.op("dve", lambda e: e.tensor_copy(out=identb[:], in_=identf), reads=[cst], writes=[identb])
        fw.touch_junk = fw.sb("touch_junk", [128, 2])
        onesb = fw.sb("onesb", [128, 128], BF16)
        fw.op("dve", lambda e: e.tensor_copy(out=onesb[:], in_=onesf), reads=[cst], writes=[onesb])

        DUMPS = []

        dbg_stage = [None]

        def dump(name, buf, ap, n):
            if not DEBUG:
                return
            dt = nc.dram_tensor("dbg_" + name, [128, n], F32, kind="ExternalOutput").ap()
            tmp = dbg_stage[0]
            for c0 in range(0, n, 512):
                m = min(512, n - c0)
                fw.op("dve", lambda e: e.tensor_copy(out=tmp[:, 0:m], in_=ap[:, c0:c0 + m]), reads=[buf], writes=[tmp])
                v = fw.view(None)
                fw.dma(dt[:, c0:c0 + m], tmp[:, 0:m], reads=[tmp], writes=[v])
                DUMPS.append(v)

        if DEBUG:
            dbg_stage[0] = fw.sb("dbgs", [128, 512])
        banks = [fw.ps("bank%d" % i, [128, 512]) for i in range(7)]
        pbf = fw.ps("pbf", [128, 1024], BF16)

        XT = [[fw.view(None, "xt%d_%d" % (b, j)) for j in range(NT)] for b in range(NSEQ)]

        def xt_ap(b, t0, t1):
            return xt_d[b, :, :, t0:t1].rearrange("c p t -> p c t")

        def xt_bufs(b, t0, t1):
            return XT[b][t0 // 128:(t1 + 127) // 128]

        b_modT = fw.sb("b_modT", [128, DEPTH * 48])
        ln_gT = fw.sb("ln_gT", [128, DEPTH * 2 * KC])
        ln_bT = fw.sb("ln_bT", [128, DEPTH * 2 * KC])
        cT = fw.sb("cT", [128, KC * 3])
        for dst, src in ((b_modT, b_modT_d), (ln_gT, ln_gT_d), (ln_bT, ln_bT_d), (cT, cT_d)):
            fw.dma(dst[:], src, reads=[fw.view(src)], writes=[dst])
        fw.op("act", lambda e: e.activation(out=cT[:], in_=cT[:], func=AF.Silu), reads=[cT], writes=[cT])
        MOD = fw.sb("MOD", [128, DEPTH * 48 * 3])

        def mod_col(l, which, c, col):
            o = ((l * 48) + which * 8 + c) * 3 + col
            return MOD[:, o:o + 1]

        stg = []

        def alloc_stg(blk=512):
            stg[:] = [fw.sb("stg%d_%d" % (i, fw.n_inst), [128, KC * blk]) for i in range(2)]
        stg_i = [0]

        def load_w(dst, dst_ap_fn, src2d, ncols, kc=KC, col0=0, blk=512):
            for c0 in range(0, ncols, blk):
                n = min(blk, ncols - c0)
                s = stg[stg_i[0] % 2]
                stg_i[0] += 1
                sv = s[:, 0:kc * n].rearrange("p (k n) -> p k n", k=kc)
                src = src2d[:, col0 + c0:col0 + c0 + n].rearrange("(k p) n -> p k n", p=128)
                fw.dma(sv, src, reads=[], writes=[s])
                fw.op("pool", lambda e: e.tensor_copy(out=dst_ap_fn(c0, n), in_=sv), reads=[s], writes=[dst])

        def prologue():
            xin = [fw.sb("xin%d" % i, [128, D]) for i in range(2)]
            xo = [fw.sb("xo%d" % i, [128, D]) for i in range(2)]
            for b in range(NSEQ):
                for j in range(NT):
                    t = xin[j % 2]
                    o = xo[j % 2]
                    src = ctx_d[b, j * 128:(j + 1) * 128, :] if j < 2 else x_d[b, (j - 2) * 128:(j - 1) * 128, :]
                    fw.dma(t[:], src, reads=[], writes=[t])
                    for half in range(2):
                        bk = banks[(j * 2 + half) % 4]
                        for c in range(4):
                            cc = half * 4 + c
                            fw.tr(bk, bk[:, c * 128:(c + 1) * 128], t, t[:, cc * 128:(cc + 1) * 128], cst, identf)
                        fw.evac(o, o[:, half * 512:(half + 1) * 512], bk, bk[:])
                    fw.dma(xt_ap(b, j * 128, (j + 1) * 128), o[:].rearrange("p (c t) -> p c t", c=KC),
                           reads=[o], writes=[XT[b][j]])

            wmb = [fw.sb("wmb%d" % i, [128, KC * 512]) for i in range(2)]
            for l in layers:
                for fb in range(12):
                    wb = wmb[fb % 2]
                    src = w_mod_d[l][:, fb * 512:(fb + 1) * 512].rearrange("(k p) n -> p k n", p=128)
                    fw.dma(wb[:].rearrange("p (k n) -> p k n", k=KC), src, reads=[], writes=[wb])
                    bk = banks[fb % 2]
                    for f4 in range(4):
                        for k in range(KC):
                            fw.mm(bk, bk[:, f4 * 3:f4 * 3 + 3], wb, wb[:, k * 512 + f4 * 128:k * 512 + f4 * 128 + 128],
                                  cT, cT[:, k * 3:k * 3 + 3], start=(k == 0), stop=(k == KC - 1))
                    for f4 in range(4):
                        ch = fb * 4 + f4
                        o = (l * 48 + ch) * 3
                        fw.op("dve", lambda e: e.tensor_scalar(out=MOD[:, o:o + 3], in0=bk[:, f4 * 3:f4 * 3 + 3],
                                                               scalar1=b_modT[:, l * 48 + ch:l * 48 + ch + 1], scalar2=None,
                                                               op0=ALU.add), reads=[bk, b_modT], writes=[MOD])
                for which in (1, 4):
                    o = (l * 48 + which * 8) * 3
                    fw.op("dve", lambda e: e.tensor_scalar_add(out=MOD[:, o:o + 24], in0=MOD[:, o:o + 24], scalar1=1.0),
                          reads=[MOD], writes=[MOD])
                for which in (2, 5):
                    o = (l * 48 + which * 8) * 3
                    fw.op("dve", lambda e: e.tensor_scalar_mul(out=MOD[:, o:o + 24], in0=MOD[:, o:o + 24], scalar1=1.0 / ALPHA),
                          reads=[MOD], writes=[MOD])


        NCH = 256
        zt = fw.sb("zt", [128, KC * NCH])

        st_mean = fw.sb("st_mean", [128, NCH])
        st_var = fw.sb("st_var", [128, NCH])
        st_tmp = fw.sb("st_tmp", [128, NCH])
        xres = fw.sb("xres", [128, KC * NCH])
        sq = xres
        xnew = xres

        def resid_ln(l, sub, b, t0, n, col, y_fn):
            gw = 2 if sub == 0 else 5
            for o in range(KC):
                pb, pap = y_fn(o)
                fw.op("dve", lambda e: e.scalar_tensor_tensor(out=zt[:, o * NCH:o * NCH + n], in0=pap,
                                                              scalar=mod_col(l, gw, o, col),
                                                              in1=xres[:, o * NCH:o * NCH + n],
                                                              op0=ALU.mult, op1=ALU.add),
                      reads=[pb, MOD, xres], writes=[zt])
            fw.op("act", lambda e: e.activation(out=sq[:], in_=zt[:], func=AF.Square), reads=[zt], writes=[sq])
            bs, bq = banks[5], banks[6]
            for o in range(KC):
                fw.mm(bs, bs[:, 0:n], cst, onesf, zt, zt[:, o * NCH:o * NCH + n], start=(o == 0), stop=(o == KC - 1))
            for o in range(KC):
                fw.mm(bq, bq[:, 0:n], cst, onesf, sq, sq[:, o * NCH:o * NCH + n], start=(o == 0), stop=(o == KC - 1))
            fw.op("act", lambda e: e.mul(out=st_mean[:, 0:n], in_=bs[:, 0:n], mul=1.0 / D), reads=[bs], writes=[st_mean])
            fw.op("dve", lambda e: e.tensor_tensor(out=st_tmp[:, 0:n], in0=st_mean[:, 0:n], in1=st_mean[:, 0:n], op=ALU.mult),
                  reads=[st_mean], writes=[st_tmp])
            fw.op("dve", lambda e: e.scalar_tensor_tensor(out=st_var[:, 0:n], in0=bq[:, 0:n], scalar=1.0 / D,
                                                          in1=st_tmp[:, 0:n], op0=ALU.mult, op1=ALU.subtract),
                  reads=[bq, st_tmp], writes=[st_var])
            fw.op("dve", lambda e: e.tensor_scalar(out=st_var[:, 0:n], in0=st_var[:, 0:n], scalar1=0.0, scalar2=LN_EPS_S,
                                                   op0=ALU.max, op1=ALU.add), reads=[st_var], writes=[st_var])
            fw.op("act", lambda e: e.activation(out=st_var[:, 0:n], in_=st_var[:, 0:n], func=AF.Sqrt), reads=[st_var], writes=[st_var])
            fw.op("dve", lambda e: e.reciprocal(out=st_var[:, 0:n], in_=st_var[:, 0:n]), reads=[st_var], writes=[st_var])
            gi = (l * 2 + sub) * KC
            for o in range(KC):
                eng = "dve" if o % 2 == 0 else "pool"
                fw.op(eng, lambda e: e.tensor_tensor(out=zt[:, o * NCH:o * NCH + n], in0=zt[:, o * NCH:o * NCH + n],
                                                     in1=st_mean[:, 0:n], op=ALU.subtract), reads=[zt, st_mean], writes=[zt])
                fw.op(eng, lambda e: e.tensor_tensor(out=zt[:, o * NCH:o * NCH + n], in0=zt[:, o * NCH:o * NCH + n],
                                                     in1=st_var[:, 0:n], op=ALU.mult), reads=[zt, st_var], writes=[zt])
                fw.op("dve", lambda e: e.tensor_scalar(out=xnew[:, o * NCH:o * NCH + n], in0=zt[:, o * NCH:o * NCH + n],
                                                       scalar1=ln_gT[:, gi + o:gi + o + 1], scalar2=ln_bT[:, gi + o:gi + o + 1],
                                                       op0=ALU.mult, op1=ALU.add), reads=[zt, ln_gT, ln_bT], writes=[xnew])
            fw.dma(xt_ap(b, t0, t0 + n), xnew[:].rearrange("p (c t) -> p c t", c=KC)[:, :, 0:n],
                   reads=[xnew], writes=xt_bufs(b, t0, t0 + n))

        def load_xres(b, t0, n):
            fw.dma(xres[:].rearrange("p (c t) -> p c t", c=KC)[:, :, 0:n], xt_ap(b, t0, t0 + n),
                   reads=xt_bufs(b, t0, t0 + n), writes=[xres])

        def ffn_pass(l, w1b, w2b, hT, aT, rtmp):
            for b in range(NSEQ):
                for ci in range(NTOK // NCH):
                    if ci == 0 and l == DEPTH - 1:
                        continue
                    t0 = ci * NCH
                    n = NCH
                    col = 2 if ci == 0 else b
                    load_xres(b, t0, n)
                    for c in range(KC):
                        fw.op("dve" if c % 2 else "pool",
                              lambda e: e.tensor_scalar(out=hT[:, c * NCH:c * NCH + n], in0=xres[:, c * NCH:c * NCH + n],
                                                        scalar1=mod_col(l, 4, c, col), scalar2=mod_col(l, 3, c, col),
                                                        op0=ALU.mult, op1=ALU.add), reads=[xres, MOD], writes=[hT])
                    for f in range(32):
                        bk = banks[f % 4]
                        for k in range(KC):
                            fw.mm(bk, bk[:, 0:n], w1b, w1b[:, k * DFF + f * 128:k * DFF + f * 128 + 128],
                                  hT, hT[:, k * NCH:k * NCH + n], start=(k == 0), stop=(k == KC - 1))
                        r = rtmp[f % 2]
                        fw.op("act", lambda e: e.activation(out=r[:, 0:n], in_=bk[:, 0:n], func=AF.Relu), reads=[bk], writes=[r])
                        fw.op("dve" if f % 2 else "pool",
                              lambda e: e.tensor_tensor(out=aT[:, f * NCH:f * NCH + n], in0=r[:, 0:n], in1=r[:, 0:n], op=ALU.mult),
                              reads=[r], writes=[aT])

                    def y_fn(o):
                        bk = banks[4]
                        for f in range(32):
                            fw.mm(bk, bk[:, 0:n], w2b, w2b[:, f * D + o * 128:f * D + o * 128 + 128],
                                  aT, aT[:, f * NCH:f * NCH + n], start=(f == 0), stop=(f == 31))
                        return bk, bk[:, 0:n]
                    resid_ln(l, 1, b, t0, n, col, y_fn)

        def mixer_epilogue(l, b, Y, woutb, YT):
            for ci in range(NTOK // NCH):
                if ci == 0 and l == DEPTH - 1:
                    continue
                t0 = ci * NCH
                n = NCH
                col = 2 if ci == 0 else b
                load_xres(b, t0, n)
                for jj in range(2):
                    j = ci * 2 + jj
                    for c in range(KC):
                        fw.tr(pbf, pbf[:, c * 128:c * 128 + 128], Y, Y[:, j * D + c * 128:j * D + c * 128 + 128], identb, identb[:])
                    for c in range(KC):
                        fw.evac(YT, YT[:, c * NCH + jj * 128:c * NCH + jj * 128 + 128], pbf, pbf[:, c * 128:c * 128 + 128])

                def y_fn(o):
                    bk = banks[4]
                    for k in range(KC):
                        fw.mm(bk, bk[:, 0:n], woutb, woutb[:, k * D + o * 128:k * D + o * 128 + 128],
                              YT, YT[:, k * NCH:k * NCH + n], start=(k == 0), stop=(k == KC - 1))
                    return bk, bk[:, 0:n]
                resid_ln(l, 0, b, t0, n, col, y_fn)


        def mixer_attn(l, jj):
            lam_init = 0.8 - 0.6 * math.exp(-0.3 * l)
            woutb = fw.sb("woutb", [128, KC * D], BF16)
            YT = fw.sb("YT", [128, KC * NCH], BF16)
            hT = fw.sb("hT_seq", [128, KC * NTOK], BF16)
            Y = fw.sb("Y_seq", [128, NT * D], BF16)
            wh = [fw.sb("wh%d" % i, [128, KC * 128], BF16) for i in range(5)]
            KT = fw.sb("KT", [128, NTOK], BF16)
            QT = fw.sb("QT", [128, NTOK], BF16)
            Va = fw.sb("Va", [128, NT * 130], BF16)
            PT = [fw.sb("PT%d" % i, [128, 512], BF16) for i in range(3)]
            O1 = fw.sb("O1", [128, 4 * 128])
            ot = fw.sb("ot", [128, 128])
            rr_ = fw.sb("rr_", [128, 8])
            t1 = fw.sb("t1", [128, 512])
            t2 = fw.sb("t2", [128, 512])
            rope = fw.sb("rope", [128, 2 * NLAT])
            subl = fw.sb("subl", [128, 128])
            lamt = fw.sb("lamt", [1, 256])
            lamp = fw.sb("lamp", [1, 128])
            lams = fw.sb("lams", [1, 4])
            neglam = fw.sb("neglam", [128, 1])
            junk = fw.sb("junk", [128, 128])
            alloc_stg()
            fw.dma(rope[:], rope_d, reads=[], writes=[rope])
            fw.dma(subl[:], subln_b_d[jj:jj + 1, :].partition_broadcast(128), reads=[], writes=[subl])
            fw.op("dve", lambda e: e.tensor_scalar_mul(out=subl[:], in0=subl[:], scalar1=1.0 - lam_init), reads=[subl], writes=[subl])
            fw.dma(lamt[:], lam_b_d[jj:jj + 1].rearrange("o a b -> o (a b)"), reads=[], writes=[lamt])
            fw.op("dve", lambda e: e.tensor_tensor(out=lamp[:].rearrange("o (a b) -> o a b", a=2), in0=lamt[:].rearrange("o (a t b) -> o a t b", a=2, t=2)[:, :, 0, :],
                                                   in1=lamt[:].rearrange("o (a t b) -> o a t b", a=2, t=2)[:, :, 1, :], op=ALU.mult), reads=[lamt], writes=[lamp])
            for a in range(2):
                fw.op("dve", lambda e: e.reduce_sum(out=lams[:, a:a + 1], in_=lamp[:, a * 64:(a + 1) * 64], axis=mybir.AxisListType.X), reads=[lamp], writes=[lams])
            fw.op("act", lambda e: e.activation(out=lams[:, 0:2], in_=lams[:, 0:2], func=AF.Exp), reads=[lams], writes=[lams])
            fw.op("dve", lambda e: e.scalar_tensor_tensor(out=lams[:, 2:3], in0=lams[:, 1:2], scalar=-lam_init, in1=lams[:, 0:1], op0=ALU.add, op1=ALU.subtract),
                  reads=[lams], writes=[lams])
            bk0 = banks[0]
            fw.mm(bk0, bk0[:, 0:1], cst, onesf[0:1, :], lams, lams[:, 2:3])
            fw.evac(neglam, neglam[:], bk0, bk0[:, 0:1])
            load_w(woutb, lambda c0, n: woutb[:].rearrange("p (k f) -> p k f", k=KC)[:, :, c0:c0 + n], w_out_b_d[jj], D)
            fw.op("pool", lambda e: e.memset(Va[:], 1.0), reads=[], writes=[Va])
            cosT = rope[:, 0:NLAT]
            sinT = rope[:, NLAT:2 * NLAT]
            chunks = [(0, 256)] + [(256 + i * 512, 512) for i in range(4)]
            for b in range(NSEQ):
                for ci in range(NTOK // NCH):
                    t0 = ci * NCH
                    col = 2 if ci == 0 else b
                    load_xres(b, t0, NCH)
                    for c in range(KC):
                        fw.op("dve" if c % 2 else "pool",
                              lambda e: e.tensor_scalar(out=hT[:, c * NTOK + t0:c * NTOK + t0 + NCH], in0=xres[:, c * NCH:(c + 1) * NCH],
                                                        scalar1=mod_col(l, 1, c, col), scalar2=mod_col(l, 0, c, col),
                                                        op0=ALU.mult, op1=ALU.add), reads=[xres, MOD], writes=[hT])
                for h in range(8):
                    srcs = [(w_in_b_d[jj], h * 128), (w_in_b_d[jj], 1024 + h * 128), (w_in_b_d[jj], 2048 + h * 128),
                            (w_in_bp_d[jj], h * 128), (w_in_bp_d[jj], 1024 + h * 128)]
                    for wi, (src, c0) in enumerate(srcs):
                        load_w(wh[wi], lambda cc, n, wi=wi: wh[wi][:].rearrange("p (k f) -> p k f", k=KC)[:, :, cc:cc + n], src, 128, col0=c0, blk=128)
                    wq, wk, wv, wqp, wkp = wh
                    for (dst, wa, wp) in ((KT, wk, wkp), (QT, wq, wqp)):
                        for (t0, n) in chunks:
                            pa, pb = banks[0], banks[1]
                            for k in range(KC):
                                fw.mm(pa, pa[:, 0:n], wa, wa[:, k * 128:(k + 1) * 128], hT, hT[:, k * NTOK + t0:k * NTOK + t0 + n],
                                      start=(k == 0), stop=(k == KC - 1))
                            if t0 < NCTX:
                                fw.evac(dst, dst[:, t0:t0 + n], pa, pa[:, 0:n])
                                continue
                            for k in range(KC):
                                fw.mm(pb, pb[:, 0:n], wp, wp[:, k * 128:(k + 1) * 128], hT, hT[:, k * NTOK + t0:k * NTOK + t0 + n],
                                      start=(k == 0), stop=(k == KC - 1))
                            lt = t0 - NCTX
                            fw.op("dve", lambda e: e.tensor_tensor(out=t1[:, 0:n], in0=pa[:, 0:n], in1=cosT[:, lt:lt + n], op=ALU.mult), reads=[pa, rope], writes=[t1])
                            fw.op("dve", lambda e: e.tensor_tensor(out=t2[:, 0:n], in0=pb[:, 0:n], in1=sinT[:, lt:lt + n], op=ALU.mult), reads=[pb, rope], writes=[t2])
                            fw.op("pool", lambda e: e.tensor_tensor(out=dst[:, t0:t0 + n], in0=t1[:, 0:n], in1=t2[:, 0:n], op=ALU.add), reads=[t1, t2], writes=[dst])
                    for j in range(NT):
                        pv = banks[j % 2]
                        for k in range(KC):
                            fw.mm(pv, pv[:, 0:128], hT, hT[:, k * NTOK + j * 128:k * NTOK + (j + 1) * 128], wv, wv[:, k * 128:(k + 1) * 128],
                                  start=(k == 0), stop=(k == KC - 1))
                        fw.evac(Va, Va[:, j * 130:j * 130 + 128], pv, pv[:, 0:128])
                    pti = 0
                    for (q0, nq) in chunks:
                        ktiles = [0, 1] if q0 < NCTX else list(range(NT))
                        nqs = nq // 128
                        for t in range(2):
                            r0 = t * 64
                            for ki, kt in enumerate(ktiles):
                                ps_ = banks[ki % 2]
                                fw.mm(ps_, ps_[:, 0:nq], KT, KT[r0:r0 + 64, kt * 128:(kt + 1) * 128], QT, QT[r0:r0 + 64, q0:q0 + nq])
                                pt = PT[pti % 3]
                                pti += 1
                                fw.op("act", lambda e: e.activation(out=pt[:, 0:nq], in_=ps_[:, 0:nq], func=AF.Exp, scale=0.125), reads=[ps_], writes=[pt])
                                for qs in range(nqs):
                                    ob = banks[2 + qs]
                                    fw.mm(ob, ob[:, 0:129], pt, pt[:, qs * 128:(qs + 1) * 128], Va, Va[:, kt * 130:kt * 130 + 129],
                                          start=(ki == 0), stop=(ki == len(ktiles) - 1))
                            for qs in range(nqs):
                                ob = banks[2 + qs]
                                fw.op("dve", lambda e: e.reciprocal(out=rr_[:, qs:qs + 1], in_=ob[:, 128:129]), reads=[ob], writes=[rr_])
                                if t == 0:
                                    fw.op("dve", lambda e: e.tensor_scalar(out=O1[:, qs * 128:(qs + 1) * 128], in0=ob[:, 0:128], scalar1=rr_[:, qs:qs + 1],
                                                                           scalar2=None, op0=ALU.mult), reads=[ob, rr_], writes=[O1])
                                else:
                                    tile_j = (q0 + qs * 128) // 128
                                    fw.op("dve", lambda e: e.tensor_scalar(out=ot[:], in0=ob[:, 0:128], scalar1=rr_[:, qs:qs + 1],
                                                                           scalar2=neglam[:, 0:1], op0=ALU.mult, op1=ALU.mult), reads=[ob, rr_, neglam], writes=[ot])
                                    fw.op("dve", lambda e: e.tensor_tensor(out=ot[:], in0=ot[:], in1=O1[:, qs * 128:(qs + 1) * 128], op=ALU.add), reads=[ot, O1], writes=[ot])
                                    fw.op("act", lambda e: e.activation(out=junk[:], in_=ot[:], func=AF.Square, accum_out=rr_[:, 4 + qs:5 + qs]), reads=[ot], writes=[junk, rr_])
                                    fw.op("dve", lambda e: e.tensor_scalar(out=rr_[:, 4 + qs:5 + qs], in0=rr_[:, 4 + qs:5 + qs], scalar1=1.0 / 128, scalar2=EPS,
                                                                           op0=ALU.mult, op1=ALU.add), reads=[rr_], writes=[rr_])
                                    fw.op("act", lambda e: e.activation(out=rr_[:, 4 + qs:5 + qs], in_=rr_[:, 4 + qs:5 + qs], func=AF.Sqrt), reads=[rr_], writes=[rr_])
                                    fw.op("dve", lambda e: e.reciprocal(out=rr_[:, 4 + qs:5 + qs], in_=rr_[:, 4 + qs:5 + qs]), reads=[rr_], writes=[rr_])
                                    fw.op("dve", lambda e: e.scalar_tensor_tensor(out=Y[:, tile_j * D + h * 128:tile_j * D + (h + 1) * 128], in0=ot[:],
                                                                                  scalar=rr_[:, 4 + qs:5 + qs], in1=subl[:], op0=ALU.mult, op1=ALU.mult),
                                          reads=[ot, rr_, subl], writes=[Y])
                    if b == 0 and h == 0:
                        dump("KT", KT, KT[:], NTOK)
                        dump("QT", QT, QT[:], NTOK)
                        dump("Va", Va, Va[:], NT * 130)
                        for j_ in range(NT):
                            dump("Y%d" % j_, Y, Y[:, j_ * D:j_ * D + 128], 128)
                        dump("neglam", neglam, neglam[:], 1)
                        dump("hT", hT, hT[:, 0:NTOK], NTOK)
                mixer_epilogue(l, b, Y, woutb, YT)

        def mixer_mlstm(l, jj):
            woutb = fw.sb("woutb", [128, KC * D], BF16)
            YT = fw.sb("YT", [128, KC * NCH], BF16)
            hT = fw.sb("hT_seq", [128, KC * NTOK], BF16)
            Y = fw.sb("Y_seq", [128, NT * D], BF16)
            wq = fw.sb("wq", [128, KC * 128], BF16)
            wk = fw.sb("wk", [128, KC * 128], BF16)
            wv = fw.sb("wv", [128, KC * 256], BF16)
            wo = fw.sb("wo", [128, KC * 256], BF16)
            wg = fw.sb("wg", [128, KC * 16], BF16)
            KT = fw.sb("KT", [128, NTOK], BF16)
            QT = fw.sb("QT", [128, NTOK], BF16)
            Va = fw.sb("Va", [128, NT * 258], BF16)
            Pb = fw.sb("Pb", [128, NTOK])
            Hacc = fw.sb("Hacc", [128, NT * 256])
            Dt = [fw.sb("Dt%d" % i, [128, 512]) for i in range(2)]
            WT = [fw.sb("WT%d" % i, [128, 512], BF16) for i in range(2)]
            tmpd = fw.sb("tmpd", [128, 128])
            Gt = fw.sb("Gt", [128, NT * 16])
            LF = fw.sb("LF", [128, NT * 16])
            CW = fw.sb("CW", [128, NT * 8])
            TOT = fw.sb("TOT", [128, NT * 8])
            PC = fw.sb("PC", [128, NT * 8])
            AC = fw.sb("AC", [128, NT * 8])
            carry = fw.sb("carry", [128, 4])
            GB = fw.sb("GB", [128, 16])
            normc = fw.sb("normc", [128, D])
            MB = [fw.sb("MB%d" % i, [128, 128]) for i in range(2)]
            sm = fw.sb("sm", [128, 8])
            og = fw.sb("og", [128, 256])
            junk = fw.sb("junk", [128, 256])
            alloc_stg(128)
            fw.dma(GB[:], gbias_c_d[jj:jj + 1, :].partition_broadcast(128), reads=[], writes=[GB])
            fw.dma(normc[:], norm_c_d[jj:jj + 1, :].partition_broadcast(128), reads=[], writes=[normc])
            for dr in range(2):
                fw.op("dve", lambda e: e.tensor_scalar(out=MB[dr][:], in0=U[dr], scalar1=-1.0, scalar2=30000.0, op0=ALU.add, op1=ALU.mult),
                      reads=[cst], writes=[MB[dr]])
            load_w(woutb, lambda c0, n: woutb[:].rearrange("p (k f) -> p k f", k=KC)[:, :, c0:c0 + n], w_out_c_d[jj], D, blk=128)
            load_w(wg, lambda c0, n: wg[:].rearrange("p (k f) -> p k f", k=KC)[:, :, c0:c0 + n], w_in_c_d[jj], 16, col0=3072, blk=16)
            fw.op("pool", lambda e: e.memset(Va[:], 1.0), reads=[], writes=[Va])
            chunks = [(0, 256)] + [(256 + i * 512, 512) for i in range(4)]
            for b in range(NSEQ):
                for ci in range(NTOK // NCH):
                    t0 = ci * NCH
                    col = 2 if ci == 0 else b
                    load_xres(b, t0, NCH)
                    for c in range(KC):
                        fw.op("dve" if c % 2 else "pool",
                              lambda e: e.tensor_scalar(out=hT[:, c * NTOK + t0:c * NTOK + t0 + NCH], in0=xres[:, c * NCH:(c + 1) * NCH],
                                                        scalar1=mod_col(l, 1, c, col), scalar2=mod_col(l, 0, c, col),
                                                        op0=ALU.mult, op1=ALU.add), reads=[xres, MOD], writes=[hT])
                for j in range(NT):
                    pg = banks[j % 2]
                    for k in range(KC):
                        fw.mm(pg, pg[:, 0:16], hT, hT[:, k * NTOK + j * 128:k * NTOK + (j + 1) * 128], wg, wg[:, k * 16:(k + 1) * 16],
                              start=(k == 0), stop=(k == KC - 1))
                    fw.op("dve", lambda e: e.tensor_tensor(out=Gt[:, j * 16:(j + 1) * 16], in0=pg[:, 0:16], in1=GB[:], op=ALU.add), reads=[pg, GB], writes=[Gt])
                fw.op("act", lambda e: e.activation(out=LF[:], in_=Gt[:], func=AF.Exp, scale=-1.0), reads=[Gt], writes=[LF])
                fw.op("act", lambda e: e.activation(out=LF[:], in_=LF[:], func=AF.Ln, bias=1.0, scale=1.0), reads=[LF], writes=[LF])
                fw.op("dve", lambda e: e.tensor_scalar_mul(out=LF[:], in0=LF[:], scalar1=-1.0), reads=[LF], writes=[LF])
                for dr in range(2):
                    pc_, pt_ = banks[0], banks[1]
                    for j in range(NT):
                        fcol = j * 16 + (1 + 2 * dr) * 4
                        fw.mm(pc_, pc_[:, j * 4:(j + 1) * 4], cst, U[dr], LF, LF[:, fcol:fcol + 4])
                        fw.mm(pt_, pt_[:, j * 4:(j + 1) * 4], cst, onesf, LF, LF[:, fcol:fcol + 4])
                    cwv = CW[:].rearrange("p (j d h) -> p j d h", j=NT, d=2)[:, :, dr, :]
                    totv = TOT[:].rearrange("p (j d h) -> p j d h", j=NT, d=2)[:, :, dr, :]
                    fw.op("dve", lambda e: e.tensor_copy(out=cwv, in_=pc_[:, 0:NT * 4].rearrange("p (j h) -> p j h", j=NT)), reads=[pc_], writes=[CW])
                    fw.op("act", lambda e: e.copy(out=totv, in_=pt_[:, 0:NT * 4].rearrange("p (j h) -> p j h", j=NT)), reads=[pt_], writes=[TOT])
                    order = list(range(NT)) if dr == 0 else [1, 0] + list(range(NT - 1, 1, -1))
                    fw.op("dve", lambda e: e.memset(carry[:], 0.0), reads=[], writes=[carry])
                    for j in order:
                        o8 = j * 8 + dr * 4
                        fw.op("dve", lambda e: e.tensor_tensor(out=PC[:, o8:o8 + 4], in0=CW[:, o8:o8 + 4], in1=carry[:], op=ALU.add), reads=[CW, carry], writes=[PC])
                        fw.op("dve", lambda e: e.tensor_tensor(out=carry[:], in0=carry[:], in1=TOT[:, o8:o8 + 4], op=ALU.add), reads=[carry, TOT], writes=[carry])
                    liv = Gt[:].rearrange("p (j g h) -> p j g h", j=NT, g=4)[:, :, 2 * dr, :]
                    pcv = PC[:].rearrange("p (j d h) -> p j d h", j=NT, d=2)[:, :, dr, :]
                    acv = AC[:].rearrange("p (j d h) -> p j d h", j=NT, d=2)[:, :, dr, :]
                    fw.op("dve", lambda e: e.tensor_tensor(out=acv, in0=liv, in1=pcv, op=ALU.subtract), reads=[Gt, PC], writes=[AC])
                for h in range(4):
                    load_w(wq, lambda c0, n: wq[:].rearrange("p (k f) -> p k f", k=KC)[:, :, c0:c0 + n], w_in_c_d[jj], 128, col0=h * 128, blk=128)
                    load_w(wk, lambda c0, n: wk[:].rearrange("p (k f) -> p k f", k=KC)[:, :, c0:c0 + n], w_in_c_d[jj], 128, col0=512 + h * 128, blk=128)
                    load_w(wv, lambda c0, n: wv[:].rearrange("p (k f) -> p k f", k=KC)[:, :, c0:c0 + n], w_in_c_d[jj], 256, col0=1024 + h * 256, blk=128)
                    load_w(wo, lambda c0, n: wo[:].rearrange("p (k f) -> p k f", k=KC)[:, :, c0:c0 + n], w_in_c_d[jj], 256, col0=2048 + h * 256, blk=128)
                    for (dst, wa, sc) in ((KT, wk, 128.0 ** -0.5), (QT, wq, 1.0)):
                        for (t0, n) in chunks:
                            pa = banks[(t0 // 256) % 2]
                            for k in range(KC):
                                fw.mm(pa, pa[:, 0:n], wa, wa[:, k * 128:(k + 1) * 128], hT, hT[:, k * NTOK + t0:k * NTOK + t0 + n],
                                      start=(k == 0), stop=(k == KC - 1))
                            fw.op("act", lambda e: e.mul(out=dst[:, t0:t0 + n], in_=pa[:, 0:n], mul=sc), reads=[pa], writes=[dst])
                    for j in range(NT):
                        pv = banks[j % 2]
                        for k in range(KC):
                            fw.mm(pv, pv[:, 0:256], hT, hT[:, k * NTOK + j * 128:k * NTOK + (j + 1) * 128], wv, wv[:, k * 256:(k + 1) * 256],
                                  start=(k == 0), stop=(k == KC - 1))
                        fw.evac(Va, Va[:, j * 258:j * 258 + 256], pv, pv[:, 0:256])
                    for dr in range(2):
                        for j in range(NT):
                            pp = banks[j % 2]
                            c8 = j * 8 + dr * 4 + h
                            fw.mm(pp, pp[:, 0:128], PC, PC[:, c8:c8 + 1].to_broadcast([128, 128]), cst, identf)
                            fw.evac(Pb, Pb[:, j * 128:(j + 1) * 128], pp, pp[:, 0:128])

                        def vis(js, jt):
                            s_ctx, t_ctx = js < 2, jt < 2
                            if t_ctx and not s_ctx:
                                return 0
                            if s_ctx and not t_ctx:
                                return 1
                            if js == jt:
                                return 2
                            if dr == 0:
                                return 1 if js < jt else 0
                            return 1 if js > jt else 0
                        bi = 0
                        for (q0, nq) in chunks:
                            ttiles = list(range(q0 // 128, (q0 + nq) // 128))
                            slist = [js for js in range(NT) if any(vis(js, jt) for jt in ttiles)]
                            first = {}
                            last = {}
                            for js in slist:
                                for jt in ttiles:
                                    if vis(js, jt):
                                        first.setdefault(jt, js)
                                        last[jt] = js
                            for js in slist:
                                need = [jt for jt in ttiles if vis(js, jt)]
                                a0, a1 = need[0] * 128, (need[-1] + 1) * 128
                                n = a1 - a0
                                ps_ = banks[bi % 2]
                                dt_ = Dt[bi % 2]
                                wt_ = WT[bi % 2]
                                bi += 1
                                fw.mm(ps_, ps_[:, 0:n], KT, KT[:, js * 128:(js + 1) * 128], QT, QT[:, a0:a1])
                                acol = AC[:, js * 8 + dr * 4 + h:js * 8 + dr * 4 + h + 1]
                                full = [jt for jt in need if vis(js, jt) == 1]
                                diag = [jt for jt in need if vis(js, jt) == 2]
                                if full:
                                    f0, f1 = full[0] * 128, (full[-1] + 1) * 128
                                    fw.op("act", lambda e: e.activation(out=dt_[:, f0 - a0:f1 - a0], in_=Pb[:, f0:f1], func=AF.Exp, bias=acol, scale=1.0),
                                          reads=[Pb, AC], writes=[dt_])
                                if diag:
                                    d0 = diag[0] * 128
                                    fw.op("dve", lambda e: e.tensor_tensor(out=tmpd[:], in0=Pb[:, d0:d0 + 128], in1=MB[dr][:], op=ALU.add), reads=[Pb, MB[dr]], writes=[tmpd])
                                    fw.op("act", lambda e: e.activation(out=dt_[:, d0 - a0:d0 - a0 + 128], in_=tmpd[:], func=AF.Exp, bias=acol, scale=1.0),
                                          reads=[tmpd, AC], writes=[dt_])
                                fw.op("dve", lambda e: e.tensor_tensor(out=wt_[:, 0:n], in0=ps_[:, 0:n], in1=dt_[:, 0:n], op=ALU.mult), reads=[ps_, dt_], writes=[wt_])
                                for jt in need:
                                    ob = banks[2 + (jt - ttiles[0])]
                                    fw.mm(ob, ob[:, 0:257], wt_, wt_[:, jt * 128 - a0:jt * 128 - a0 + 128], Va, Va[:, js * 258:js * 258 + 257],
                                          start=(first[jt] == js), stop=(last[jt] == js))
                            for jt in ttiles:
                                ob = banks[2 + (jt - ttiles[0])]
                                fw.op("act", lambda e: e.activation(out=sm[:, 0:1], in_=ob[:, 256:257], func=AF.Abs), reads=[ob], writes=[sm])
                                fw.op("dve", lambda e: e.tensor_scalar_max(out=sm[:, 0:1], in0=sm[:, 0:1], scalar1=1.0), reads=[sm], writes=[sm])
                                fw.op("dve", lambda e: e.reciprocal(out=sm[:, 0:1], in_=sm[:, 0:1]), reads=[sm], writes=[sm])
                                hv = Hacc[:, jt * 256:(jt + 1) * 256]
                                if dr == 0:
                                    fw.op("dve", lambda e: e.tensor_scalar(out=hv, in0=ob[:, 0:256], scalar1=sm[:, 0:1], scalar2=None, op0=ALU.mult), reads=[ob, sm], writes=[Hacc])
                                else:
                                    fw.op("dve", lambda e: e.scalar_tensor_tensor(out=hv, in0=ob[:, 0:256], scalar=sm[:, 0:1], in1=hv, op0=ALU.mult, op1=ALU.add),
                                          reads=[ob, sm, Hacc], writes=[Hacc])
                    for j in range(NT):
                        po = banks[j % 2]
                        for k in range(KC):
                            fw.mm(po, po[:, 0:256], hT, hT[:, k * NTOK + j * 128:k * NTOK + (j + 1) * 128], wo, wo[:, k * 256:(k + 1) * 256],
                                  start=(k == 0), stop=(k == KC - 1))
                        fw.op("act", lambda e: e.activation(out=og[:], in_=po[:, 0:256], func=AF.Sigmoid), reads=[po], writes=[og])
                        hv = Hacc[:, j * 256:(j + 1) * 256]
                        fw.op("act", lambda e: e.activation(out=junk[:], in_=hv, func=AF.Square, accum_out=sm[:, 1:2]), reads=[Hacc], writes=[junk, sm])
                        fw.op("dve", lambda e: e.tensor_scalar(out=sm[:, 1:2], in0=sm[:, 1:2], scalar1=1.0 / 256, scalar2=EPS, op0=ALU.mult, op1=ALU.add), reads=[sm], writes=[sm])
                        fw.op("act", lambda e: e.activation(out=sm[:, 1:2], in_=sm[:, 1:2], func=AF.Sqrt), reads=[sm], writes=[sm])
                        fw.op("dve", lambda e: e.reciprocal(out=sm[:, 1:2], in_=sm[:, 1:2]), reads=[sm], writes=[sm])
                        fw.op("dve", lambda e: e.scalar_tensor_tensor(out=og[:], in0=og[:], scalar=sm[:, 1:2], in1=normc[:, h * 256:(h + 1) * 256], op0=ALU.mult, op1=ALU.mult),
                              reads=[og, sm, normc], writes=[og])
                        fw.op("dve", lambda e: e.tensor_tensor(out=Y[:, j * D + h * 256:j * D + (h + 1) * 256], in0=og[:], in1=hv, op=ALU.mult), reads=[og, Hacc], writes=[Y])
                mixer_epilogue(l, b, Y, woutb, YT)

        def mixer_gdn(l, jj):
            convw = fw.sb("convw", [128, 240])
            normg = fw.sb("normg", [128, 128])
            DTB = fw.sb("DTB", [128, 32])
            NEGA = fw.sb("NEGA", [128, 32])
            fw.dma(convw[:], conv_aT_d, reads=[], writes=[convw])
            fw.dma(normg[:], norm_a_d[jj:jj + 1, :].partition_broadcast(128), reads=[], writes=[normg])
            fw.op("dve", lambda e: e.memset(DTB[:], 0.0), reads=[], writes=[DTB])
            fw.op("dve", lambda e: e.memset(NEGA[:], 0.0), reads=[], writes=[NEGA])
            for dr in range(2):
                fw.dma(DTB[:, dr * 16:dr * 16 + 8], dtb_d[jj:jj + 1, dr * 8:dr * 8 + 8].partition_broadcast(128), reads=[], writes=[DTB])
                fw.dma(NEGA[:, dr * 16:dr * 16 + 8], alog_d[jj:jj + 1, dr * 8:dr * 8 + 8].partition_broadcast(128), reads=[], writes=[NEGA])
            fw.op("act", lambda e: e.activation(out=NEGA[:], in_=NEGA[:], func=AF.Exp), reads=[NEGA], writes=[NEGA])
            fw.op("dve", lambda e: e.tensor_scalar_mul(out=NEGA[:], in0=NEGA[:], scalar1=-1.0), reads=[NEGA], writes=[NEGA])
            chunks = [(0, 256)] + [(256 + i * 512, 512) for i in range(4)]
            slots = [[SlotView(banks[2 * dr + q % 2], banks[2 * dr + q % 2][:, (q // 2) * 128:(q // 2 + 1) * 128]) for q in range(8)] for dr in range(2)]
            si = [0, 0]
            uid = [0]

            def mmq(dr, lb, lap, rb, rap):
                sl = slots[dr][si[dr] % 8]
                si[dr] += 1
                fw.mm(sl, sl.t, lb, lap, rb, rap)
                return sl

            mmq.slots = slots
            mmq.si = si
            for b in range(NSEQ):
                with fw.scope():
                    hT = fw.sb("hT_seq", [128, KC * NTOK], BF16)
                    ZY = fw.sb("ZY_seq", [128, NT * D], BF16)
                    Gt = fw.sb("Gt", [128, NT * 32])
                    GL = fw.sb("GL", [128, NT * 32])
                    BT = fw.sb("BT", [128, NT * 32])
                    CC = fw.sb("CC", [128, NT * 16])
                    for ci in range(NTOK // NCH):
                        t0 = ci * NCH
                        col = 2 if ci == 0 else b
                        load_xres(b, t0, NCH)
                        for c in range(KC):
                            fw.op("dve" if c % 2 else "pool",
                                  lambda e: e.tensor_scalar(out=hT[:, c * NTOK + t0:c * NTOK + t0 + NCH], in0=xres[:, c * NCH:(c + 1) * NCH],
                                                            scalar1=mod_col(l, 1, c, col), scalar2=mod_col(l, 0, c, col),
                                                            op0=ALU.mult, op1=ALU.add), reads=[xres, MOD], writes=[hT])
                    with fw.scope():
                        alloc_stg(128)
                        wz = fw.sb("wz", [128, KC * 128], BF16)
                        wgt = fw.sb("wgt", [128, KC * 32], BF16)
                        for hb in range(8):
                            load_w(wz, lambda c0, n: wz[:].rearrange("p (k f) -> p k f", k=KC)[:, :, c0:c0 + n], w_in_a_d[jj], 128, col0=3072 + hb * 128, blk=128)
                            for j in range(NT):
                                pz = banks[4 + j % 3]
                                for k in range(KC):
                                    fw.mm(pz, pz[:, 0:128], hT, hT[:, k * NTOK + j * 128:k * NTOK + (j + 1) * 128], wz, wz[:, k * 128:(k + 1) * 128],
                                          start=(k == 0), stop=(k == KC - 1))
                                fw.op("act", lambda e: e.activation(out=ZY[:, j * D + hb * 128:j * D + (hb + 1) * 128], in_=pz[:, 0:128], func=AF.Silu), reads=[pz], writes=[ZY])
                        load_w(wgt, lambda c0, n: wgt[:].rearrange("p (k f) -> p k f", k=KC)[:, :, c0:c0 + n], w_in_a_d[jj], 32, col0=4096, blk=32)
                        for j in range(NT):
                            pg = banks[4 + j % 3]
                            for k in range(KC):
                                fw.mm(pg, pg[:, 0:32], hT, hT[:, k * NTOK + j * 128:k * NTOK + (j + 1) * 128], wgt, wgt[:, k * 32:(k + 1) * 32],
                                      start=(k == 0), stop=(k == KC - 1))
                            fw.evac(Gt, Gt[:, j * 32:(j + 1) * 32], pg, pg[:, 0:32])
                            fw.op("dve", lambda e: e.tensor_tensor(out=GL[:, j * 32:(j + 1) * 32], in0=Gt[:, j * 32:(j + 1) * 32], in1=DTB[:], op=ALU.add), reads=[Gt, DTB], writes=[GL])
                        fw.op("act", lambda e: e.activation(out=GL[:], in_=GL[:], func=AF.Exp), reads=[GL], writes=[GL])
                        fw.op("act", lambda e: e.activation(out=GL[:], in_=GL[:], func=AF.Ln, bias=1.0, scale=1.0), reads=[GL], writes=[GL])
                        for j in range(NT):
                            fw.op("dve", lambda e: e.tensor_tensor(out=GL[:, j * 32:(j + 1) * 32], in0=GL[:, j * 32:(j + 1) * 32], in1=NEGA[:], op=ALU.mult), reads=[GL, NEGA], writes=[GL])
                        fw.op("act", lambda e: e.activation(out=BT[:], in_=Gt[:], func=AF.Sigmoid), reads=[Gt], writes=[BT])
                        pc_ = banks[6]
                        for j in range(NT):
                            for dr in range(2):
                                fw.mm(pc_, pc_[:, j * 16 + dr * 8:j * 16 + dr * 8 + 8], cst, U[dr], GL, GL[:, j * 32 + dr * 16:j * 32 + dr * 16 + 8])
                        fw.evac(CC, CC[:], pc_, pc_[:, 0:NT * 16])
                    for h in range(GDN_HEADS):
                        with fw.scope():
                            gdn_head(l, jj, b, h, hT, ZY, GL, BT, CC, convw, normg, mmq, chunks)
                    with fw.scope():
                        alloc_stg(256)
                        woutb = fw.sb("woutb", [128, KC * D], BF16)
                        YT = fw.sb("YT", [128, KC * NCH], BF16)
                        load_w(woutb, lambda c0, n: woutb[:].rearrange("p (k f) -> p k f", k=KC)[:, :, c0:c0 + n], w_out_a_d[jj], D, blk=256)
                        mixer_epilogue(l, b, ZY, woutb, YT)

        def gdn_head(l, jj, b, h, hT, ZY, GL, BT, CC, convw, normg, mmq, chunks):
            if GDN_STAGE < 1:
                return
            alloc_stg(128)
            wq = fw.sb("wq", [128, KC * 128], BF16)
            wk = fw.sb("wk", [128, KC * 128], BF16)
            wv = fw.sb("wv", [128, KC * 128], BF16)
            Pc = fw.sb("Pc", [128, 260])
            Pl = fw.sb("Pl", [128, 2052])
            raw = fw.sb("raw", [128, 512])
            sqt = fw.sb("sqt", [128, 512])
            rs = fw.sb("rs", [128, 512])
            QT = fw.sb("QT", [128, NTOK], BF16)
            KT = fw.sb("KT", [128, NTOK], BF16)
            VT = fw.sb("VT", [128, NTOK], BF16)
            Ktok = fw.sb("Ktok", [128, NT * 128], BF16)
            Vtok = fw.sb("Vtok", [128, NT * 128], BF16)
            Oacc = fw.sb("Oacc", [128, NT * 128])
            S = [fw.sb("S%d" % d, [128, 128]) for d in range(2)]
            Sbf = [fw.sb("Sbf%d" % d, [128, 128], BF16) for d in range(2)]
            sm = fw.sb("smg", [128, 4])
            junk = fw.sb("junkg", [128, 128])
            yt = fw.sb("ytg", [128, 128])
            fw.op("pool", lambda e: e.memset(Pc[:], 0.0), reads=[], writes=[Pc])
            fw.op("pool", lambda e: e.memset(Pl[:], 0.0), reads=[], writes=[Pl])
            fw.op("pool", lambda e: e.memset(Oacc[:], 0.0), reads=[], writes=[Oacc])
            for d in range(2):
                fw.op("pool", lambda e: e.memset(S[d][:], 0.0), reads=[], writes=[S[d]])
                fw.op("pool", lambda e: e.memset(Sbf[d][:], 0.0), reads=[], writes=[Sbf[d]])
            for wi, (wt_, c0) in enumerate(((wq, h * 128), (wk, 1024 + h * 128), (wv, 2048 + h * 128))):
                load_w(wt_, lambda cc, n, wt_=wt_: wt_[:].rearrange("p (k f) -> p k f", k=KC)[:, :, cc:cc + n], w_in_a_d[jj], 128, col0=c0, blk=128)
            for kind, (wt_, dst) in enumerate(((wq, QT), (wk, KT), (wv, VT))):
                chn = kind * 8 + h
                for (t0, n) in chunks:
                    pa = banks[4 + (t0 // 256) % 3]
                    for k in range(KC):
                        fw.mm(pa, pa[:, 0:n], wt_, wt_[:, k * 128:(k + 1) * 128], hT, hT[:, k * NTOK + t0:k * NTOK + t0 + n],
                              start=(k == 0), stop=(k == KC - 1))
                    if t0 < NCTX:
                        fw.evac(Pc, Pc[:, 2:2 + n], pa, pa[:, 0:n])
                    else:
                        fw.evac(Pl, Pl[:, 2 + t0 - NCTX:2 + t0 - NCTX + n], pa, pa[:, 0:n])
                for (t0, n) in chunks:
                    src, off = (Pc, t0) if t0 < NCTX else (Pl, t0 - NCTX)
                    cw = lambda k: convw[:, (jj * 5 + k) * 24 + chn:(jj * 5 + k) * 24 + chn + 1]
                    fw.op("dve", lambda e: e.tensor_scalar(out=raw[:, 0:n], in0=src[:, off:off + n], scalar1=cw(0), scalar2=None, op0=ALU.mult),
                          reads=[src, convw], writes=[raw])
                    for k in range(1, 5):
                        fw.op("dve", lambda e: e.scalar_tensor_tensor(out=raw[:, 0:n], in0=src[:, off + k:off + k + n], scalar=cw(k), in1=raw[:, 0:n],
                                                                      op0=ALU.mult, op1=ALU.add), reads=[src, convw, raw], writes=[raw])
                    if kind == 2:
                        fw.op("act", lambda e: e.activation(out=dst[:, t0:t0 + n], in_=raw[:, 0:n], func=AF.Silu), reads=[raw], writes=[dst])
                        continue
                    fw.op("act", lambda e: e.activation(out=raw[:, 0:n], in_=raw[:, 0:n], func=AF.Silu), reads=[raw], writes=[raw])
                    fw.op("act", lambda e: e.activation(out=sqt[:, 0:n], in_=raw[:, 0:n], func=AF.Square), reads=[raw], writes=[sqt])
                    pss = banks[4 + (t0 // 256) % 3]
                    fw.mm(pss, pss[:, 0:n], cst, onesf, sqt, sqt[:, 0:n])
                    fw.op("dve", lambda e: e.tensor_scalar_add(out=rs[:, 0:n], in0=pss[:, 0:n], scalar1=EPS), reads=[pss], writes=[rs])
                    fw.op("act", lambda e: e.activation(out=rs[:, 0:n], in_=rs[:, 0:n], func=AF.Sqrt), reads=[rs], writes=[rs])
                    fw.op("dve", lambda e: e.reciprocal(out=rs[:, 0:n], in_=rs[:, 0:n]), reads=[rs], writes=[rs])
                    sc = 128.0 ** -0.5 if kind == 0 else 1.0
                    fw.op("dve", lambda e: e.scalar_tensor_tensor(out=dst[:, t0:t0 + n], in0=raw[:, 0:n], scalar=sc, in1=rs[:, 0:n], op0=ALU.mult, op1=ALU.mult),
                          reads=[raw, rs], writes=[dst])
            for (srcT, dstK) in ((KT, Ktok), (VT, Vtok)):
                for j in range(NT):
                    q8 = j % 8
                    fw.tr(pbf, pbf[:, q8 * 128:(q8 + 1) * 128], srcT, srcT[:, j * 128:(j + 1) * 128], identb, identb[:])
                    fw.evac(dstK, dstK[:, j * 128:(j + 1) * 128], pbf, pbf[:, q8 * 128:(q8 + 1) * 128])
            if GDN_STAGE < 2:
                return
            tcache = {}

            def T(dr, par, name, dtype=F32, w=128):
                key = (dr, par, name)
                if key not in tcache:
                    tcache[key] = fw.sb("t%d%d%s_%d%d" % (dr, par, name, b, h), [128, w], dtype)
                return tcache[key]

            def chunk_gen(dr, j, par):
                gcol = j * 32 + dr * 16 + h
                bcol = gcol + 8
                ccol = j * 16 + dr * 8 + h
                jl = 127 if dr == 0 else 0
                t0, t1_ = j * 128, (j + 1) * 128
                ccap = CC[:, ccol:ccol + 1]
                btap = BT[:, bcol:bcol + 1]
                TT = lambda name, dtype=F32, w=128: T(dr, par, name, dtype, w)
                DV = lambda fn, r, w_: fw.op("dve", fn, reads=r, writes=w_)
                AC_ = lambda fn, r, w_: fw.op("act", fn, reads=r, writes=w_)
                PL = lambda fn, r, w_: fw.op("pool", fn, reads=r, writes=w_)
                sl = mmq(dr, GL, GL[:, gcol:gcol + 1].to_broadcast([128, 128]), cst, U[dr])
                Cb = TT("Cb")
                AC_(lambda e: e.copy(out=Cb[:], in_=sl.t), [sl], [Cb])
                yield
                ta, Dm, tb, DTm, E, qg = TT("ta"), TT("Dm"), TT("tb"), TT("DTm"), TT("E"), TT("qg", BF16)
                DV(lambda e: e.tensor_scalar(out=ta[:], in0=Cb[:], scalar1=ccap, scalar2=0.0, op0=ALU.subtract, op1=ALU.max), [Cb, CC], [ta])
                DV(lambda e: e.tensor_scalar(out=tb[:], in0=Cb[:], scalar1=ccap, scalar2=0.0, op0=ALU.subtract, op1=ALU.min), [Cb, CC], [tb])
                AC_(lambda e: e.activation(out=Dm[:], in_=ta[:], func=AF.Exp, scale=-1.0), [ta], [Dm])
                AC_(lambda e: e.activation(out=DTm[:], in_=tb[:], func=AF.Exp), [tb], [DTm])
                AC_(lambda e: e.activation(out=E[:], in_=Cb[:], func=AF.Exp), [Cb], [E])
                cols = TT("cols", F32, 8)
                AC_(lambda e: e.activation(out=cols[:, 0:1], in_=ccap, func=AF.Exp), [CC], [cols])
                AC_(lambda e: e.activation(out=cols[:, 2:3], in_=ccap, func=AF.Exp, scale=-1.0, bias=Cb[:, jl:jl + 1]), [CC, Cb], [cols])
                AC_(lambda e: e.activation(out=cols[:, 3:4], in_=Cb[:, jl:jl + 1], func=AF.Exp), [Cb], [cols])
                yield
                PL(lambda e: e.tensor_tensor(out=Dm[:], in0=Dm[:], in1=Ms[dr], op=ALU.mult), [Dm, cst], [Dm])
                yield
                PL(lambda e: e.tensor_tensor(out=DTm[:], in0=DTm[:], in1=Mi[1 - dr], op=ALU.mult), [DTm, cst], [DTm])
                yield
                DV(lambda e: e.tensor_tensor(out=qg[:], in0=QT[:, t0:t1_], in1=E[:], op=ALU.mult), [QT, E], [qg])
                yield
                DV(lambda e: e.tensor_tensor(out=cols[:, 1:2], in0=cols[:, 0:1], in1=btap, op=ALU.mult), [cols, BT], [cols])
                yield
                bke, bv, kdec = TT("bke"), TT("bv"), TT("kdec", BF16)
                DV(lambda e: e.tensor_scalar(out=bke[:], in0=Ktok[:, t0:t1_], scalar1=cols[:, 1:2], scalar2=None, op0=ALU.mult), [Ktok, cols], [bke])
                yield
                DV(lambda e: e.tensor_scalar(out=bv[:], in0=Vtok[:, t0:t1_], scalar1=btap, scalar2=None, op0=ALU.mult), [Vtok, BT], [bv])
                yield
                DV(lambda e: e.tensor_scalar(out=kdec[:], in0=Ktok[:, t0:t1_], scalar1=cols[:, 2:3], scalar2=None, op0=ALU.mult), [Ktok, cols], [kdec])
                yield
                sl = mmq(dr, KT, KT[:, t0:t1_], KT, KT[:, t0:t1_])
                yield
                A, AT = TT("A"), TT("AT")
                yield
                DV(lambda e: e.scalar_tensor_tensor(out=A[:], in0=sl.t, scalar=btap, in1=Dm[:], op0=ALU.mult, op1=ALU.mult), [sl, BT, Dm], [A])
                sl2 = mmq.slots[dr][mmq.si[dr] % 8]
                mmq.si[dr] += 1
                fw.tr(sl2, sl2.t, A, A[:], cst, identf)
                AC_(lambda e: e.copy(out=AT[:], in_=sl2.t), [sl2], [AT])
                yield
                N0, N0T, A1, A1T, A2 = TT("N0"), TT("N0T"), TT("A1"), TT("A1T"), TT("A2")
                PL(lambda e: e.tensor_tensor(out=N0[:], in0=A[:], in1=NB32, op=ALU.mult), [A, cst], [N0])
                PL(lambda e: e.tensor_tensor(out=N0T[:], in0=AT[:], in1=NB32, op=ALU.mult), [AT, cst], [N0T])
                PL(lambda e: e.tensor_tensor(out=A1[:], in0=A[:], in1=B6432, op=ALU.mult), [A, cst], [A1])
                PL(lambda e: e.tensor_tensor(out=A1T[:], in0=AT[:], in1=B6432, op=ALU.mult), [AT, cst], [A1T])
                PL(lambda e: e.tensor_tensor(out=A2[:], in0=A[:], in1=NB64c, op=ALU.mult), [A, cst], [A2])
                R, Tm = TT("R0"), TT("T0")
                DV(lambda e: e.tensor_tensor(out=R[:], in0=N0T[:], in1=identf, op=ALU.add), [N0T, cst], [R])
                DV(lambda e: e.tensor_tensor(out=Tm[:], in0=N0[:], in1=identf, op=ALU.add), [N0, cst], [Tm])
                P, PT = N0, N0T
                yield
                for k in range(4):
                    s1 = mmq(dr, PT, PT[:], P, P[:])
                    s2 = mmq(dr, P, P[:], PT, PT[:])
                    yield
                    P2, PT2 = TT("P%d" % (k % 2)), TT("PT%d" % (k % 2))
                    AC_(lambda e: e.copy(out=P2[:], in_=s1.t), [s1], [P2])
                    DV(lambda e: e.tensor_copy(out=PT2[:], in_=s2.t), [s2], [PT2])
                    yield
                    s3 = mmq(dr, P2, P2[:], R, R[:])
                    s4 = mmq(dr, PT2, PT2[:], Tm, Tm[:])
                    yield
                    Rn, Tn = TT("R%d" % ((k + 1) % 2)), TT("T%d" % ((k + 1) % 2))
                    DV(lambda e: e.tensor_tensor(out=Rn[:], in0=R[:], in1=s3.t, op=ALU.add), [R, s3], [Rn])
                    DV(lambda e: e.tensor_tensor(out=Tn[:], in0=Tm[:], in1=s4.t, op=ALU.add), [Tm, s4], [Tn])
                    R, Tm, P, PT = Rn, Tn, P2, PT2
                    yield
                s1 = mmq(dr, A1T, A1T[:], Tm, Tm[:])
                s2 = mmq(dr, A1, A1[:], R, R[:])
                yield
                Xp, X = TT("Xp"), TT("X")
                AC_(lambda e: e.copy(out=Xp[:], in_=s1.t), [s1], [Xp])
                DV(lambda e: e.tensor_copy(out=X[:], in_=s2.t), [s2], [X])
                yield
                s3 = mmq(dr, R, R[:], Xp, Xp[:])
                s4 = mmq(dr, Tm, Tm[:], X, X[:])
                yield
                T1, R1 = TT("T1"), TT("R1")
                DV(lambda e: e.tensor_tensor(out=T1[:], in0=Tm[:], in1=s3.t, op=ALU.subtract), [Tm, s3], [T1])
                DV(lambda e: e.tensor_tensor(out=R1[:], in0=R[:], in1=s4.t, op=ALU.subtract), [R, s4], [R1])
                yield
                s1 = mmq(dr, A2, A2[:], R1, R1[:])
                yield
                X2 = TT("X2")
                AC_(lambda e: e.copy(out=X2[:], in_=s1.t), [s1], [X2])
                yield
                s2 = mmq(dr, T1, T1[:], X2, X2[:])
                yield
                R2 = TT("R2")
                DV(lambda e: e.tensor_tensor(out=R2[:], in0=R1[:], in1=s2.t, op=ALU.subtract), [R1, s2], [R2])
                yield
                s1 = mmq(dr, R2, R2[:], bv, bv[:])
                s2 = mmq(dr, bke, bke[:], R2, R2[:])
                s3 = mmq(dr, KT, KT[:, t0:t1_], QT, QT[:, t0:t1_])
                yield
                u, wT, QKm = TT("u"), TT("wT", BF16), TT("QKm", BF16)
                AC_(lambda e: e.copy(out=u[:], in_=s1.t), [s1], [u])
                AC_(lambda e: e.copy(out=wT[:], in_=s2.t), [s2], [wT])
                DV(lambda e: e.tensor_tensor(out=QKm[:], in0=s3.t, in1=DTm[:], op=ALU.mult), [s3, DTm], [QKm])
                yield
                s1 = mmq(dr, wT, wT[:], Sbf[dr], Sbf[dr][:])
                yield
                vn = TT("vn", BF16)
                DV(lambda e: e.tensor_tensor(out=vn[:], in0=u[:], in1=s1.t, op=ALU.subtract), [u, s1], [vn])
                yield
                ob = banks[4 + dr]
                fw.mm(ob, ob[:, 0:128], qg, qg[:], Sbf[dr], Sbf[dr][:], start=True, stop=False)
                fw.mm(ob, ob[:, 0:128], QKm, QKm[:], vn, vn[:], start=False, stop=True)
                s2 = mmq(dr, kdec, kdec[:], vn, vn[:])
                yield
                DV(lambda e: e.tensor_tensor(out=Oacc[:, t0:t1_], in0=Oacc[:, t0:t1_], in1=ob[:, 0:128], op=ALU.add), [Oacc, ob], [Oacc])
                DV(lambda e: e.scalar_tensor_tensor(out=S[dr][:], in0=S[dr][:], scalar=cols[:, 3:4], in1=s2.t, op0=ALU.mult, op1=ALU.add), [S[dr], cols, s2], [S[dr]])
                AC_(lambda e: e.copy(out=Sbf[dr][:], in_=S[dr][:]), [S[dr]], [Sbf[dr]])
                yield

            orders = [list(range(NT)), [1, 0] + list(range(NT - 1, 1, -1))]
            fw.pe_drain = PE_DRAIN
            for step in range(GDN_NSTEPS if GDN_STAGE >= 3 else 0):
                gens = [chunk_gen(dr, orders[dr][step], 0) for dr in range(2)]
                alive = list(gens)
                ny = 0
                while alive and ny < GDN_YIELDS:
                    ny += 1
                    nxt = []
                    for g in alive:
                        try:
                            next(g)
                            nxt.append(g)
                        except StopIteration:
                            pass
                        if BAR:
                            fw.barrier()
                    alive = nxt
            fw.pe_drain = False
            if b == 0 and h == 0:
                dump("gQT", QT, QT[:], NTOK)
                dump("gKT", KT, KT[:], NTOK)
                dump("gVT", VT, VT[:], NTOK)
                dump("gGL", GL, GL[:], NT * 32)
                dump("gBT", BT, BT[:], NT * 32)
                dump("gCC", CC, CC[:], NT * 16)
                dump("gO", Oacc, Oacc[:], NT * 128)
                dump("gS0", S[0], S[0][:], 128)
                dump("gS1", S[1], S[1][:], 128)
            for j in range(NT if GDN_STAGE >= 4 else 0):
                ov = Oacc[:, j * 128:(j + 1) * 128]
                fw.op("act", lambda e: e.activation(out=junk[:], in_=ov, func=AF.Square, accum_out=sm[:, 0:1]), reads=[Oacc], writes=[junk, sm])
                fw.op("dve", lambda e: e.tensor_scalar(out=sm[:, 0:1], in0=sm[:, 0:1], scalar1=1.0 / 128, scalar2=EPS, op0=ALU.mult, op1=ALU.add), reads=[sm], writes=[sm])
                fw.op("act", lambda e: e.activation(out=sm[:, 0:1], in_=sm[:, 0:1], func=AF.Sqrt), reads=[sm], writes=[sm])
                fw.op("dve", lambda e: e.reciprocal(out=sm[:, 0:1], in_=sm[:, 0:1]), reads=[sm], writes=[sm])
                fw.op("dve", lambda e: e.scalar_tensor_tensor(out=yt[:], in0=ov, scalar=sm[:, 0:1], in1=normg[:], op0=ALU.mult, op1=ALU.mult), reads=[Oacc, sm, normg], writes=[yt])
                zv = ZY[:, j * D + h * 128:j * D + (h + 1) * 128]
                fw.op("dve", lambda e: e.tensor_tensor(out=zv, in0=yt[:], in1=zv, op=ALU.mult), reads=[yt, ZY], writes=[ZY])

        mixers = {0: mixer_gdn, 1: mixer_attn, 2: mixer_mlstm}

        with fw.scope():
            prologue()
        for l in layers:
            kind = l % 3
            if kind in mixers:
                with fw.scope():
                    mixers[kind](l, l // 3)
            if do_ffn:
                with fw.scope():
                    w1b = fw.sb("w1b", [128, KC * DFF], BF16)
                    w2b = fw.sb("w2b", [128, 32 * D], BF16)
                    with fw.scope():
                        alloc_stg()
                        load_w(w1b, lambda c0, n: w1b[:].rearrange("p (k f) -> p k f", k=KC)[:, :, c0:c0 + n], w1_d[l], DFF)
                        for kg in range(4):
                            load_w(w2b, lambda c0, n, kg=kg: w2b[:].rearrange("p (k f) -> p k f", k=32)[:, kg * 8:(kg + 1) * 8, c0:c0 + n],
                                   w2_d[l][kg * 1024:(kg + 1) * 1024, :], D)
                    hT = fw.sb("hT_ffn", [128, KC * NCH], BF16)
                    aT = fw.sb("aT_ffn", [128, 32 * NCH], BF16)
                    rtmp = [fw.sb("rtmp%d" % i, [128, NCH]) for i in range(2)]
                    ffn_pass(l, w1b, w2b, hT, aT, rtmp)
        with fw.scope():
            xin = [fw.sb("oin%d" % i, [128, KC * 128]) for i in range(2)]
            xo = [fw.sb("oo%d" % i, [128, D]) for i in range(2)]
            OUT = [[fw.view(None) for j in range(NT)] for b in range(NSEQ)]
            for b in range(NSEQ):
                for j in range(2, NT):
                    t = xin[j % 2]
                    o = xo[j % 2]
                    fw.dma(t[:].rearrange("p (c t) -> p c t", c=KC), xt_ap(b, j * 128, (j + 1) * 128), reads=[XT[b][j]], writes=[t])
                    for half in range(2):
                        bk = banks[(j * 2 + half) % 4]
                        for c in range(4):
                            cc = half * 4 + c
                            fw.tr(bk, bk[:, c * 128:(c + 1) * 128], t, t[:, cc * 128:(cc + 1) * 128], cst, identf)
                        fw.evac(o, o[:, half * 512:(half + 1) * 512], bk, bk[:])
                    fw.dma(out_d[b, (j - 2) * 128:(j - 1) * 128, :], o[:], reads=[o], writes=[OUT[b][j]])
            fw.finish([OUT[b][j] for b in range(NSEQ) for j in range(2, NT)] + DUMPS, "sp")
        fw.barrier()
        print("n_inst", fw.n_inst)
    return nc


def _consts():
    i = np.arange(128)
    ident = np.eye(128, dtype=np.float32)
    ones = np.ones((128, 128), np.float32)
    U_f = (i[:, None] <= i[None, :]).astype(np.float32)
    U_b = U_f.T.copy()
    Ms_f = (i[None, :] < i[:, None]).astype(np.float32)
    Ms_b = Ms_f.T.copy()
    b32 = (i[:, None] // 32 == i[None, :] // 32).astype(np.float32)
    b64 = (i[:, None] // 64 == i[None, :] // 64).astype(np.float32)
    z = np.zeros((128, 128), np.float32)
    return np.ascontiguousarray(np.concatenate([ident, ones, U_f, U_b, Ms_f, Ms_b, -b32, b64 - b32, 1.0 - b64, z], axis=1))


def _rope_tables():
    t = np.arange(NLAT)
    r = (t // 64).astype(np.float64)
    c = (t % 64).astype(np.float64)
    half = 32
    inv = 10000.0 ** (-np.arange(0, half, 2, dtype=np.float64) / half)
    cos = np.zeros((128, NLAT), np.float32)
    sin = np.zeros((128, NLAT), np.float32)
    for p in range(128):
        d = p % 64
        axis, hf, f = d // 32, (d % 32) // 16, d % 16
        ang = (r if axis == 0 else c) * inv[f]
        cos[p] = np.cos(ang)
        sin[p] = np.sin(ang) * (-1.0 if hf == 0 else 1.0)
    return np.ascontiguousarray(np.concatenate([cos, sin], axis=1))


def _layout_inputs(inputs, core):
    f = lambda a: np.ascontiguousarray(np.asarray(a, dtype=np.float32))
    b0 = core * NSEQ
    c = np.asarray(inputs["c"], np.float32)
    cvec = np.stack([c[b0], c[b0 + 1], np.asarray(inputs["c_ctx"], np.float32)], 0)
    cT = cvec.reshape(3, KC, 128).transpose(2, 1, 0).reshape(128, KC * 3)
    b_modT = np.asarray(inputs["b_mod"], np.float32).reshape(DEPTH, 48, 128).transpose(2, 0, 1).reshape(128, DEPTH * 48)
    ln_gT = np.asarray(inputs["ln_g"], np.float32).reshape(DEPTH, 2, KC, 128).transpose(3, 0, 1, 2).reshape(128, -1)
    ln_bT = np.asarray(inputs["ln_b"], np.float32).reshape(DEPTH, 2, KC, 128).transpose(3, 0, 1, 2).reshape(128, -1)
    conv_aT = np.asarray(inputs["conv_a"], np.float32).reshape(2, 5, 24, 128).transpose(3, 0, 1, 2).reshape(128, -1)
    wb = np.asarray(inputs["w_in_b"], np.float32)
    d = np.arange(64)
    partner = (d // 32) * 32 + (1 - (d % 32) // 16) * 16 + d % 16
    cols = (np.arange(2048) // 64) * 64 + partner[np.arange(2048) % 64]
    w_in_bp = wb[:, :, cols]
    return {
        "x": f(inputs["x"][b0:b0 + NSEQ]), "ctx": f(inputs["ctx"][b0:b0 + NSEQ]), "cT": f(cT),
        "w_mod": f(inputs["w_mod"]), "b_modT": f(b_modT), "ln_gT": f(ln_gT), "ln_bT": f(ln_bT),
        "w_in_a": f(inputs["w_in_a"]), "conv_aT": f(conv_aT),
        "a_log_a": f(np.asarray(inputs["a_log_a"]).reshape(2, 16)), "dt_bias_a": f(np.asarray(inputs["dt_bias_a"]).reshape(2, 16)),
        "norm_a": f(inputs["norm_a"]), "w_out_a": f(inputs["w_out_a"]),
        "w_in_b": f(wb), "w_in_bp": f(w_in_bp), "lam_b": f(inputs["lam_b"]), "subln_b": f(inputs["subln_b"]),
        "w_out_b": f(inputs["w_out_b"]), "w_in_c": f(inputs["w_in_c"]),
        "gate_bias_c": f(np.asarray(inputs["gate_bias_c"]).reshape(1, 16)), "norm_c": f(inputs["norm_c"]),
        "w_out_c": f(inputs["w_out_c"]), "w1": f(inputs["w1"]), "w2": f(inputs["w2"]),
        "cst": _consts(), "rope": _rope_tables(),
    }


def kernel(**inputs):
    nc = build()
    shared = None
    in_maps = []
    for core in range(NCORES):
        m = _layout_inputs(inputs, core)
        if shared is None:
            shared = m
        else:
            for k in m:
                if k not in ("x", "ctx", "cT"):
                    m[k] = shared[k]
        in_maps.append(m)
    res = run_bass_kernel_spmd(nc, in_maps, core_ids=list(range(NCORES)))
    out = np.concatenate([np.asarray(r["out"]) for r in res.results], axis=0)
    return out.astype(np.float32)
```
